# Optimizing a Trainium2 kernel written in Bass

```python
import math
import jax
import jax.numpy as jnp
from jax import lax
import numpy as np


D_MODEL = 1024
BATCH = 1
SEQ = 16384
DEPTH = 2

GRID_W = 64
CTX_LEN = 256
EPS = 1e-6
NEG = -1e30
Q_BLOCK = 128
ROPE_DIM = 64
ROPE_THETA = 10000.0

A_HEADS = 8
A_KV_HEADS = 2
A_HEAD_DIM = ROPE_DIM
A_WINDOW = 128
B_HEADS = 4
B_HEAD_DIM = ROPE_DIM
C_HEADS = 4
C_Q_RANK = 256
C_KV_RANK = 128
C_NOPE_DIM = 128
C_ROPE_DIM = ROPE_DIM
C_V_DIM = 128
D_HEADS = 8
D_HEAD_DIM = 64
D_WIN_ROWS = 8
D_WIN_COLS = 16

MIX_EVEN = A_HEADS * A_HEAD_DIM + B_HEADS * 2 * B_HEAD_DIM
MIX_ODD = C_HEADS * C_V_DIM + D_HEADS * D_HEAD_DIM
EVEN_SPLITS = (A_HEADS * A_HEAD_DIM, A_KV_HEADS * A_HEAD_DIM, A_KV_HEADS * A_HEAD_DIM, B_HEADS * 2 * B_HEAD_DIM, B_HEADS * 2 * B_HEAD_DIM, B_HEADS * 2 * B_HEAD_DIM, MIX_EVEN)
ODD_SPLITS = (C_Q_RANK, C_KV_RANK, C_ROPE_DIM, D_HEADS * D_HEAD_DIM, D_HEADS * D_HEAD_DIM, D_HEADS * D_HEAD_DIM, MIX_ODD)
IN_EVEN = sum(EVEN_SPLITS)
IN_ODD = sum(ODD_SPLITS)

kernel_name = 'hybrid_dit_window_diff_mla_natten'


def rmsnorm(x, g):
    xf = x.astype(jnp.float32)
    y = xf * lax.rsqrt(jnp.mean(xf * xf, axis=-1, keepdims=True) + EPS)
    return (y * g.astype(jnp.float32)).astype(x.dtype)


def _split(t, sizes):
    return jnp.split(t, np.cumsum(sizes)[:-1].tolist(), axis=-1)


def _axial_rope_tables(S, dim):
    t = jnp.arange(S)
    row = (t // GRID_W).astype(jnp.float32)
    col = (t % GRID_W).astype(jnp.float32)
    quarter = dim // 4
    inv = ROPE_THETA ** (-jnp.arange(quarter, dtype=jnp.float32) / quarter)
    ang_r = row[:, None] * inv[None, :]
    ang_c = col[:, None] * inv[None, :]
    ang = jnp.concatenate([ang_r, ang_r, ang_c, ang_c], axis=-1)
    return jnp.cos(ang), jnp.sin(ang)


def _rope(x, cos, sin):
    half = x.shape[-1] // 2
    qtr = half // 2
    xr, xc = x[..., :half], x[..., half:]
    rot = jnp.concatenate([-xr[..., qtr:], xr[..., :qtr], -xc[..., qtr:], xc[..., :qtr]], axis=-1)
    return (x * cos[:, None, :] + rot * sin[:, None, :]).astype(x.dtype)


def _modulate(x, cond, norm_g, w_ada, b_ada):
    mod = jax.nn.silu(cond) @ w_ada + b_ada
    shift, scale, gate = jnp.split(mod[:, None, :], 3, axis=-1)
    return rmsnorm(x, norm_g) * (1.0 + scale) + shift, gate


def _sweep_query_blocks(fn, q):
    B, S = q.shape[:2]
    nb = S // Q_BLOCK
    qb = jnp.moveaxis(q.reshape((B, nb, Q_BLOCK) + q.shape[2:]), 1, 0)
    out = lax.map(lambda a: fn(a[0], a[1]), (jnp.arange(nb), qb))
    out = jnp.moveaxis(out, 0, 1)
    return out.reshape((B, S) + out.shape[3:])


def _softmax_attend(q, k, v):
    s = jnp.einsum('bqhd,bkhd->bhqk', q, k).astype(jnp.float32) * (q.shape[-1] ** -0.5)
    p = jax.nn.softmax(s, axis=-1).astype(v.dtype)
    return jnp.einsum('bhqk,bkhe->bqhe', p, v)


def _sink_probs(parts, sink):
    base = parts[0]
    Hk, G = base.shape[1], base.shape[2]
    snk = jnp.broadcast_to(sink.astype(jnp.float32).reshape(Hk, G)[None, :, :, None, None], base.shape[:-1] + (1,))
    p = jax.nn.softmax(jnp.concatenate(list(parts) + [snk], axis=-1), axis=-1)
    return p[..., :-1]


def _window_gqa_sink(q, k, v, kc, vc, sink):
    B, S, Hk, G, d = q.shape
    L = kc.shape[1]
    W = A_WINDOW
    span = Q_BLOCK + 2 * W
    scale = d ** -0.5
    kp = jnp.pad(k, ((0, 0), (W, W), (0, 0), (0, 0)))
    vp = jnp.pad(v, ((0, 0), (W, W), (0, 0), (0, 0)))

    def block(n, qn):
        start = n * Q_BLOCK
        kn = lax.dynamic_slice_in_dim(kp, start, span, axis=1)
        vn = lax.dynamic_slice_in_dim(vp, start, span, axis=1)
        qpos = start + jnp.arange(Q_BLOCK)
        kpos = start - W + jnp.arange(span)
        valid = (jnp.abs(qpos[:, None] - kpos[None, :]) <= W) & (kpos >= 0)[None, :] & (kpos < S)[None, :]
        s_loc = jnp.einsum('bqkgd,bskd->bkgqs', qn, kn).astype(jnp.float32) * scale
        s_loc = jnp.where(valid, s_loc, NEG)
        s_ctx = jnp.einsum('bqkgd,bckd->bkgqc', qn, kc).astype(jnp.float32) * scale
        p = _sink_probs([s_loc, s_ctx], sink).astype(v.dtype)
        return (jnp.einsum('bkgqs,bskd->bqkgd', p[..., :span], vn)
                + jnp.einsum('bkgqc,bckd->bqkgd', p[..., span:span + L], vc))

    return _sweep_query_blocks(block, q)


def _ctx_sink_attend(q, k, v, sink):
    s = jnp.einsum('bqkgd,bckd->bkgqc', q, k).astype(jnp.float32) * (q.shape[-1] ** -0.5)
    p = _sink_probs([s], sink).astype(v.dtype)
    return jnp.einsum('bkgqc,bckd->bqkgd', p, v)


def _diff_lambda(b_lambda, lam_init):
    lf = b_lambda.astype(jnp.float32)
    return jnp.exp(jnp.sum(lf[0] * lf[1])) - jnp.exp(jnp.sum(lf[2] * lf[3])) + lam_init


def _diff_attend(q, k, v, lam):
    s = jnp.einsum('bqhte,bkhte->bhtqk', q, k).astype(jnp.float32) * (q.shape[-1] ** -0.5)
    p = jax.nn.softmax(s, axis=-1)
    a = (p[:, :, 0] - lam * p[:, :, 1]).astype(v.dtype)
    return jnp.einsum('bhqk,bkhe->bqhe', a, v)


def _mla_qkv(cq, ckv, kr, q_norm_g, kv_norm_g, w_qb, w_kvb, cos, sin):
    B, T, _ = cq.shape
    q = (rmsnorm(cq, q_norm_g) @ w_qb).reshape(B, T, C_HEADS, C_NOPE_DIM + C_ROPE_DIM)
    kv = (rmsnorm(ckv, kv_norm_g) @ w_kvb).reshape(B, T, C_HEADS, C_NOPE_DIM + C_V_DIM)
    q_nope, q_pe = q[..., :C_NOPE_DIM], q[..., C_NOPE_DIM:]
    k_nope, v = kv[..., :C_NOPE_DIM], kv[..., C_NOPE_DIM:]
    k_pe = kr[:, :, None, :]
    if cos is not None:
        q_pe = _rope(q_pe, cos, sin)
        k_pe = _rope(k_pe, cos, sin)
    q = jnp.concatenate([q_nope, q_pe], axis=-1)
    k = jnp.concatenate([k_nope, jnp.broadcast_to(k_pe, (B, T, C_HEADS, C_ROPE_DIM))], axis=-1)
    return q, k, v


def _neighbourhood_attend(q, k, v, kc, vc, rpb):
    B, S, H, d = q.shape
    L = kc.shape[1]
    rows = S // GRID_W
    kr_n = min(D_WIN_ROWS, rows)
    kw_n = D_WIN_COLS
    n_loc = kr_n * GRID_W
    scale = d ** -0.5
    kg = k.reshape(B, rows, GRID_W, H, d)
    vg = v.reshape(B, rows, GRID_W, H, d)
    cols = jnp.arange(GRID_W)
    cs = jnp.clip(cols - kw_n // 2, 0, GRID_W - kw_n)
    col_ok = (cols[None, :] >= cs[:, None]) & (cols[None, :] < cs[:, None] + kw_n)
    mask = jnp.broadcast_to(col_ok[:, None, :], (GRID_W, kr_n, GRID_W)).reshape(GRID_W, n_loc)
    dc = jnp.clip(cols[None, :] - cols[:, None] + (D_WIN_COLS - 1), 0, 2 * D_WIN_COLS - 2)
    rpb_f = rpb.astype(jnp.float32)
    qg = jnp.moveaxis(q.reshape(B, rows, GRID_W, H, d), 1, 0)

    def row_fn(args):
        r, qr = args
        rs = jnp.clip(r - kr_n // 2, 0, rows - kr_n)
        ks = lax.dynamic_slice_in_dim(kg, rs, kr_n, axis=1).reshape(B, n_loc, H, d)
        vs = lax.dynamic_slice_in_dim(vg, rs, kr_n, axis=1).reshape(B, n_loc, H, d)
        dr = rs + jnp.arange(kr_n) - r + (D_WIN_ROWS - 1)
        bias = rpb_f[:, dr[:, None, None], dc[None, :, :]]
        bias = jnp.transpose(bias, (0, 2, 1, 3)).reshape(H, GRID_W, n_loc)
        s_loc = jnp.einsum('bqhd,bkhd->bhqk', qr, ks).astype(jnp.float32) * scale + bias[None]
        s_loc = jnp.where(mask, s_loc, NEG)
        s_ctx = jnp.einsum('bqhd,bchd->bhqc', qr, kc).astype(jnp.float32) * scale
        p = jax.nn.softmax(jnp.concatenate([s_loc, s_ctx], axis=-1), axis=-1).astype(v.dtype)
        return (jnp.einsum('bhqk,bkhd->bqhd', p[..., :n_loc], vs)
                + jnp.einsum('bhqc,bchd->bqhd', p[..., n_loc:], vc))

    out = lax.map(row_fn, (jnp.arange(rows), qg))
    return jnp.moveaxis(out, 0, 1).reshape(B, S, H, d)


def _even_layer(x, xc, c, c_ctx, li, norm_g, w_ada, b_ada, w_in, a_sink, b_lambda, b_subln_g, w_out, cos, sin, ctx_out):
    B, S, _ = x.shape
    L = xc.shape[1]
    d = A_HEAD_DIM
    e = B_HEAD_DIM
    Hk, G = A_KV_HEADS, A_HEADS // A_KV_HEADS
    h, gate = _modulate(x, c, norm_g, w_ada, b_ada)
    hc, gate_c = _modulate(xc, c_ctx[None, :], norm_g, w_ada, b_ada)
    qa, ka, va, qb, kb, vb, g = _split(h @ w_in, EVEN_SPLITS)
    qac, kac, vac, qbc, kbc, vbc, gc = _split(hc @ w_in, EVEN_SPLITS)
    qa = _rope(qa.reshape(B, S, A_HEADS, d), cos, sin).reshape(B, S, Hk, G, d)
    ka = _rope(ka.reshape(B, S, Hk, d), cos, sin)
    va = va.reshape(B, S, Hk, d)
    kac = kac.reshape(B, L, Hk, d)
    vac = vac.reshape(B, L, Hk, d)
    ya = _window_gqa_sink(qa, ka, va, kac, vac, a_sink)
    qb = _rope(qb.reshape(B, S, B_HEADS * 2, e), cos, sin).reshape(B, S, B_HEADS, 2, e)
    kb = _rope(kb.reshape(B, S, B_HEADS * 2, e), cos, sin).reshape(B, S, B_HEADS, 2, e)
    vb = vb.reshape(B, S, B_HEADS, 2 * e)
    kbc = kbc.reshape(B, L, B_HEADS, 2, e)
    vbc = vbc.reshape(B, L, B_HEADS, 2 * e)
    lam_init = 0.8 - 0.6 * math.exp(-0.3 * li)
    lam = _diff_lambda(b_lambda, lam_init)
    kb_all = jnp.concatenate([kb, kbc], axis=1)
    vb_all = jnp.concatenate([vb, vbc], axis=1)
    yb = _sweep_query_blocks(lambda n, qn: _diff_attend(qn, kb_all, vb_all, lam), qb)
    yb = rmsnorm(yb, b_subln_g) * (1.0 - lam_init)
    y = jnp.concatenate([ya.reshape(B, S, -1), yb.reshape(B, S, -1)], axis=-1) * jax.nn.silu(g)
    x = x + gate * (y @ w_out)
    if ctx_out:
        yac = _ctx_sink_attend(qac.reshape(B, L, Hk, G, d), kac, vac, a_sink)
        ybc = rmsnorm(_diff_attend(qbc.reshape(B, L, B_HEADS, 2, e), kbc, vbc, lam), b_subln_g) * (1.0 - lam_init)
        yc = jnp.concatenate([yac.reshape(B, L, -1), ybc.reshape(B, L, -1)], axis=-1) * jax.nn.silu(gc)
        xc = xc + gate_c * (yc @ w_out)
    return x, xc


def _odd_layer(x, xc, c, c_ctx, norm_g, w_ada, b_ada, w_in, q_norm_g, kv_norm_g, w_qb, w_kvb, rpb, w_out, cos, sin, ctx_out):
    B, S, _ = x.shape
    L = xc.shape[1]
    dd = D_HEAD_DIM
    h, gate = _modulate(x, c, norm_g, w_ada, b_ada)
    hc, gate_c = _modulate(xc, c_ctx[None, :], norm_g, w_ada, b_ada)
    cq, ckv, kr, qd, kd, vd, g = _split(h @ w_in, ODD_SPLITS)
    cqc, ckvc, krc, qdc, kdc, vdc, gc = _split(hc @ w_in, ODD_SPLITS)
    qm, km, vm = _mla_qkv(cq, ckv, kr, q_norm_g, kv_norm_g, w_qb, w_kvb, cos, sin)
    qmc, kmc, vmc = _mla_qkv(cqc, ckvc, krc, q_norm_g, kv_norm_g, w_qb, w_kvb, None, None)
    km_all = jnp.concatenate([km, kmc], axis=1)
    vm_all = jnp.concatenate([vm, vmc], axis=1)
    ym = _sweep_query_blocks(lambda n, qn: _softmax_attend(qn, km_all, vm_all), qm)
    kdc = kdc.reshape(B, L, D_HEADS, dd)
    vdc = vdc.reshape(B, L, D_HEADS, dd)
    yd = _neighbourhood_attend(qd.reshape(B, S, D_HEADS, dd), kd.reshape(B, S, D_HEADS, dd), vd.reshape(B, S, D_HEADS, dd), kdc, vdc, rpb)
    y = jnp.concatenate([ym.reshape(B, S, -1), yd.reshape(B, S, -1)], axis=-1) * jax.nn.silu(g)
    x = x + gate * (y @ w_out)
    if ctx_out:
        ymc = _softmax_attend(qmc, kmc, vmc)
        ydc = _softmax_attend(qdc.reshape(B, L, D_HEADS, dd), kdc, vdc)
        yc = jnp.concatenate([ymc.reshape(B, L, -1), ydc.reshape(B, L, -1)], axis=-1) * jax.nn.silu(gc)
        xc = xc + gate_c * (yc @ w_out)
    return x, xc


def setup_inputs(seed: int = 0) -> dict:
    key = jax.random.key(seed)
    keys = iter(jax.random.split(key, 24))

    def nrm(shape, std):
        return std * jax.random.normal(next(keys), shape, jnp.float32)

    D = D_MODEL
    n_ev = (DEPTH + 1) // 2
    n_od = DEPTH // 2
    return {
        'x': nrm((BATCH, SEQ, D), 1.0),
        'c': nrm((BATCH, D), 1.0),
        'ctx': nrm((BATCH, CTX_LEN, D), 1.0),
        'c_ctx': nrm((D,), 1.0),
        'ev_norm_g': 1.0 + nrm((n_ev, D), 0.02),
        'ev_w_ada': nrm((n_ev, D, 3 * D), D ** -0.5),
        'ev_b_ada': nrm((n_ev, 3 * D), 0.02),
        'ev_w_in': nrm((n_ev, D, IN_EVEN), D ** -0.5),
        'ev_a_sink': nrm((n_ev, A_HEADS), 0.5),
        'ev_b_lambda': nrm((n_ev, 4, B_HEAD_DIM), 0.1),
        'ev_b_subln_g': 1.0 + nrm((n_ev, 2 * B_HEAD_DIM), 0.02),
        'ev_w_out': nrm((n_ev, MIX_EVEN, D), MIX_EVEN ** -0.5),
        'od_norm_g': 1.0 + nrm((n_od, D), 0.02),
        'od_w_ada': nrm((n_od, D, 3 * D), D ** -0.5),
        'od_b_ada': nrm((n_od, 3 * D), 0.02),
        'od_w_in': nrm((n_od, D, IN_ODD), D ** -0.5),
        'od_c_q_norm_g': 1.0 + nrm((n_od, C_Q_RANK), 0.02),
        'od_c_kv_norm_g': 1.0 + nrm((n_od, C_KV_RANK), 0.02),
        'od_c_w_qb': nrm((n_od, C_Q_RANK, C_HEADS * (C_NOPE_DIM + C_ROPE_DIM)), C_Q_RANK ** -0.5),
        'od_c_w_kvb': nrm((n_od, C_KV_RANK, C_HEADS * (C_NOPE_DIM + C_V_DIM)), C_KV_RANK ** -0.5),
        'od_d_rpb': nrm((n_od, D_HEADS, 2 * D_WIN_ROWS - 1, 2 * D_WIN_COLS - 1), 0.02),
        'od_w_out': nrm((n_od, MIX_ODD, D), MIX_ODD ** -0.5),
        'final_norm_g': 1.0 + nrm((D,), 0.02),
    }


def reference(x, c, ctx, c_ctx, ev_norm_g, ev_w_ada, ev_b_ada, ev_w_in, ev_a_sink, ev_b_lambda, ev_b_subln_g, ev_w_out, od_norm_g, od_w_ada, od_b_ada, od_w_in, od_c_q_norm_g, od_c_kv_norm_g, od_c_w_qb, od_c_w_kvb, od_d_rpb, od_w_out, final_norm_g):
    S = x.shape[1]
    cos, sin = _axial_rope_tables(S, ROPE_DIM)
    xl, xc = x, ctx
    for i in range(DEPTH):
        ctx_out = i < DEPTH - 1
        j = i // 2
        if i % 2 == 0:
            xl, xc = _even_layer(xl, xc, c, c_ctx, i, ev_norm_g[j], ev_w_ada[j], ev_b_ada[j], ev_w_in[j], ev_a_sink[j], ev_b_lambda[j], ev_b_subln_g[j], ev_w_out[j], cos, sin, ctx_out)
        else:
            xl, xc = _odd_layer(xl, xc, c, c_ctx, od_norm_g[j], od_w_ada[j], od_b_ada[j], od_w_in[j], od_c_q_norm_g[j], od_c_kv_norm_g[j], od_c_w_qb[j], od_c_w_kvb[j], od_d_rpb[j], od_w_out[j], cos, sin, ctx_out)
    return rmsnorm(xl, final_norm_g)
```

```python
import math
import numpy as np
from contextlib import ExitStack
import ml_dtypes
import concourse.bass as bass
import concourse.mybir as mybir
from concourse.bass_utils import run_bass_kernel_spmd

F32 = mybir.dt.float32
BF16 = mybir.dt.bfloat16
I32 = mybir.dt.int32
AF = mybir.ActivationFunctionType
ALU = mybir.AluOpType
AX = mybir.AxisListType
NPBF = ml_dtypes.bfloat16

NCORES = 8
S = 16384
T = 2048
NTB = 16
L = 256
TT = T + L
D = 1024
EPS = 1e-6
GRID_W = 64
NIDX = 26


class Buf:
    __slots__ = ("name", "writers", "readers", "dma_sem", "dma_cnt", "excl")

    def __init__(self, name, excl=False):
        self.name = name
        self.excl = excl
        self.writers = []
        self.readers = []
        self.dma_sem = None
        self.dma_cnt = 0


class Op:
    __slots__ = ("eng", "emit", "deps", "signal", "sigval", "is_dma", "dsem", "dval", "cc_inc")

    def __init__(self, eng, emit, is_dma=False):
        self.eng = eng
        self.emit = emit
        self.deps = []
        self.signal = False
        self.sigval = 0
        self.is_dma = is_dma
        self.dsem = None
        self.dval = 0
        self.cc_inc = 16


class Prog:
    ENGS = ("pe", "act", "dve", "pool", "sp")

    def __init__(self, nc, tag=""):
        self.nc = nc
        self.tag = tag
        self.ops = {e: [] for e in self.ENGS}
        self.stack = ExitStack()
        self.esem = {}
        self.bufs = []
        self.dma_bufs = []
        self.sems = []

    def sem(self, name):
        h = self.nc.alloc_semaphore(name=self.tag + name)
        self.sems.append(h)
        return h

    def sbuf(self, name, shape, dt):
        return self.stack.enter_context(self.nc.sbuf_tensor(self.tag + name, shape, dt))

    def psum(self, name, shape, dt):
        return self.stack.enter_context(self.nc.psum_tensor(self.tag + name, shape, dt))

    def buf(self, name, excl=False):
        b = Buf(name, excl)
        self.bufs.append(b)
        return b

    def bufs_n(self, name, n):
        return [self.buf(f"{name}{i}") for i in range(n)]

    def _add(self, op, reads, writes, join=False):
        xr = [b for b in reads if b.excl]
        reads = [b for b in reads if not b.excl]
        xw = [b for b in writes if b.excl]
        writes = [b for b in writes if not b.excl]
        deps = []
        for b in xr + xw:
            deps.extend(b.readers)
            deps.extend(b.writers)
        for b in reads:
            deps.extend(b.writers)
        for b in writes:
            deps.extend(b.readers)
            if not join:
                deps.extend(b.writers)
        op.deps = [d for d in deps if not (d.eng == "pe" and op.eng == "pe" and not d.is_dma and not op.is_dma)]
        for b in xr + xw:
            b.writers = [op]
            b.readers = []
        for b in reads:
            b.readers.append(op)
        for b in writes:
            if join:
                b.writers.append(op)
            else:
                b.writers = [op]
            b.readers = []
        self.ops[op.eng].append(op)
        return op

    def op(self, eng, emit, reads=(), writes=(), join=False):
        return self._add(Op(eng, emit), list(reads), list(writes), join)

    def dma(self, eng, out, in_, reads, writes, join=False, emit=None, sem_buf=None, **kw):
        assert len(writes) == 1
        b = sem_buf if sem_buf is not None else (reads[0] if (len(reads) == 1 and self.outbound(out)) else writes[0])
        if b.dma_sem is None:
            b.dma_sem = self.sem("d_" + b.name)
            self.dma_bufs.append(b)
        b.dma_cnt += 16
        if emit is None:
            emit = lambda e, out=out, in_=in_, kw=kw: e.dma_start(out=out, in_=in_, **kw)
        o = Op(eng, emit, is_dma=True)
        o.dsem = b.dma_sem
        o.dval = b.dma_cnt
        return self._add(o, list(reads), list(writes), join)

    @staticmethod
    def outbound(out_ap):
        try:
            return "DRam" in type(out_ap.tensor).__name__ or "Dram" in type(out_ap.tensor).__name__ or "DRAM" in type(out_ap.tensor).__name__
        except Exception:
            return False

    def gather(self, out, src2d, idx, reads, writes, join=False):
        def emit(e, out=out, src2d=src2d, idx=idx):
            return e.indirect_dma_start(out=out, out_offset=None, in_=src2d,
                                        in_offset=bass.IndirectOffsetOnAxis(ap=idx, axis=0))
        return self.dma("pool", None, None, reads, writes, join=join, emit=emit, sem_buf=writes[0])

    def collective(self, kind, in_ap, out_ap, reads, writes):
        def emit(e):
            return e.collective_compute(kind, ALU.bypass, replica_groups=[list(range(NCORES))], ins=[in_ap], outs=[out_ap])
        o = self.dma("pool", None, None, reads, writes, emit=emit, sem_buf=writes[0])
        b = writes[0]
        b.dma_cnt += 1 - 16
        o.dval = b.dma_cnt
        o.cc_inc = 1
        return o

    def wait_all(self, eng, bufs):
        return self._add(Op(eng, None), list(bufs), [])

    def build(self):
        nc = self.nc
        fin = Op("sp", None)
        fin.deps = []
        for b in self.dma_bufs:
            d = Op("sp", None, is_dma=True)
            d.dsem = b.dma_sem
            d.dval = b.dma_cnt
            fin.deps.append(d)
        self.ops["sp"].append(fin)
        for e in self.ENGS:
            self.esem[e] = self.sem("e_" + e)
        for e in self.ENGS:
            for o in self.ops[e]:
                for d in o.deps:
                    if not d.is_dma:
                        d.signal = True
        for e in self.ENGS:
            c = 0
            for o in self.ops[e]:
                if o.signal and not o.is_dma:
                    c += 1
                    o.sigval = c
        engobj = {"pe": "tensor", "act": "scalar", "dve": "vector", "pool": "gpsimd", "sp": "sync"}

        def make(e):
            def fn(eng):
                waited = {}
                for o in self.ops[e]:
                    need = {}
                    for d in o.deps:
                        if d.is_dma:
                            s, v = d.dsem, d.dval
                        else:
                            s, v = self.esem[d.eng], d.sigval
                        k = id(s)
                        if k not in need or need[k][1] < v:
                            need[k] = (s, v)
                    for k, (s, v) in need.items():
                        if waited.get(k, 0) >= v:
                            continue
                        waited[k] = v
                        eng.wait_ge(s, v)
                    if o.emit is None:
                        continue
                    inst = o.emit(eng)
                    if o.is_dma:
                        inst.then_inc(o.dsem, o.cc_inc)
                    elif o.signal:
                        inst.then_inc(self.esem[e], 1)
                last = max([o.sigval for o in self.ops[e]] + [0])
                if last > 0:
                    eng.wait_ge(self.esem[e], last)
            return fn

        with nc.Block() as block:
            for e in self.ENGS:
                if self.ops[e]:
                    getattr(block, engobj[e])(make(e))
        self.stack.close()
        nc.all_engine_barrier()
        nc.clear_and_free_semaphores(self.sems)
        nc.all_engine_barrier()


def mm(P, out, lhsT, rhs, start, stop, reads, writes):
    return P.op("pe", lambda e: e.matmul(out, lhsT, rhs, start=start, stop=stop, skip_group_check=True), reads, writes)


def tr(P, out, in_, ident, reads, writes):
    return P.op("pe", lambda e: e.transpose(out, in_, ident), reads, writes)


def act(P, out, in_, func, reads, writes, **kw):
    return P.op("act", lambda e: e.activation(out, in_, func, **kw), reads, writes)


def tsc(P, eng, out, in0, s1, s2, op0, op1, reads, writes):
    if s2 is None:
        return P.op(eng, lambda e: e.tensor_scalar(out, in0, s1, None, op0), reads, writes)
    return P.op(eng, lambda e: e.tensor_scalar(out, in0, s1, s2, op0, op1), reads, writes)


def tt(P, eng, out, in0, in1, op, reads, writes):
    return P.op(eng, lambda e: e.tensor_tensor(out, in0, in1, op), reads, writes)


def cp(P, eng, out, in_, reads, writes):
    if eng == "act":
        return P.op(eng, lambda e: e.copy(out, in_), reads, writes)
    return P.op(eng, lambda e: e.tensor_copy(out, in_), reads, writes)


def mset(P, eng, ap, val, writes):
    return P.op(eng, lambda e: e.memset(ap, val), [], writes)


def recip(P, out, in_, reads, writes):
    return P.op("dve", lambda e: e.reciprocal(out, in_), reads, writes)


def bcast_rows(ap_row, nparts):
    a = [list(x) for x in ap_row.ap]
    a[0] = [0, nparts]
    return bass.AP(ap_row.tensor, ap_row.offset, a)


class Rot:
    def __init__(self, P, kind, name, shape, dt, n):
        alloc = P.sbuf if kind == "sb" else P.psum
        self.t = [alloc(f"{name}{i}", shape, dt) for i in range(n)]
        self.b = [P.buf(f"{name}{i}", excl=(kind == "ps")) for i in range(n)]
        self.i = 0
        self.n = n

    def next(self):
        r = (self.t[self.i], self.b[self.i])
        self.i = (self.i + 1) % self.n
        return r


ROPE_PERM = np.concatenate([np.arange(16, 32), np.arange(0, 16), np.arange(48, 64), np.arange(32, 48)])
ROPE_SIGN = np.concatenate([-np.ones(16), np.ones(16), -np.ones(16), np.ones(16)]).astype(np.float32)


class Lay0:
    idx = 0
    C = 3328
    fm = ([(128 * i, 128, i, "q", 128 * i) for i in range(4)]
          + [(512, 128, 4, "kt", 0)]
          + [(768 + 128 * i, 128, 5 + i, "q", 512 + 128 * i) for i in range(4)]
          + [(1280 + 128 * i, 128, 9 + i, "kt", 128 + 128 * i) for i in range(4)])
    rope_cols = ([128 * i for i in range(4)] + [512] + [768 + 128 * i for i in range(4)]
                 + [1280 + 128 * i for i in range(4)])
    NR = 13
    QROWS = 1024
    KTROWS = 640
    VCOLS = 646
    tmv = [(640, 128, 2, 64, 0), (1792, 512, 4, 128, 130)]
    gcol = 2304
    halo = {("kt", 0): (0, 128, 128)}
    HKSHAPE = (256, 128)


class Lay1:
    idx = 1
    C = 3008
    fm = ([(448 + 128 * i, 128, None, "q", 768 + 128 * i) for i in range(4)]
          + [(960 + 128 * i, 128, None, "kt", 192 + 128 * i) for i in range(4)]
          + [(384, 64, 0, "kt", 128)])
    rope_cols = [384]
    NR = 1
    QROWS = 1280
    KTROWS = 704
    VCOLS = 649
    tmv = [(1472, 512, 8, 64, 129)]
    gcol = 1984
    halo = {("kt", 192 + 128 * i): (128 * i, 256, 512) for i in range(4)}
    HKSHAPE = (1024, 256)


def phase_A(nc, lay, io, tag):
    P = Prog(nc, tag)
    C = lay.C
    NRC = lay.NR * 128
    identb = P.sbuf("identb", [128, 128], BF16); Bidb = P.buf("identb")
    P.dma("pool", identb[:], io["ident"], [], [Bidb])
    identf = P.sbuf("identf", [128, 128], F32); Bidf = P.buf("identf")
    P.dma("sp", identf[:], io["ident"], [], [Bidf])
    wbf = P.sbuf("wbf", [128, 8, C], BF16); Bw = P.bufs_n("w", 8)
    wrp = P.sbuf("wrp", [128, 8, NRC], BF16); Bwr = P.bufs_n("wr", 8)
    cosT = P.sbuf("cosT", [128, TT], F32); Bcos = P.buf("cos")
    sinT = P.sbuf("sinT", [128, TT], F32); Bsin = P.buf("sin")
    P.dma("sp", cosT[:], io["cosT"], [], [Bcos])
    P.dma("sp", sinT[:], io["sinT"], [], [Bsin])
    cc = P.sbuf("cc", [128, 8, 2], F32); Bcc = P.buf("cc")
    P.dma("sp", cc[:], io["cc2"], [], [Bcc])
    ng = P.sbuf("ng", [128, 8], F32); Bng = P.buf("ng")
    P.dma("sp", ng[:], io["norm_g"], [], [Bng])
    sc = P.sbuf("sc", [128, 8, 2], F32); Bsc = P.buf("sc")
    act(P, sc[:], cc[:], AF.Silu, [Bcc], [Bsc])
    warot = Rot(P, "sb", "wada", [128, 8, 256], F32, 2)
    barot = Rot(P, "sb", "bada", [2, 256], F32, 2)
    mrrot = Rot(P, "sb", "mr", [2, 256], F32, 2)
    psA = Rot(P, "ps", "psA", [128, 512], F32, 2)
    psB = Rot(P, "ps", "psB", [128, 512], F32, 2)
    psm = psA
    pscol = P.psum("pscol", [128, 512], F32); Bpscol = P.buf("pscol", excl=True)
    w_ada_v = io["w_ada"].rearrange("(k p) c -> p k c", p=128)
    Bmodd = P.buf("mod_dram")
    for ct in range(12):
        wa, Bwa = warot.next()
        P.dma("sp", wa[:], w_ada_v[:, :, ct * 256:(ct + 1) * 256], [], [Bwa])
        ba, Bba = barot.next()
        P.dma("sp", ba[:], io["b_ada2"][:, ct * 256:(ct + 1) * 256], [], [Bba])
        ps, Bps = psm.next()
        for k in range(8):
            mm(P, ps[0:2, 0:256], sc[:, k, :], wa[:, k, :], k == 0, k == 7, [Bsc, Bwa], [Bps])
        mr, Bmr = mrrot.next()
        tt(P, "dve", mr[:], ps[0:2, 0:256], ba[:], ALU.add, [Bps, Bba], [Bmr])
        P.dma("sp", io["mod"][:, ct * 256:(ct + 1) * 256], mr[:], [Bmr], [Bmodd], join=True)
        if ct < 8:
            for cc_ in range(2):
                ch = 2 * ct + cc_
                mm(P, pscol[:, 2 * ch:2 * ch + 2], mr[0:2, cc_ * 128:(cc_ + 1) * 128], identf[0:2, 0:2], True, True,
                   [Bmr, Bidf], [Bpscol])
    for k in range(8):
        P.dma("pool", wbf[:, k, :], io["w_in"][k * 128:(k + 1) * 128, :], [], [Bw[k]])
    for k in range(8):
        P.dma("pool", wrp[:, k, :], io["w_rope"][k * 128:(k + 1) * 128, :], [], [Bwr[k]])
    ps, Bps = pscol, Bpscol
    modT = P.sbuf("modT", [128, 16, 2], F32); BmodT = P.buf("modT")
    cp(P, "dve", modT[:].rearrange("p a b -> p (a b)"), ps[:, 0:32], [Bps], [BmodT])
    Acol = P.sbuf("Acol", [128, 8, 2], F32); BA = P.buf("Acol")
    tsc(P, "dve", Acol[:].rearrange("p a b -> p (a b)"), modT[:, 8:16, :].rearrange("p a b -> p (a b)"), 1.0, None,
        ALU.add, None, [BmodT], [BA])
    for j in range(2):
        P.op("dve", lambda e, j=j: e.tensor_tensor(Acol[:, :, j], Acol[:, :, j], ng[:, :], ALU.mult), [BA, Bng], [BA])

    if io.get("_stop") == 1:
        P.build(); return
    hT = P.sbuf("hT", [128, 8, TT], BF16); BhT = P.bufs_n("hT", TT // 128)
    xrot = Rot(P, "sb", "xt", [128, 1024], F32, 2)
    junk = Rot(P, "sb", "junk", [128, 1024], BF16, 1)
    xnrot = Rot(P, "sb", "xn", [128, 1024], BF16, 2)
    strot = Rot(P, "sb", "st", [128, 4], F32, 3)
    ptr = Rot(P, "ps", "ptr", [128, 8, 128], BF16, 2)
    for i in range(io.get("_ntiles", TT // 128)):
        j = 0 if i < NTB else 1
        src = io["x"][i * 128:(i + 1) * 128, :] if i < NTB else io["xc"][(i - NTB) * 128:(i - NTB + 1) * 128, :]
        xt, Bxt = xrot.next()
        P.dma("sp", xt[:], src, [], [Bxt])
        jk, Bjk = junk.next()
        st, Bst = strot.next()
        act(P, jk[:], xt[:], AF.Square, [Bxt], [Bjk, Bst], accum_out=st[:, 0:1])
        act(P, st[:, 1:2], st[:, 0:1], AF.Sqrt, [Bst], [Bst], scale=1.0 / D, bias=EPS)
        recip(P, st[:, 2:3], st[:, 1:2], [Bst], [Bst])
        xn, Bxn = xnrot.next()
        tsc(P, "dve", xn[:], xt[:], st[:, 2:3], None, ALU.mult, None, [Bxt, Bst], [Bxn])
        pt, Bpt = ptr.next()
        for k in range(8):
            tr(P, pt[:, k, :], xn[:, k * 128:(k + 1) * 128], identb[:], [Bxn, Bidb], [Bpt])
        for k in range(8):
            dst = hT[:, k, i * 128:(i + 1) * 128]
            if i % 2 == 0:
                P.op("dve", lambda e, dst=dst, pt=pt, k=k, j=j: e.tensor_scalar(
                    dst, pt[:, k, :], Acol[:, k, j:j + 1], modT[:, k, j:j + 1], ALU.mult, ALU.add),
                    [Bpt, BA, BmodT], [BhT[i]], join=(k > 0))
            else:
                P.op("act", lambda e, dst=dst, pt=pt, k=k, j=j: e.activation(
                    dst, pt[:, k, :], AF.Identity, bias=modT[:, k, j:j + 1], scale=Acol[:, k, j:j + 1]),
                    [Bpt, BA, BmodT], [BhT[i]], join=(k > 0))

    if io.get("_stop") == 2:
        P.build(); return
    ttiles = [(0, 512), (512, 512), (1024, 512), (1536, 512), (2048, 256)]

    def hbufs(t0, n):
        return BhT[t0 // 128:(t0 + n) // 128]

    ostage = Rot(P, "sb", "ost", [128, TT], BF16, 2)
    t1rot = Rot(P, "sb", "t1", [128, 512], F32, 1)
    t2rot = Rot(P, "sb", "t2", [128, 512], F32, 1)
    dcount = [0]

    def dq():
        dcount[0] += 1
        return "sp" if dcount[0] % 2 else "pool"


    def fm_job(c0, M, dst_ap, w_t, Bw_l, rhs_fn, nk, rope=None, scale_bc=None, post=None):
        og, Bog = ostage.next()
        first = True
        for (t0, n) in ttiles:
            ps, Bps = psA.next()
            for k in range(nk):
                rhs, rb = rhs_fn(k, t0, n)
                mm(P, ps[0:M, 0:n], w_t[:, k, c0:c0 + M], rhs, k == 0, k == nk - 1, Bw_l + rb, [Bps])
            if rope is not None:
                wr_t, Bwr_l, rc0 = rope
                ps2, Bps2 = psB.next()
                for k in range(nk):
                    rhs, rb = rhs_fn(k, t0, n)
                    mm(P, ps2[0:M, 0:n], wr_t[:, k, rc0:rc0 + M], rhs, k == 0, k == nk - 1, Bwr_l + rb, [Bps2])
                t1, Bt1 = t1rot.next()
                t2, Bt2 = t2rot.next()
                tt(P, "dve", t1[0:M, 0:n], ps[0:M, 0:n], cosT[0:M, t0:t0 + n], ALU.mult, [Bps, Bcos], [Bt1])
                tt(P, "dve", t2[0:M, 0:n], ps2[0:M, 0:n], sinT[0:M, t0:t0 + n], ALU.mult, [Bps2, Bsin], [Bt2])
                if scale_bc is None:
                    P.op("pool", lambda e, og=og, t1=t1, t2=t2, t0=t0, n=n: e.tensor_tensor(
                        og[0:M, t0:t0 + n], t1[0:M, 0:n], t2[0:M, 0:n], ALU.add), [Bt1, Bt2], [Bog], join=not first)
                else:
                    sbt, Bsb = scale_bc
                    tt(P, "pool", t1[0:M, 0:n], t1[0:M, 0:n], t2[0:M, 0:n], ALU.add, [Bt1, Bt2], [Bt1])
                    P.op("pool", lambda e, og=og, t1=t1, t0=t0, n=n, sbt=sbt: e.tensor_tensor(
                        og[0:M, t0:t0 + n], t1[0:M, 0:n], sbt[0:M, t0:t0 + n], ALU.mult), [Bt1, Bsb], [Bog],
                        join=not first)
            elif post is not None:
                post(ps, Bps, og, Bog, t0, n, first)
            elif scale_bc is not None:
                sbt, Bsb = scale_bc
                P.op("dve", lambda e, og=og, ps=ps, t0=t0, n=n, sbt=sbt: e.tensor_tensor(
                    og[0:M, t0:t0 + n], ps[0:M, 0:n], sbt[0:M, t0:t0 + n], ALU.mult), [Bps, Bsb], [Bog],
                    join=not first)
            else:
                P.op("act", lambda e, og=og, ps=ps, t0=t0, n=n: e.copy(og[0:M, t0:t0 + n], ps[0:M, 0:n]),
                     [Bps], [Bog], join=not first)
            first = False
        if dst_ap is not None:
            P.dma(dq(), dst_ap, og[0:M, :], [Bog], [Bfmout], join=True)
        return og, Bog

    Bfmout = P.buf("fmout"); Bvout = P.buf("vout"); Bgout = P.buf("gout")

    def h_rhs(k, t0, n):
        return hT[:, k, t0:t0 + n], hbufs(t0, n)

    for (c0, M, ridx, dname, drow) in lay.fm:
        og, Bog = fm_job(c0, M, io[dname][drow:drow + M, :], wbf, Bw, h_rhs, 8,
                         rope=None if ridx is None else (wrp, Bwr, ridx * 128))
        hp = lay.halo.get((dname, drow))
        if hp is not None:
            hrow, hw, hrows = hp
            P.dma(dq(), io["hk"][hrow:hrow + M, :], og[0:M, 0:hw], [Bog], [Bfmout], join=True)
            P.dma(dq(), io["hk"][hrows + hrow:hrows + hrow + M, :], og[0:M, T - hw:T], [Bog], [Bfmout], join=True)

    if io.get("_stop") == 3:
        P.build(); return
    vst = Rot(P, "sb", "vst", [128, lay.VCOLS], BF16, 2)
    gst = Rot(P, "sb", "gst", [128, 1024], F32, 2)
    for i in range(2):
        mset(P, "pool", vst.t[i][:], 1.0, [vst.b[i]])

    def tm_mm(i, c0, ncols):
        ps, Bps = psA.next()
        for k in range(8):
            mm(P, ps[:, 0:ncols], hT[:, k, i * 128:(i + 1) * 128], wbf[:, k, c0:c0 + ncols], k == 0, k == 7,
               [BhT[i]] + Bw, [Bps])
        return ps, Bps

    if lay.idx == 1:
        gkvb = P.sbuf("gkvb", [128, 128], F32); Bgkvb = P.buf("gkvb")
        P.dma("sp", gkvb[:], bcast_rows(io["kvng_row"], 128), [], [Bgkvb])
        ckT = P.sbuf("ckT", [128, TT], BF16); BckT = P.buf("ckT")
        st2 = Rot(P, "sb", "st2", [128, 4], F32, 3)
        jk2 = Rot(P, "sb", "jk2", [128, 128], F32, 2)
        ptc = ptr

    for i in range(TT // 128):
        vt, Bvt = vst.next()
        wfirst = True
        for (c0, ncols, nh, e, dcol) in lay.tmv:
            ps, Bps = tm_mm(i, c0, ncols)
            dstv = vt[:, dcol:dcol + nh * (e + 1)].rearrange("p (h e) -> p h e", e=e + 1)[:, :, 0:e]
            srcv = ps[:, 0:ncols].rearrange("p (h e) -> p h e", e=e)
            P.op("act", lambda en, dstv=dstv, srcv=srcv: en.copy(dstv, srcv), [Bps], [Bvt], join=not wfirst)
            wfirst = False
        if lay.idx == 1:
            ps, Bps = tm_mm(i, 256, 128)
            s2, Bs2 = st2.next()
            j2, Bj2 = jk2.next()
            act(P, j2[:], ps[:, 0:128], AF.Square, [Bps], [Bj2, Bs2], accum_out=s2[:, 0:1])
            act(P, s2[:, 1:2], s2[:, 0:1], AF.Sqrt, [Bs2], [Bs2], scale=1.0 / 128, bias=EPS)
            recip(P, s2[:, 2:3], s2[:, 1:2], [Bs2], [Bs2])
            P.op("dve", lambda en, vt=vt, ps=ps, s2=s2: en.scalar_tensor_tensor(
                vt[:, 0:128], ps[:, 0:128], s2[:, 2:3], gkvb[:], ALU.mult, ALU.mult), [Bps, Bs2, Bgkvb], [Bvt], join=True)
            pc, Bpc = ptc.next()
            tr(P, pc[:, 0, :], vt[:, 0:128], identb[:], [Bvt, Bidb], [Bpc])
            P.op("act", lambda en, pc=pc, i=i: en.copy(ckT[:, i * 128:(i + 1) * 128], pc[:, 0, :]), [Bpc], [BckT], join=True)
        P.dma(dq(), io["v"][i * 128:(i + 1) * 128, :], vt[:], [Bvt], [Bvout], join=True)
        gt, Bgt = gst.next()
        for hh in range(2):
            ps, Bps = tm_mm(i, lay.gcol + 512 * hh, 512)
            P.op("act", lambda en, gt=gt, ps=ps, hh=hh: en.activation(gt[:, hh * 512:(hh + 1) * 512], ps[:], AF.Silu),
                 [Bps], [Bgt], join=(hh == 1))
        P.dma(dq(), io["sg"][i * 128:(i + 1) * 128, :], gt[:], [Bgt], [Bgout], join=True)

    if lay.idx == 1:
        P.dma(dq(), io["kt"][0:128, :], ckT[:], [BckT], [Bfmout], join=True)
        cqT = P.sbuf("cqT", [128, 2, TT], BF16); BcqT = P.buf("cqT")
        sqr = Rot(P, "sb", "sqr", [128, 2, 512], F32, 1)
        rsq = P.sbuf("rsq", [128, TT], F32); Brsq = P.buf("rsq")
        onesf = P.sbuf("onesf", [128, 128], F32); Bones = P.buf("onesf")
        mset(P, "pool", onesf[:], 1.0, [Bones])
        first = True
        for (t0, n) in ttiles:
            sq, Bsq = sqr.next()
            for kk in range(2):
                ps, Bps = psA.next()
                for k in range(8):
                    mm(P, ps[:, 0:n], wbf[:, k, kk * 128:(kk + 1) * 128], hT[:, k, t0:t0 + n], k == 0, k == 7,
                       Bw + hbufs(t0, n), [Bps])
                P.op("act", lambda e, ps=ps, kk=kk, t0=t0, n=n: e.copy(cqT[:, kk, t0:t0 + n], ps[:, 0:n]),
                     [Bps], [BcqT], join=not (first and kk == 0))
                P.op("dve", lambda e, ps=ps, sq=sq, kk=kk, n=n: e.tensor_copy(sq[:, kk, 0:n], ps[:, 0:n]),
                     [Bps], [Bsq], join=(kk == 1))
            P.op("pool", lambda e, sq=sq, n=n: e.tensor_tensor(sq[:, :, 0:n], sq[:, :, 0:n], sq[:, :, 0:n], ALU.mult),
                 [Bsq], [Bsq])
            ps, Bps = psB.next()
            for kk in range(2):
                mm(P, ps[:, 0:n], onesf[:], sq[:, kk, 0:n], kk == 0, kk == 1, [Bones, Bsq], [Bps])
            P.op("act", lambda e, ps=ps, t0=t0, n=n: e.activation(rsq[:, t0:t0 + n], ps[:, 0:n], AF.Sqrt,
                                                                scale=1.0 / 256, bias=EPS), [Bps], [Brsq], join=not first)
            first = False
        recip(P, rsq[:], rsq[:], [Brsq], [Brsq])
        wqf = P.sbuf("wqf", [128, 1024], F32); Bwqf = P.buf("wqf")
        qng = P.sbuf("qng", [128, 2], F32); Bqng = P.buf("qng")
        P.dma("sp", qng[:], io["qng_col"], [], [Bqng])
        wqb = P.sbuf("wqb", [128, 2, 1024], BF16); Bwqb = P.buf("wqb")
        for kk in range(2):
            P.dma("sp", wqf[:], io["w_qb_all"][kk * 128:(kk + 1) * 128, :], [], [Bwqf])
            P.op("dve", lambda e, kk=kk: e.tensor_scalar(wqb[:, kk, :], wqf[:], qng[:, kk:kk + 1], None, ALU.mult),
                 [Bwqf, Bqng], [Bwqb], join=(kk == 1))
        wkT = P.sbuf("wkT", [128, 4, 128], BF16); BwkT = P.buf("wkT")
        P.dma("pool", wkT[:], io["wkT"], [], [BwkT])

        def cq_rhs(k, t0, n):
            return cqT[:, k, t0:t0 + n], [BcqT]

        qn_rot = Rot(P, "sb", "qn", [128, 512], BF16, 2)
        for h in range(4):
            def post(ps, Bps, og, Bog, t0, n, first, h=h):
                qn, Bqn = qn_rot.next()
                tt(P, "dve", qn[:, 0:n], ps[:, 0:n], rsq[:, t0:t0 + n], ALU.mult, [Bps, Brsq], [Bqn])
                ps3, Bps3 = psB.next()
                mm(P, ps3[:, 0:n], wkT[:, h, :], qn[:, 0:n], True, True, [BwkT, Bqn], [Bps3])
                P.op("act", lambda e, og=og, ps3=ps3, t0=t0, n=n: e.copy(og[:, t0:t0 + n], ps3[:, 0:n]),
                     [Bps3], [Bog], join=not first)
            fm_job(192 * h, 128, io["q"][128 * h:128 * h + 128, :], wqb, [Bwqb], cq_rhs, 2, post=post)
            fm_job(192 * h + 128, 64, io["q"][512 + 64 * h:512 + 64 * h + 64, :], wqb, [Bwqb], cq_rhs, 2,
                   rope=(wqb, [Bwqb], 768 + 64 * h), scale_bc=(rsq, Brsq))
    P.build()


def load_common_B(P, io):
    identb = P.sbuf("identb", [128, 128], BF16); Bidb = P.buf("identb")
    P.dma("pool", identb[:], io["ident"], [], [Bidb])
    return identb, Bidb


def phase_BA0(nc, io, tag):
    P = Prog(nc, tag)
    scale = 64 ** -0.5
    qaT = P.sbuf("qaT", [64, 8, TT], BF16); Bqa = P.buf("qaT")
    P.dma("sp", qaT[:], io["q"][0:512, :].rearrange("(h d) t -> d h t", d=64), [], [Bqa])
    kaT = P.sbuf("kaT", [64, 2, TT], BF16); Bka = P.buf("kaT")
    P.dma("pool", kaT[:], io["kt"][0:128, :].rearrange("(j d) t -> d j t", d=64), [], [Bka])
    vaX = P.sbuf("vaX", [128, 18, 130], BF16); Bva = P.buf("vaX")
    P.dma("sp", vaX[:], io["v"][:, 0:130].rearrange("(c p) e -> p c e", p=128), [], [Bva])
    hidx = P.sbuf("hidx", [128, NIDX], I32); Bhidx = P.buf("hidx")
    P.dma("sp", hidx[:], io["hidx"], [], [Bhidx])
    kaH = P.sbuf("kaH", [64, 2, 2, 128], BF16); BkaH = P.buf("kaH")
    vaH = P.sbuf("vaH", [128, 2, 646], BF16); BvaH = P.buf("vaH")
    for side in range(2):
        for j in range(2):
            P.gather(kaH[:, j, side, :], io["hk_all"], hidx[0:64, 2 * side + j:2 * side + j + 1], [Bhidx], [BkaH],
                     join=not (side == 0 and j == 0))
        P.gather(vaH[:, side, :], io["v_all"], hidx[:, 4 + side:5 + side], [Bhidx], [BvaH], join=(side > 0))
    mk = P.sbuf("mk", [128, 4, 512], BF16); Bmk = P.buf("mk")
    P.dma("pool", mk[:], io["maskA"].rearrange("m p f -> p m f"), [], [Bmk])
    sk = P.sbuf("sk", [128, 8], F32); Bsk = P.buf("sk")
    P.dma("sp", sk[:], bcast_rows(io["a_sink"], 128), [], [Bsk])
    esk = P.sbuf("esk", [128, 8], F32); Besk = P.buf("esk")
    act(P, esk[:], sk[:], AF.Exp, [Bsk], [Besk])
    psS = Rot(P, "ps", "psS", [128, 512], F32, 2)
    accr = Rot(P, "ps", "acc", [128, 512], F32, 2)
    ptr_ = Rot(P, "sb", "pT", [128, 512], BF16, 3)
    ytr = Rot(P, "sb", "yt", [128, 512], F32, 2)
    ztr = Rot(P, "sb", "zt", [128, 8], F32, 2)
    By = P.bufs_n("y", 5)
    for n in range(TT // 128):
        own = n < NTB
        q0 = n * 128
        yt, Byt = ytr.next()
        for j in range(2):
            def kown(c, j=j):
                return (kaT[:, j, c * 128:(c + 1) * 128], vaX[:, c, 65 * j:65 * j + 65], [Bka, Bva])

            def khalo(side, j=j):
                return (kaH[:, j, side, :], vaH[:, side, 65 * j:65 * j + 65], [BkaH, BvaH])
            if own:
                chunks = [(khalo(0) if n == 0 else kown(n - 1), 0 if n == 0 else 1), (kown(n), None),
                          (khalo(1) if n == NTB - 1 else kown(n + 1), 3 if n == NTB - 1 else 2),
                          (kown(16), None), (kown(17), None)]
            else:
                chunks = [(kown(16), None), (kown(17), None)]
            acc_, Bacc = accr.next()
            acc = acc_[:, 0:260].rearrange("p (g e) -> p g e", g=4)
            for ci, ((kap, vap, kvb), m) in enumerate(chunks):
                ps, Bps = psS.next()
                mm(P, ps[:, :].rearrange("p (g q) -> p g q", g=4), kap,
                   qaT[:, 4 * j:4 * j + 4, q0:q0 + 128], True, True, kvb + [Bqa], [Bps])
                pT, BpT = ptr_.next()
                act(P, pT[:], ps[:], AF.Exp, [Bps], [BpT], scale=scale)
                if m is not None:
                    tt(P, "pool", pT[:], pT[:], mk[:, m, :], ALU.mult, [BpT, Bmk], [BpT])
                for g in range(4):
                    mm(P, acc[:, g, :], pT[:, g * 128:(g + 1) * 128], vap,
                       ci == 0 and g == 0, ci == len(chunks) - 1, [BpT] + kvb, [Bacc])
            zt, Bzt = ztr.next()
            tt(P, "dve", zt[:, 0:4], acc[:, :, 64], esk[:, 4 * j:4 * j + 4], ALU.add, [Bacc, Besk], [Bzt])
            recip(P, zt[:, 4:8], zt[:, 0:4], [Bzt], [Bzt])
            for g in range(4):
                hd = 4 * j + g
                P.op("dve", lambda e, yt=yt, acc=acc, zt=zt, g=g, hd=hd: e.tensor_scalar(
                    yt[:, hd * 64:(hd + 1) * 64], acc[:, g, 0:64], zt[:, 4 + g:5 + g], None, ALU.mult),
                    [Bacc, Bzt], [Byt], join=not (j == 0 and g == 0))
        P.dma("sp", io["y"][q0:q0 + 128, 0:512], yt[:], [Byt], [By[n // 4]], join=True)
    P.build()


def full_attn_pass(P, qk_list, nq, chunks, vaug_fn, vbufs, scale, psS, accs, ptr_, E1):
    nb = nq // 128
    a0, Ba0, a1, Ba1 = accs
    for ci, ch in enumerate(chunks):
        ps, Bps = psS.next()
        for qi, (kT_fn, qT, bl) in enumerate(qk_list):
            mm(P, ps[:, 0:nq], kT_fn(ch), qT, qi == 0, qi == len(qk_list) - 1, bl, [Bps])
        pT, BpT = ptr_.next()
        act(P, pT[:, 0:nq], ps[:, 0:nq], AF.Exp, [Bps], [BpT], scale=scale)
        for b in range(nb):
            at, Bat = (a0, Ba0) if b < 2 else (a1, Ba1)
            mm(P, at[:, b % 2, :], pT[:, b * 128:(b + 1) * 128], vaug_fn(ch), ci == 0 and b % 2 == 0, ci == len(chunks) - 1,
               [BpT] + vbufs, [Bat])


def phase_BB0(nc, io, tag):
    P = Prog(nc, tag)
    scale = 64 ** -0.5
    lam_init = 0.8 - 0.6 * math.exp(-0.3 * 0)
    NCH = S // 128 + 2
    qbT = P.sbuf("qbT", [128, 4, TT], BF16); Bqb = P.buf("qbT")
    P.dma("sp", qbT[:], io["q"][512:1024, :].rearrange("(h r) t -> r h t", r=128), [], [Bqb])
    lb = P.sbuf("lb", [128, 256], F32); Blb = P.buf("lb")
    P.dma("sp", lb[:], bcast_rows(io["b_lambda"], 128), [], [Blb])
    lt = P.sbuf("lt", [128, 128], F32); Blt = P.buf("lt")
    ls = P.sbuf("ls", [128, 8], F32); Bls = P.buf("ls")
    tt(P, "dve", lt[:].rearrange("p (a b) -> p a b", a=2), lb[:].rearrange("p (a b c) -> p a b c", a=2, b=2)[:, :, 0, :],
       lb[:].rearrange("p (a b c) -> p a b c", a=2, b=2)[:, :, 1, :], ALU.mult, [Blb], [Blt])
    P.op("dve", lambda e: e.reduce_sum(ls[:, 0:2], lt[:].rearrange("p (a b) -> p a b", a=2), AX.X), [Blt], [Bls])
    act(P, ls[:, 2:4], ls[:, 0:2], AF.Exp, [Bls], [Bls])
    tt(P, "dve", ls[:, 4:5], ls[:, 3:4], ls[:, 2:3], ALU.subtract, [Bls], [Bls])
    tsc(P, "dve", ls[:, 5:6], ls[:, 4:5], -lam_init, None, ALU.add, None, [Bls], [Bls])
    sgl = P.sbuf("sgl", [128, 128], F32); Bsgl = P.buf("sgl")
    P.dma("sp", sgl[:], bcast_rows(io["subln_g"], 128), [], [Bsgl])
    tsc(P, "dve", sgl[:], sgl[:], 1.0 - lam_init, None, ALU.mult, None, [Bsgl], [Bsgl])

    kbr = Rot(P, "sb", "kb", [128, NCH * 128], BF16, 2)
    vbr = Rot(P, "sb", "vb", [128, NCH, 129], BF16, 2)
    psS = Rot(P, "ps", "psS", [128, 512], F32, 2)
    acc_f = [P.psum(f"acc{i}", [128, 512], F32) for i in range(4)]
    acc_t = [a[:, 0:258].rearrange("p (b e) -> p b e", b=2) for a in acc_f]
    acc_b = [P.buf(f"acc{i}", excl=True) for i in range(4)]
    ptr_ = Rot(P, "sb", "pT", [128, 512], BF16, 3)
    Or = [P.sbuf(f"O{t}", [128, 4, 129], F32) for t in range(2)]
    BO = [P.buf(f"O{t}") for t in range(2)]
    ur = Rot(P, "sb", "u", [128, 128], F32, 2)
    jr = Rot(P, "sb", "jk", [128, 128], F32, 2)
    sr = Rot(P, "sb", "s", [128, 8], F32, 3)
    ybr = Rot(P, "sb", "yb", [128, 4, 128], F32, 2)
    By = P.bufs_n("y", 5)
    qtiles = [(0, 512), (512, 512), (1024, 512), (1536, 512), (2048, 256)]
    for h in range(4):
        kb, Bkb = kbr.next()
        vb, Bvb = vbr.next()
        for r in range(NCORES):
            P.dma("sp" if r % 2 else "pool", kb[:, r * T:(r + 1) * T], io["kt_all"][r, 128 + 128 * h:256 + 128 * h, 0:T],
                  [], [Bkb], join=(r > 0))
            P.dma("pool" if r % 2 else "sp", vb[:, r * 16:(r + 1) * 16, :],
                  io["v_all"][r, 0:T, 130 + 129 * h:259 + 129 * h].rearrange("(c p) e -> p c e", p=128),
                  [], [Bvb], join=(r > 0))
        P.dma("sp", kb[:, S:S + L], io["kt_all"][0, 128 + 128 * h:256 + 128 * h, T:TT], [], [Bkb], join=True)
        P.dma("pool", vb[:, 128:130, :], io["v_all"][0, T:TT, 130 + 129 * h:259 + 129 * h].rearrange("(c p) e -> p c e", p=128),
              [], [Bvb], join=True)
        for qi_, (q0, nq) in enumerate(qtiles):
            nb = nq // 128
            chunks = list(range(NCH)) if q0 < T else [128, 129]
            for t in range(2):
                accs = (acc_t[2 * t], acc_b[2 * t], acc_t[2 * t + 1], acc_b[2 * t + 1])
                full_attn_pass(P, [(lambda ch, t=t, kb=kb: kb[64 * t:64 * t + 64, ch * 128:(ch + 1) * 128],
                                    qbT[64 * t:64 * t + 64, h, q0:q0 + nq], [Bkb, Bqb])],
                               nq, chunks, lambda ch, vb=vb: vb[:, ch, :], [Bvb], scale, psS, accs, ptr_, 129)
                P.op("act", lambda e, t=t: e.copy(Or[t][:, 0:2, :], acc_t[2 * t]), [acc_b[2 * t]], [BO[t]])
                if nb > 2:
                    P.op("act", lambda e, t=t: e.copy(Or[t][:, 2:4, :], acc_t[2 * t + 1]), [acc_b[2 * t + 1]], [BO[t]], join=True)
            yb, Byb = ybr.next()
            for b in range(nb):
                s_, Bs = sr.next()
                P.op("dve", lambda e, s_=s_, b=b: e.reciprocal(s_[:, 0:1], Or[0][:, b, 128:129]), [BO[0]], [Bs])
                P.op("dve", lambda e, s_=s_, b=b: e.reciprocal(s_[:, 1:2], Or[1][:, b, 128:129]), [BO[1]], [Bs])
                tt(P, "dve", s_[:, 2:3], s_[:, 1:2], ls[:, 5:6], ALU.mult, [Bs, Bls], [Bs])
                u, Bu = ur.next()
                P.op("dve", lambda e, u=u, s_=s_, b=b: e.tensor_scalar(u[:], Or[0][:, b, 0:128], s_[:, 0:1], None, ALU.mult),
                     [BO[0], Bs], [Bu])
                P.op("dve", lambda e, u=u, s_=s_, b=b: e.scalar_tensor_tensor(u[:], Or[1][:, b, 0:128], s_[:, 2:3], u[:],
                                                                               ALU.mult, ALU.add), [BO[1], Bs, Bu], [Bu])
                jk, Bjk = jr.next()
                act(P, jk[:], u[:], AF.Square, [Bu], [Bjk, Bs], accum_out=s_[:, 3:4])
                act(P, s_[:, 4:5], s_[:, 3:4], AF.Sqrt, [Bs], [Bs], scale=1.0 / 128, bias=EPS)
                recip(P, s_[:, 5:6], s_[:, 4:5], [Bs], [Bs])
                P.op("dve", lambda e, yb=yb, u=u, s_=s_, b=b: e.scalar_tensor_tensor(
                    yb[:, b, :], u[:], s_[:, 5:6], sgl[:], ALU.mult, ALU.mult), [Bu, Bs, Bsgl], [Byb], join=(b > 0))
            P.dma("sp", io["y"][q0:q0 + nq, 512 + 128 * h:640 + 128 * h].rearrange("(b p) c -> p b c", p=128),
                  yb[:, 0:nb, :], [Byb], [By[qi_]], join=True)
    P.build()


def phase_BO(nc, io, tag, last):
    P = Prog(nc, tag)
    identb, Bidb = load_common_B(P, io)
    wo = P.sbuf("wo", [128, 8, 1024], BF16); Bwo = P.buf("wo")
    P.dma("pool", wo[:], io["w_out"].rearrange("(k p) c -> p k c", p=128), [], [Bwo])
    gate = P.sbuf("gate", [128, 2, 1024], F32); Bgate = P.buf("gate")
    for j in range(2):
        P.dma("sp", gate[:, j, :], bcast_rows(io["mod"][j:j + 1, 2048:3072], 128), [], [Bgate], join=(j > 0))
    if last:
        fng = P.sbuf("fng", [128, 1024], F32); Bfng = P.buf("fng")
        P.dma("sp", fng[:], bcast_rows(io["final_g"], 128), [], [Bfng])
    yr = Rot(P, "sb", "yt", [128, 1024], F32, 2)
    gr = Rot(P, "sb", "gt", [128, 1024], F32, 2)
    xr = Rot(P, "sb", "xt", [128, 1024], F32, 2)
    ygr = Rot(P, "sb", "yg", [128, 1024], BF16, 2)
    ptr_ = Rot(P, "ps", "pt", [128, 8, 128], BF16, 2)
    ygTr = Rot(P, "sb", "ygT", [128, 8, 128], BF16, 2)
    pso = Rot(P, "ps", "pso", [128, 512], F32, 3)
    tmr = Rot(P, "sb", "tm", [128, 1024], F32, 2)
    x1r = Rot(P, "sb", "x1", [128, 1024], F32, 2)
    sr = Rot(P, "sb", "s", [128, 4], F32, 3)
    jr = Rot(P, "sb", "jk", [128, 1024], BF16, 2)
    Bout = P.buf("outd")
    nblk = NTB if last else TT // 128
    for i in range(nblk):
        own = i < NTB
        j = 0 if own else 1
        r0 = i * 128
        yt, Byt = yr.next(); gt, Bgt = gr.next(); xt, Bxt = xr.next()
        P.dma("sp", yt[:], io["y"][r0:r0 + 128, :], [], [Byt])
        P.dma("pool", gt[:], io["sg"][r0:r0 + 128, :], [], [Bgt])
        P.dma("sp", xt[:], io["x"][r0:r0 + 128, :] if own else io["xc"][r0 - T:r0 - T + 128, :], [], [Bxt])
        yg, Byg = ygr.next()
        tt(P, "pool", yg[:], yt[:], gt[:], ALU.mult, [Byt, Bgt], [Byg])
        pt, Bpt = ptr_.next()
        for k in range(8):
            tr(P, pt[:, k, :], yg[:, k * 128:(k + 1) * 128], identb[:], [Byg, Bidb], [Bpt])
        ygT, BygT = ygTr.next()
        if i % 2 == 0:
            P.op("act", lambda e, ygT=ygT, pt=pt: e.copy(ygT[:], pt[:]), [Bpt], [BygT])
        else:
            P.op("dve", lambda e, ygT=ygT, pt=pt: e.tensor_copy(ygT[:], pt[:]), [Bpt], [BygT])
        tm, Btm = tmr.next()
        x1, Bx1 = x1r.next()
        for ct in range(2):
            ps, Bps = pso.next()
            for k in range(8):
                mm(P, ps[:], ygT[:, k, :], wo[:, k, ct * 512:(ct + 1) * 512], k == 0, k == 7, [BygT, Bwo], [Bps])
            P.op("dve", lambda e, tm=tm, ps=ps, ct=ct, j=j: e.tensor_tensor(
                tm[:, ct * 512:(ct + 1) * 512], ps[:], gate[:, j, ct * 512:(ct + 1) * 512], ALU.mult),
                [Bps, Bgate], [Btm], join=(ct == 1))
        tt(P, "pool", x1[:], xt[:], tm[:], ALU.add, [Bxt, Btm], [Bx1])
        if last:
            s_, Bs = sr.next()
            jk, Bjk = jr.next()
            act(P, jk[:], x1[:], AF.Square, [Bx1], [Bjk, Bs], accum_out=s_[:, 0:1])
            act(P, s_[:, 1:2], s_[:, 0:1], AF.Sqrt, [Bs], [Bs], scale=1.0 / D, bias=EPS)
            recip(P, s_[:, 2:3], s_[:, 1:2], [Bs], [Bs])
            P.op("dve", lambda e, x1=x1, s_=s_, tm=tm: e.scalar_tensor_tensor(
                tm[:], x1[:], s_[:, 2:3], fng[:], ALU.mult, ALU.mult), [Bx1, Bs, Bfng], [Btm])
            P.dma("sp", io["out"][r0:r0 + 128, :], tm[:], [Btm], [Bout], join=True)
        else:
            dst = io["x1"][r0:r0 + 128, :] if own else io["xc1"][r0 - T:r0 - T + 128, :]
            P.dma("sp", dst, x1[:], [Bx1], [Bout], join=True)
    P.build()


def phase_BC1(nc, io, tag):
    P = Prog(nc, tag)
    scale = 192 ** -0.5
    NCH = S // 128 + 2
    identb, Bidb = load_common_B(P, io)
    qlT = P.sbuf("qlT", [128, 4, T], BF16); Bql = P.buf("qlT")
    P.dma("sp", qlT[:], io["q"][0:512, 0:T].rearrange("(h r) t -> r h t", r=128), [], [Bql])
    qpT = P.sbuf("qpT", [64, 4, T], BF16); Bqp = P.buf("qpT")
    P.dma("pool", qpT[:], io["q"][512:768, 0:T].rearrange("(h r) t -> r h t", r=64), [], [Bqp])
    wv = P.sbuf("wv", [128, 4, 128], BF16); Bwv = P.buf("wv")
    P.dma("pool", wv[:], io["wv"], [], [Bwv])
    kl = P.sbuf("kl", [128, NCH * 128], BF16); Bkl = P.buf("kl")
    kp = P.sbuf("kp", [64, NCH * 128], BF16); Bkp = P.buf("kp")
    vl = P.sbuf("vl", [128, NCH, 129], BF16); Bvl = P.buf("vl")
    for r in range(NCORES):
        P.dma("sp" if r % 2 else "pool", kl[:, r * T:(r + 1) * T], io["kt_all"][r, 0:128, 0:T], [], [Bkl], join=(r > 0))
        P.dma("pool" if r % 2 else "sp", kp[:, r * T:(r + 1) * T], io["kt_all"][r, 128:192, 0:T], [], [Bkp], join=(r > 0))
        P.dma("sp", vl[:, r * 16:(r + 1) * 16, :], io["v_all"][r, 0:T, 0:129].rearrange("(c p) e -> p c e", p=128),
              [], [Bvl], join=(r > 0))
    P.dma("sp", kl[:, S:S + L], io["kt_all"][0, 0:128, T:TT], [], [Bkl], join=True)
    P.dma("sp", kp[:, S:S + L], io["kt_all"][0, 128:192, T:TT], [], [Bkp], join=True)
    P.dma("pool", vl[:, 128:130, :], io["v_all"][0, T:TT, 0:129].rearrange("(c p) e -> p c e", p=128), [], [Bvl], join=True)
    psS = Rot(P, "ps", "psS", [128, 512], F32, 2)
    acc_f = [P.psum(f"acc{i}", [128, 512], F32) for i in range(4)]
    acc_t = [a[:, 0:258].rearrange("p (b e) -> p b e", b=2) for a in acc_f]
    acc_b = [P.buf(f"acc{i}", excl=True) for i in range(4)]
    ptr_ = Rot(P, "sb", "pT", [128, 512], BF16, 3)
    pst = Rot(P, "ps", "pst", [128, 8, 128], BF16, 1)
    pso = Rot(P, "ps", "pso", [128, 512], F32, 1)
    Or = Rot(P, "sb", "O", [128, 4, 129], F32, 2)
    sr = Rot(P, "sb", "s", [128, 4], F32, 3)
    ur = Rot(P, "sb", "u", [128, 128], BF16, 2)
    uTr = Rot(P, "sb", "uT", [128, 128], BF16, 2)
    ybr = Rot(P, "sb", "yb", [128, 4, 128], F32, 2)
    By = P.bufs_n("y", 5)
    par = 0
    for h in range(4):
        for qi_ in range(4):
            q0 = qi_ * 512
            accs = (acc_t[2 * par], acc_b[2 * par], acc_t[2 * par + 1], acc_b[2 * par + 1])
            full_attn_pass(P, [(lambda ch: kl[:, ch * 128:(ch + 1) * 128], qlT[:, h, q0:q0 + 512], [Bkl, Bql]),
                               (lambda ch: kp[:, ch * 128:(ch + 1) * 128], qpT[:, h, q0:q0 + 512], [Bkp, Bqp])],
                           512, list(range(NCH)), lambda ch: vl[:, ch, :], [Bvl], scale, psS, accs, ptr_, 129)
            O, BO = Or.next()
            P.op("act", lambda e, O=O, par=par: e.copy(O[:, 0:2, :], acc_t[2 * par]), [acc_b[2 * par]], [BO])
            P.op("act", lambda e, O=O, par=par: e.copy(O[:, 2:4, :], acc_t[2 * par + 1]), [acc_b[2 * par + 1]], [BO], join=True)
            par ^= 1
            yb, Byb = ybr.next()
            for b in range(4):
                s_, Bs = sr.next()
                P.op("dve", lambda e, s_=s_, O=O, b=b: e.reciprocal(s_[:, 0:1], O[:, b, 128:129]), [BO], [Bs])
                u, Bu = ur.next()
                P.op("dve", lambda e, u=u, s_=s_, O=O, b=b: e.tensor_scalar(u[:], O[:, b, 0:128], s_[:, 0:1], None, ALU.mult),
                     [BO, Bs], [Bu])
                pt, Bpt = pst.next()
                tr(P, pt[:, 0, :], u[:], identb[:], [Bu, Bidb], [Bpt])
                uT, BuT = uTr.next()
                P.op("act", lambda e, uT=uT, pt=pt: e.copy(uT[:], pt[:, 0, :]), [Bpt], [BuT])
                po, Bpo = pso.next()
                mm(P, po[:, 0:128], uT[:], wv[:, h, :], True, True, [BuT, Bwv], [Bpo])
                P.op("dve", lambda e, yb=yb, po=po, b=b: e.tensor_copy(yb[:, b, :], po[:, 0:128]), [Bpo], [Byb], join=(b > 0))
            P.dma("sp", io["y"][q0:q0 + 512, 128 * h:128 * h + 128].rearrange("(b p) c -> p b c", p=128),
                  yb[:], [Byb], [By[qi_]], join=True)
    P.build()


def bcast_mid(ap2d, n):
    a = [list(x) for x in ap2d.ap]
    return bass.AP(ap2d.tensor, ap2d.offset, [a[0], [0, n]] + a[1:])


def phase_BD1(nc, io, tag):
    P = Prog(nc, tag)
    scale = 64 ** -0.5
    NX = 22
    qdT = P.sbuf("qdT", [64, 8, T], BF16); Bqd = P.buf("qdT")
    P.dma("sp", qdT[:], io["q"][768:1280, 0:T].rearrange("(h d) t -> d h t", d=64), [], [Bqd])
    kdX = P.sbuf("kdX", [64, 8, TT], BF16); Bkd = P.buf("kdX")
    P.dma("pool", kdX[:], io["kt"][192:704, :].rearrange("(h d) t -> d h t", d=64), [], [Bkd])
    vdX = P.sbuf("vdX", [128, 18, 520], BF16); Bvd = P.buf("vdX")
    P.dma("sp", vdX[:], io["v"][:, 129:649].rearrange("(c p) e -> p c e", p=128), [], [Bvd])
    hidx = P.sbuf("hidx", [128, NIDX], I32); Bhidx = P.buf("hidx")
    P.dma("sp", hidx[:], io["hidx"], [], [Bhidx])
    kdH = P.sbuf("kdH", [64, 8, 2, 256], BF16); BkdH = P.buf("kdH")
    vdH = P.sbuf("vdH", [128, 4, 649], BF16); BvdH = P.buf("vdH")
    for side in range(2):
        for h in range(8):
            c = 6 + 8 * side + h
            P.gather(kdH[:, h, side, :], io["hk_all"], hidx[0:64, c:c + 1], [Bhidx], [BkdH], join=not (side == 0 and h == 0))
        for c2 in range(2):
            c = 22 + 2 * side + c2
            P.gather(vdH[:, 2 * side + c2, :], io["v_all"], hidx[:, c:c + 1], [Bhidx], [BvdH], join=not (side == 0 and c2 == 0))
    cm = P.sbuf("cm", [128, 64], F32); Bcm = P.buf("cm")
    P.dma("sp", cm[:], io["colmask"], [], [Bcm])
    vm = P.sbuf("vm", [128, NTB, 6, 2], F32); Bvm = P.buf("vm")
    P.dma("sp", vm[:], io["vmD"], [], [Bvm])
    TB = P.sbuf("TB", [128, 120, 64], F32); BTB = P.buf("TB")
    rp = io["rp_pad"]
    TBr = P.sbuf("TBr", [128, 120 * 64], F32); BTBr = P.buf("TBr")
    for a in range(2):
        src = bass.AP(rp.tensor, rp.offset, [[1, 64], [128, 120], [1, 64]])
        P.dma("sp" if a else "pool", TBr[64 * a:64 * a + 64, :].rearrange("p (r q) -> p r q", q=64), src, [], [BTBr], join=(a > 0))
    J2 = P.sbuf("J2", [128, 128], F32); BJ2 = P.buf("J2")
    P.dma("sp", J2[:], io["antidiag"], [], [BJ2])
    psT = Rot(P, "ps", "psT", [128, 512], F32, 2)
    TBf = TB[:].rearrange("p r q -> p (r q)")
    for cchunk in range(15):
        ps, Bps = psT.next()
        mm(P, ps[:], J2[:], TBr[:, cchunk * 512:(cchunk + 1) * 512], True, True, [BJ2, BTBr], [Bps])
        P.op("act", lambda e, ps=ps, cchunk=cchunk: e.activation(TBf[:, cchunk * 512:(cchunk + 1) * 512], ps[:], AF.Exp),
             [Bps], [BTB], join=(cchunk > 0))
    tt(P, "dve", TB[:], TB[:], bcast_mid(cm[:], 120), ALU.mult, [BTB, Bcm], [BTB])
    TB4 = TB[:].rearrange("p (h r) q -> p h r q", h=8)
    psS = Rot(P, "ps", "psS", [128, 512], F32, 2)
    accr = Rot(P, "ps", "acc", [128, 512], F32, 4)
    ptr_ = Rot(P, "sb", "pT", [128, 512], BF16, 3)
    ebr = Rot(P, "sb", "eb", [128, 8, 128], BF16, 3)
    ytr = Rot(P, "sb", "yt", [128, 512], F32, 2)
    ztr = Rot(P, "sb", "zt", [128, 8], F32, 2)
    By = P.bufs_n("y", 5)
    for n in range(NTB):
        q0 = n * 128
        cis = list(range(0, 6)) if n == 0 else (list(range(-1, 5)) if n == NTB - 1 else list(range(0, 5)))
        def ksrc(oc):
            if oc < 0:
                return (lambda h, oc=oc: kdH[:, h, 0, (oc + 2) * 128:(oc + 3) * 128],
                        lambda h, oc=oc: vdH[:, oc + 2, 129 + 65 * h:129 + 65 * h + 65], [BkdH, BvdH])
            if oc >= NTB and oc < NTB + 2:
                return (lambda h, oc=oc: kdH[:, h, 1, (oc - NTB) * 128:(oc - NTB + 1) * 128],
                        lambda h, oc=oc: vdH[:, 2 + oc - NTB, 129 + 65 * h:129 + 65 * h + 65], [BkdH, BvdH])
            if oc >= 100:
                oc = NTB + (oc - 100)
            return (lambda h, oc=oc: kdX[:, h, oc * 128:(oc + 1) * 128],
                    lambda h, oc=oc: vdX[:, oc, 65 * h:65 * h + 65], [Bkd, Bvd])
        chunks = [(ksrc(n + ci - 2), ci) for ci in cis] + [(ksrc(100), None), (ksrc(101), None)]
        accs = [accr.next() for _ in range(2)]
        for idx, ((kfn, vfn, kvb), ci) in enumerate(chunks):
            eb = None
            if ci is not None:
                eb, Beb = ebr.next()
                slot = ci - cis[0]
                first = True
                for a in range(2):
                    for b in range(2):
                        dr = 3 + 2 * ci + a - b
                        P.op("pool", lambda e, eb=eb, a=a, b=b, dr=dr, n=n, slot=slot: e.tensor_scalar(
                            eb[64 * a:64 * a + 64, :, 64 * b:64 * b + 64], TB4[64 * a:64 * a + 64, :, dr, :],
                            vm[64 * a:64 * a + 64, n, slot, b:b + 1], None, ALU.mult), [BTB, Bvm], [Beb], join=not first)
                        first = False
            for hg in range(2):
                ps, Bps = psS.next()
                for hh in range(4):
                    h = 4 * hg + hh
                    mm(P, ps[:, hh * 128:(hh + 1) * 128], kfn(h), qdT[:, h, q0:q0 + 128],
                       True, True, kvb + [Bqd], [Bps])
                pT, BpT = ptr_.next()
                act(P, pT[:], ps[:], AF.Exp, [Bps], [BpT], scale=scale)
                if eb is not None:
                    P.op("dve", lambda e, pT=pT, eb=eb, hg=hg: e.tensor_tensor(
                        pT[:], pT[:], eb[:, 4 * hg:4 * hg + 4, :].rearrange("p h q -> p (h q)"), ALU.mult),
                        [BpT, Beb], [BpT])
                acc_, Bacc = accs[hg]
                acc = acc_[:, 0:260].rearrange("p (g e) -> p g e", g=4)
                for hh in range(4):
                    h = 4 * hg + hh
                    mm(P, acc[:, hh, :], pT[:, hh * 128:(hh + 1) * 128], vfn(h),
                       idx == 0 and hh == 0, idx == len(chunks) - 1, [BpT] + kvb, [Bacc])
        yt, Byt = ytr.next()
        for hg in range(2):
            acc_, Bacc = accs[hg]
            acc = acc_[:, 0:260].rearrange("p (g e) -> p g e", g=4)
            zt, Bzt = ztr.next()
            P.op("dve", lambda e, zt=zt, acc=acc: e.reciprocal(zt[:, 0:4], acc[:, :, 64]), [Bacc], [Bzt])
            for hh in range(4):
                h = 4 * hg + hh
                P.op("dve", lambda e, yt=yt, acc=acc, zt=zt, hh=hh, h=h: e.tensor_scalar(
                    yt[:, h * 64:(h + 1) * 64], acc[:, hh, 0:64], zt[:, hh:hh + 1], None, ALU.mult),
                    [Bacc, Bzt], [Byt], join=not (hg == 0 and hh == 0))
        P.dma("sp", io["y"][q0:q0 + 128, 512:1024], yt[:], [Byt], [By[n // 4]], join=True)
    P.build()


def _rope_tables():
    t = np.arange(S)
    row = (t // GRID_W).astype(np.float32)
    col = (t % GRID_W).astype(np.float32)
    inv = (np.float32(10000.0) ** (-np.arange(16, dtype=np.float32) / np.float32(16))).astype(np.float32)
    ang_r = row[:, None] * inv[None, :]
    ang_c = col[:, None] * inv[None, :]
    ang = np.concatenate([ang_r, ang_r, ang_c, ang_c], axis=-1).astype(np.float32)
    return np.cos(ang).astype(np.float32), np.sin(ang).astype(np.float32)


def _rope_core_tables(r, cos, sin):
    cT = np.ones((128, TT), np.float32)
    sT = np.zeros((128, TT), np.float32)
    c = cos[r * T:(r + 1) * T].T
    s_ = (sin[r * T:(r + 1) * T] * ROPE_SIGN[None, :]).T
    cT[0:64, 0:T] = c; cT[64:128, 0:T] = c
    sT[0:64, 0:T] = s_; sT[64:128, 0:T] = s_
    return cT, sT


class IO(dict):
    pass


def _mk(nc, specs):
    io = IO()
    for (name, shape, dt, kind) in specs:
        io[name] = nc.dram_tensor(name, list(shape), dt, kind=kind).ap()
    return io


def _dt(a):
    if a.dtype == np.float32:
        return F32
    if a.dtype == NPBF:
        return BF16
    if a.dtype == np.int32:
        return I32
    raise ValueError(a.dtype)


def _launch(build, in_maps, outs):
    nc = bass.Bass("TRN2", target_bir_lowering=False)
    specs = [(k, v.shape, _dt(v), "ExternalInput") for k, v in in_maps[0].items()]
    specs += [(k, shp, dt, "ExternalOutput") for (k, shp, dt) in outs]
    io = _mk(nc, specs)
    build(nc, io)
    res = run_bass_kernel_spmd(nc, in_maps, core_ids=list(range(NCORES)))
    return res.results


def _arr_col(v):
    return np.ascontiguousarray(v.reshape(8, 128).T)


def _maskA(r):
    kk = np.arange(128)[:, None]
    qq = np.arange(128)[None, :]
    tp = np.tile((qq <= kk).astype(np.float32), (1, 4))
    tn = np.tile((kk <= qq).astype(np.float32), (1, 4))
    z = np.zeros_like(tp)
    return np.stack([z if r == 0 else tp, tp, tn, z if r == NCORES - 1 else tn]).astype(NPBF)


def _colmask():
    qc = np.arange(64)
    cs = np.clip(qc - 8, 0, 48)
    kc = np.arange(64)[:, None]
    m = ((kc >= cs[None, :]) & (kc < cs[None, :] + 16)).astype(np.float32)
    return np.ascontiguousarray(np.concatenate([m, m], 0))


def _antidiag():
    j = np.zeros((128, 128), np.float32)
    for a in range(2):
        for kc in range(64):
            j[64 * a + 63 - kc, 64 * a + kc] = 1.0
    return j


def _vmD(r):
    vm = np.zeros((128, NTB, 6, 2), np.float32)
    for n in range(NTB):
        cis = list(range(0, 6)) if n == 0 else (list(range(-1, 5)) if n == NTB - 1 else list(range(0, 5)))
        for slot, ci in enumerate(cis):
            for a in range(2):
                for b in range(2):
                    gr = 32 * r + 2 * n + b
                    rs = min(max(gr - 4, 0), 248)
                    kr = 32 * r + 2 * n - 4 + 2 * ci + a
                    if rs <= kr <= rs + 7:
                        vm[64 * a:64 * a + 64, n, slot, b] = 1.0
    return vm


def phase_AG(nc, pairs, tag):
    P = Prog(nc, tag)
    prev = []
    for i, (own, allg) in enumerate(pairs):
        b = P.buf(f"ag{i}")
        P.collective("AllGather", own, allg, prev, [b])
        prev = [b]
    P.build()


def _hidx(r):
    rp, rn = max(r - 1, 0), min(r + 1, NCORES - 1)
    p = np.arange(128, dtype=np.int64)
    cols = []
    for j in range(2):
        cols.append(rp * 256 + 128 + 64 * j + p)
    for j in range(2):
        cols.append(rn * 256 + 0 + 64 * j + p)
    cols.append(rp * TT + (T - 128) + p)
    cols.append(rn * TT + 0 + p)
    for h in range(8):
        cols.append(rp * 1024 + 512 + 64 * h + p)
    for h in range(8):
        cols.append(rn * 1024 + 0 + 64 * h + p)
    for c2 in range(2):
        cols.append(rp * TT + (T - 256) + 128 * c2 + p)
    for c2 in range(2):
        cols.append(rn * TT + 128 * c2 + p)
    a = np.stack(cols, 1)
    a[64:, 0:4] = 0
    a[64:, 6:22] = 0
    return np.ascontiguousarray(a.astype(np.int32))


def _layer_inputs(lay, sfx, norm_g, w_ada, b_ada, w_in):
    cols = []
    for c0 in lay.rope_cols:
        for hh in range(2):
            cols.append(c0 + 64 * hh + ROPE_PERM)
    cols = np.concatenate(cols)
    if lay.idx == 1:
        cols = np.concatenate([384 + ROPE_PERM, 384 + ROPE_PERM])
    return {"norm_g" + sfx: _arr_col(norm_g), "b_ada2" + sfx: np.ascontiguousarray(np.tile(b_ada.reshape(1, -1), (2, 1))),
            "w_ada" + sfx: np.ascontiguousarray(w_ada), "w_in" + sfx: np.ascontiguousarray(w_in),
            "w_rope" + sfx: np.ascontiguousarray(w_in[:, cols])}


_STOP = [None]


def build_fused(nc, ext):
    def scr(name, shape, dt):
        return nc.dram_tensor(name, list(shape), dt, kind="Internal").ap()

    common = {k: ext[k] for k in ("cc2", "cosT", "sinT", "ident", "hidx")}
    y = scr("y_scr", (TT, 1024), F32)
    x1 = scr("x1_scr", (T, 1024), F32)
    xc1 = scr("xc1_scr", (L, 1024), F32)
    xin = [ext["x"], x1]
    xcin = [ext["xc"], xc1]
    for li, lay in enumerate((Lay0, Lay1)):
        sfx = str(li)
        q = scr("q" + sfx, (lay.QROWS, TT), BF16)
        kt = scr("kt" + sfx, (lay.KTROWS, TT), BF16)
        v = scr("v" + sfx, (TT, lay.VCOLS), BF16)
        hk = scr("hk" + sfx, lay.HKSHAPE, BF16)
        sg = scr("sg" + sfx, (TT, 1024), F32)
        mod = scr("mod" + sfx, (2, 3072), F32)
        kt_all = scr("kt_all" + sfx, (NCORES * lay.KTROWS, TT), BF16)
        v_all = scr("v_all" + sfx, (NCORES * TT, lay.VCOLS), BF16)
        hk_all = scr("hk_all" + sfx, (NCORES * lay.HKSHAPE[0], lay.HKSHAPE[1]), BF16)
        ioA = IO(common)
        ioA.update(x=xin[li], xc=xcin[li], q=q, kt=kt, v=v, hk=hk, sg=sg, mod=mod)
        for k in ("norm_g", "b_ada2", "w_ada", "w_in", "w_rope"):
            ioA[k] = ext[k + sfx]
        if li == 1:
            for k in ("kvng_row", "w_qb_all", "qng_col", "wkT"):
                ioA[k] = ext[k]
        phase_A(nc, lay, ioA, f"a{li}_")
        nc.all_engine_barrier()
        if _STOP[0] == "A":
            return
        phase_AG(nc, [(kt, kt_all), (v, v_all), (hk, hk_all)], f"g{li}_")
        nc.all_engine_barrier()
        ioB = IO(common)
        ioB.update(q=q, kt=kt, v=v, sg=sg, mod=mod, y=y, x=xin[li], xc=xcin[li], hk_all=hk_all, v_all=v_all,
                   kt_all=kt_all.rearrange("(r a) c -> r a c", r=NCORES),
                   w_out=ext["w_out" + sfx])
        ioB3 = IO(ioB)
        ioB3["v_all"] = v_all.rearrange("(r a) c -> r a c", r=NCORES)
        if li == 0:
            for k in ("maskA", "a_sink", "b_lambda", "subln_g"):
                ioB[k] = ext[k]; ioB3[k] = ext[k]
            ioB["x1"] = x1; ioB["xc1"] = xc1
            if _STOP[0] == "AG":
                return
            phase_BA0(nc, ioB, "ba_")
            nc.all_engine_barrier()
            if _STOP[0] == "BA":
                return
            phase_BB0(nc, ioB3, "bb_")
            nc.all_engine_barrier()
            if _STOP[0] == "BB":
                return
            phase_BO(nc, ioB, "bo0_", last=False)
            nc.all_engine_barrier()
            if _STOP[0] == "BO":
                return
        else:
            for k in ("wv", "rp_pad", "colmask", "vmD", "antidiag", "final_g"):
                ioB[k] = ext[k]; ioB3[k] = ext[k]
            ioB["out"] = ext["out"]
            phase_BC1(nc, ioB3, "bc_")
            nc.all_engine_barrier()
            phase_BD1(nc, ioB, "bd_")
            nc.all_engine_barrier()
            phase_BO(nc, ioB, "bo1_", last=True)


def kernel(**inputs):
    inp = {k: np.asarray(v) for k, v in inputs.items()}
    cos, sin = _rope_tables()
    x = inp["x"][0]
    ident = np.eye(128, dtype=np.float32)
    cc2 = np.ascontiguousarray(np.stack([_arr_col(inp["c"].reshape(-1)), _arr_col(inp["c_ctx"].reshape(-1))], axis=-1))
    shared = dict(xc=np.ascontiguousarray(inp["ctx"][0]), cc2=cc2, ident=ident)
    shared.update(_layer_inputs(Lay0, "0", inp["ev_norm_g"][0], inp["ev_w_ada"][0], inp["ev_b_ada"][0], inp["ev_w_in"][0]))
    shared.update(_layer_inputs(Lay1, "1", inp["od_norm_g"][0], inp["od_w_ada"][0], inp["od_b_ada"][0], inp["od_w_in"][0]))
    shared.update(a_sink=np.ascontiguousarray(inp["ev_a_sink"][0].reshape(1, 8)),
                  b_lambda=np.ascontiguousarray(inp["ev_b_lambda"][0].reshape(1, 256)),
                  subln_g=np.ascontiguousarray(inp["ev_b_subln_g"][0].reshape(1, 128)),
                  w_out0=np.ascontiguousarray(inp["ev_w_out"][0]), w_out1=np.ascontiguousarray(inp["od_w_out"][0]))
    w_qb = inp["od_c_w_qb"][0]
    pe_cols = np.concatenate([192 * h + 128 + ROPE_PERM for h in range(4)])
    w_kvb = inp["od_c_w_kvb"][0].reshape(128, 4, 256)
    rpb = inp["od_d_rpb"][0]
    rp = np.zeros((8, 15, 128), np.float32)
    rp[:, :, 48:79] = rpb[:, :, ::-1]
    shared.update(kvng_row=np.ascontiguousarray(inp["od_c_kv_norm_g"][0].reshape(1, 128)),
                  w_qb_all=np.ascontiguousarray(np.concatenate([w_qb, w_qb[:, pe_cols]], axis=1)),
                  qng_col=np.ascontiguousarray(inp["od_c_q_norm_g"][0].reshape(2, 128).T),
                  wkT=np.ascontiguousarray(np.transpose(w_kvb[:, :, 0:128], (2, 1, 0))),
                  wv=np.ascontiguousarray(w_kvb[:, :, 128:256]), rp_pad=np.ascontiguousarray(rp.reshape(120, 128)),
                  colmask=_colmask(), antidiag=_antidiag(),
                  final_g=np.ascontiguousarray(inp["final_norm_g"].reshape(1, 1024)))
    maps = []
    for r in range(NCORES):
        cT, sT = _rope_core_tables(r, cos, sin)
        m = dict(shared)
        m.update(x=np.ascontiguousarray(x[r * T:(r + 1) * T]), cosT=cT, sinT=sT, hidx=_hidx(r), maskA=_maskA(r), vmD=_vmD(r))
        maps.append(m)
    res = _launch(build_fused, maps, [("out", (T, 1024), F32)])
    out = np.concatenate([res[r]["out"] for r in range(NCORES)], axis=0)
    return out.reshape(1, S, D).astype(np.float32)
```

```python
import math
import numpy as np
from contextlib import ExitStack
import ml_dtypes
import concourse.bass as bass
import concourse.mybir as mybir
from concourse.bass_utils import run_bass_kernel_spmd

F32 = mybir.dt.float32
BF16 = mybir.dt.bfloat16
I32 = mybir.dt.int32
AF = mybir.ActivationFunctionType
ALU = mybir.AluOpType
AX = mybir.AxisListType
NPBF = ml_dtypes.bfloat16

NCORES = 8
S = 16384
T = 2048
NTB = 16
L = 256
TT = T + L
D = 1024
EPS = 1e-6
GRID_W = 64
NIDX = 26


class Buf:
    __slots__ = ("name", "writers", "readers", "dma_sem", "dma_cnt", "excl")

    def __init__(self, name, excl=False):
        self.name = name
        self.excl = excl
        self.writers = []
        self.readers = []
        self.dma_sem = None
        self.dma_cnt = 0


class Op:
    __slots__ = ("eng", "emit", "deps", "signal", "sigval", "is_dma", "dsem", "dval", "cc_inc")

    def __init__(self, eng, emit, is_dma=False):
        self.eng = eng
        self.emit = emit
        self.deps = []
        self.signal = False
        self.sigval = 0
        self.is_dma = is_dma
        self.dsem = None
        self.dval = 0
        self.cc_inc = 16


class Prog:
    ENGS = ("pe", "act", "dve", "pool", "sp")

    def __init__(self, nc, tag=""):
        self.nc = nc
        self.tag = tag
        self.ops = {e: [] for e in self.ENGS}
        self.stack = ExitStack()
        self.esem = {}
        self.bufs = []
        self.dma_bufs = []
        self.sems = []

    def sem(self, name):
        h = self.nc.alloc_semaphore(name=self.tag + name)
        self.sems.append(h)
        return h

    def sbuf(self, name, shape, dt):
        return self.stack.enter_context(self.nc.sbuf_tensor(self.tag + name, shape, dt))

    def psum(self, name, shape, dt):
        return self.stack.enter_context(self.nc.psum_tensor(self.tag + name, shape, dt))

    def buf(self, name, excl=False):
        b = Buf(name, excl)
        self.bufs.append(b)
        return b

    def bufs_n(self, name, n):
        return [self.buf(f"{name}{i}") for i in range(n)]

    def _add(self, op, reads, writes, join=False):
        xr = [b for b in reads if b.excl]
        reads = [b for b in reads if not b.excl]
        xw = [b for b in writes if b.excl]
        writes = [b for b in writes if not b.excl]
        deps = []
        for b in xr + xw:
            deps.extend(b.readers)
            deps.extend(b.writers)
        for b in reads:
            deps.extend(b.writers)
        for b in writes:
            deps.extend(b.readers)
            if not join:
                deps.extend(b.writers)
        op.deps = [d for d in deps if not (d.eng == "pe" and op.eng == "pe" and not d.is_dma and not op.is_dma)]
        for b in xr + xw:
            b.writers = [op]
            b.readers = []
        for b in reads:
            b.readers.append(op)
        for b in writes:
            if join:
                b.writers.append(op)
            else:
                b.writers = [op]
            b.readers = []
        self.ops[op.eng].append(op)
        return op

    def op(self, eng, emit, reads=(), writes=(), join=False):
        return self._add(Op(eng, emit), list(reads), list(writes), join)

    def dma(self, eng, out, in_, reads, writes, join=False, emit=None, sem_buf=None, **kw):
        assert len(writes) == 1
        b = sem_buf if sem_buf is not None else (reads[0] if (len(reads) == 1 and self.outbound(out)) else writes[0])
        if b.dma_sem is None:
            b.dma_sem = self.sem("d_" + b.name)
            self.dma_bufs.append(b)
        b.dma_cnt += 16
        if emit is None:
            emit = lambda e, out=out, in_=in_, kw=kw: e.dma_start(out=out, in_=in_, **kw)
        o = Op(eng, emit, is_dma=True)
        o.dsem = b.dma_sem
        o.dval = b.dma_cnt
        return self._add(o, list(reads), list(writes), join)

    @staticmethod
    def outbound(out_ap):
        try:
            return "DRam" in type(out_ap.tensor).__name__ or "Dram" in type(out_ap.tensor).__name__ or "DRAM" in type(out_ap.tensor).__name__
        except Exception:
            return False

    def gather(self, out, src2d, idx, reads, writes, join=False):
        def emit(e, out=out, src2d=src2d, idx=idx):
            return e.indirect_dma_start(out=out, out_offset=None, in_=src2d,
                                        in_offset=bass.IndirectOffsetOnAxis(ap=idx, axis=0))
        return self.dma("pool", None, None, reads, writes, join=join, emit=emit, sem_buf=writes[0])

    def collective(self, kind, in_ap, out_ap, reads, writes):
        def emit(e):
            return e.collective_compute(kind, ALU.bypass, replica_groups=[list(range(NCORES))], ins=[in_ap], outs=[out_ap])
        o = self.dma("pool", None, None, reads, writes, emit=emit, sem_buf=writes[0])
        b = writes[0]
        b.dma_cnt += 1 - 16
        o.dval = b.dma_cnt
        o.cc_inc = 1
        return o

    def wait_all(self, eng, bufs):
        return self._add(Op(eng, None), list(bufs), [])

    def build(self):
        nc = self.nc
        fin = Op("sp", None)
        fin.deps = []
        for b in self.dma_bufs:
            d = Op("sp", None, is_dma=True)
            d.dsem = b.dma_sem
            d.dval = b.dma_cnt
            fin.deps.append(d)
        self.ops["sp"].append(fin)
        for e in self.ENGS:
            self.esem[e] = self.sem("e_" + e)
        for e in self.ENGS:
            for o in self.ops[e]:
                for d in o.deps:
                    if not d.is_dma:
                        d.signal = True
        for e in self.ENGS:
            c = 0
            for o in self.ops[e]:
                if o.signal and not o.is_dma:
                    c += 1
                    o.sigval = c
        engobj = {"pe": "tensor", "act": "scalar", "dve": "vector", "pool": "gpsimd", "sp": "sync"}

        def make(e):
            def fn(eng):
                waited = {}
                for o in self.ops[e]:
                    need = {}
                    for d in o.deps:
                        if d.is_dma:
                            s, v = d.dsem, d.dval
                        else:
                            s, v = self.esem[d.eng], d.sigval
                        k = id(s)
                        if k not in need or need[k][1] < v:
                            need[k] = (s, v)
                    for k, (s, v) in need.items():
                        if waited.get(k, 0) >= v:
                            continue
                        waited[k] = v
                        eng.wait_ge(s, v)
                    if o.emit is None:
                        continue
                    inst = o.emit(eng)
                    if o.is_dma:
                        inst.then_inc(o.dsem, o.cc_inc)
                    elif o.signal:
                        inst.then_inc(self.esem[e], 1)
                last = max([o.sigval for o in self.ops[e]] + [0])
                if last > 0:
                    eng.wait_ge(self.esem[e], last)
            return fn

        with nc.Block() as block:
            for e in self.ENGS:
                if self.ops[e]:
                    getattr(block, engobj[e])(make(e))
        self.stack.close()
        nc.all_engine_barrier()
        nc.clear_and_free_semaphores(self.sems)
        nc.all_engine_barrier()


def mm(P, out, lhsT, rhs, start, stop, reads, writes):
    return P.op("pe", lambda e: e.matmul(out, lhsT, rhs, start=start, stop=stop, skip_group_check=True), reads, writes)


def tr(P, out, in_, ident, reads, writes):
    return P.op("pe", lambda e: e.transpose(out, in_, ident), reads, writes)


def act(P, out, in_, func, reads, writes, **kw):
    return P.op("act", lambda e: e.activation(out, in_, func, **kw), reads, writes)


def tsc(P, eng, out, in0, s1, s2, op0, op1, reads, writes):
    if s2 is None:
        return P.op(eng, lambda e: e.tensor_scalar(out, in0, s1, None, op0), reads, writes)
    return P.op(eng, lambda e: e.tensor_scalar(out, in0, s1, s2, op0, op1), reads, writes)


def tt(P, eng, out, in0, in1, op, reads, writes):
    return P.op(eng, lambda e: e.tensor_tensor(out, in0, in1, op), reads, writes)


def cp(P, eng, out, in_, reads, writes):
    if eng == "act":
        return P.op(eng, lambda e: e.copy(out, in_), reads, writes)
    return P.op(eng, lambda e: e.tensor_copy(out, in_), reads, writes)


def mset(P, eng, ap, val, writes):
    return P.op(eng, lambda e: e.memset(ap, val), [], writes)


def recip(P, out, in_, reads, writes):
    return P.op("dve", lambda e: e.reciprocal(out, in_), reads, writes)


def bcast_rows(ap_row, nparts):
    a = [list(x) for x in ap_row.ap]
    a[0] = [0, nparts]
    return bass.AP(ap_row.tensor, ap_row.offset, a)


class Rot:
    def __init__(self, P, kind, name, shape, dt, n):
        alloc = P.sbuf if kind == "sb" else P.psum
        self.t = [alloc(f"{name}{i}", shape, dt) for i in range(n)]
        self.b = [P.buf(f"{name}{i}", excl=(kind == "ps")) for i in range(n)]
        self.i = 0
        self.n = n

    def next(self):
        r = (self.t[self.i], self.b[self.i])
        self.i = (self.i + 1) % self.n
        return r


ROPE_PERM = np.concatenate([np.arange(16, 32), np.arange(0, 16), np.arange(48, 64), np.arange(32, 48)])
ROPE_SIGN = np.concatenate([-np.ones(16), np.ones(16), -np.ones(16), np.ones(16)]).astype(np.float32)


class Lay0:
    idx = 0
    C = 3328
    fm = ([(128 * i, 128, i, "q", 128 * i) for i in range(4)]
          + [(512, 128, 4, "kt", 0)]
          + [(768 + 128 * i, 128, 5 + i, "q", 512 + 128 * i) for i in range(4)]
          + [(1280 + 128 * i, 128, 9 + i, "kt", 128 + 128 * i) for i in range(4)])
    rope_cols = ([128 * i for i in range(4)] + [512] + [768 + 128 * i for i in range(4)]
                 + [1280 + 128 * i for i in range(4)])
    NR = 13
    QROWS = 1024
    KTROWS = 640
    VCOLS = 646
    tmv = [(640, 128, 2, 64, 0), (1792, 512, 4, 128, 130)]
    gcol = 2304
    halo = {("kt", 0): (0, 128, 128)}
    HKSHAPE = (256, 128)


class Lay1:
    idx = 1
    C = 3008
    fm = ([(448 + 128 * i, 128, None, "q", 768 + 128 * i) for i in range(4)]
          + [(960 + 128 * i, 128, None, "kt", 192 + 128 * i) for i in range(4)]
          + [(384, 64, 0, "kt", 128)])
    rope_cols = [384]
    NR = 1
    QROWS = 1280
    KTROWS = 704
    VCOLS = 649
    tmv = [(1472, 512, 8, 64, 129)]
    gcol = 1984
    halo = {("kt", 192 + 128 * i): (128 * i, 256, 512) for i in range(4)}
    HKSHAPE = (1024, 256)


def phase_A(nc, lay, io, tag):
    P = Prog(nc, tag)
    C = lay.C
    NRC = lay.NR * 128
    identb = P.sbuf("identb", [128, 128], BF16); Bidb = P.buf("identb")
    P.dma("pool", identb[:], io["ident"], [], [Bidb])
    identf = P.sbuf("identf", [128, 128], F32); Bidf = P.buf("identf")
    P.dma("sp", identf[:], io["ident"], [], [Bidf])
    wbf = P.sbuf("wbf", [128, 8, C], BF16); Bw = P.bufs_n("w", 8)
    wrp = P.sbuf("wrp", [128, 8, NRC], BF16); Bwr = P.bufs_n("wr", 8)
    cosT = P.sbuf("cosT", [128, TT], F32); Bcos = P.buf("cos")
    sinT = P.sbuf("sinT", [128, TT], F32); Bsin = P.buf("sin")
    P.dma("sp", cosT[:], io["cosT"], [], [Bcos])
    P.dma("sp", sinT[:], io["sinT"], [], [Bsin])
    cc = P.sbuf("cc", [128, 8, 2], F32); Bcc = P.buf("cc")
    P.dma("sp", cc[:], io["cc2"], [], [Bcc])
    ng = P.sbuf("ng", [128, 8], F32); Bng = P.buf("ng")
    P.dma("sp", ng[:], io["norm_g"], [], [Bng])
    sc = P.sbuf("sc", [128, 8, 2], F32); Bsc = P.buf("sc")
    act(P, sc[:], cc[:], AF.Silu, [Bcc], [Bsc])
    warot = Rot(P, "sb", "wada", [128, 8, 256], F32, 2)
    barot = Rot(P, "sb", "bada", [2, 256], F32, 2)
    mrrot = Rot(P, "sb", "mr", [2, 256], F32, 2)
    psA = Rot(P, "ps", "psA", [128, 512], F32, 2)
    psB = Rot(P, "ps", "psB", [128, 512], F32, 2)
    psm = psA
    pscol = P.psum("pscol", [128, 512], F32); Bpscol = P.buf("pscol", excl=True)
    w_ada_v = io["w_ada"].rearrange("(k p) c -> p k c", p=128)
    Bmodd = P.buf("mod_dram")
    for ct in range(12):
        wa, Bwa = warot.next()
        P.dma("sp", wa[:], w_ada_v[:, :, ct * 256:(ct + 1) * 256], [], [Bwa])
        ba, Bba = barot.next()
        P.dma("sp", ba[:], io["b_ada2"][:, ct * 256:(ct + 1) * 256], [], [Bba])
        ps, Bps = psm.next()
        for k in range(8):
            mm(P, ps[0:2, 0:256], sc[:, k, :], wa[:, k, :], k == 0, k == 7, [Bsc, Bwa], [Bps])
        mr, Bmr = mrrot.next()
        tt(P, "dve", mr[:], ps[0:2, 0:256], ba[:], ALU.add, [Bps, Bba], [Bmr])
        P.dma("sp", io["mod"][:, ct * 256:(ct + 1) * 256], mr[:], [Bmr], [Bmodd], join=True)
        if ct < 8:
            for cc_ in range(2):
                ch = 2 * ct + cc_
                mm(P, pscol[:, 2 * ch:2 * ch + 2], mr[0:2, cc_ * 128:(cc_ + 1) * 128], identf[0:2, 0:2], True, True,
                   [Bmr, Bidf], [Bpscol])
    for k in range(8):
        P.dma("pool", wbf[:, k, :], io["w_in"][k * 128:(k + 1) * 128, :], [], [Bw[k]])
    for k in range(8):
        P.dma("pool", wrp[:, k, :], io["w_rope"][k * 128:(k + 1) * 128, :], [], [Bwr[k]])
    ps, Bps = pscol, Bpscol
    modT = P.sbuf("modT", [128, 16, 2], F32); BmodT = P.buf("modT")
    cp(P, "dve", modT[:].rearrange("p a b -> p (a b)"), ps[:, 0:32], [Bps], [BmodT])
    Acol = P.sbuf("Acol", [128, 8, 2], F32); BA = P.buf("Acol")
    tsc(P, "dve", Acol[:].rearrange("p a b -> p (a b)"), modT[:, 8:16, :].rearrange("p a b -> p (a b)"), 1.0, None,
        ALU.add, None, [BmodT], [BA])
    for j in range(2):
        P.op("dve", lambda e, j=j: e.tensor_tensor(Acol[:, :, j], Acol[:, :, j], ng[:, :], ALU.mult), [BA, Bng], [BA])

    if io.get("_stop") == 1:
        P.build(); return
    hT = P.sbuf("hT", [128, 8, TT], BF16); BhT = P.bufs_n("hT", TT // 128)
    xrot = Rot(P, "sb", "xt", [128, 1024], F32, 2)
    junk = Rot(P, "sb", "junk", [128, 1024], BF16, 1)
    xnrot = Rot(P, "sb", "xn", [128, 1024], BF16, 2)
    strot = Rot(P, "sb", "st", [128, 4], F32, 3)
    ptr = Rot(P, "ps", "ptr", [128, 8, 128], BF16, 2)
    for i in range(io.get("_ntiles", TT // 128)):
        j = 0 if i < NTB else 1
        src = io["x"][i * 128:(i + 1) * 128, :] if i < NTB else io["xc"][(i - NTB) * 128:(i - NTB + 1) * 128, :]
        xt, Bxt = xrot.next()
        P.dma("sp", xt[:], src, [], [Bxt])
        jk, Bjk = junk.next()
        st, Bst = strot.next()
        act(P, jk[:], xt[:], AF.Square, [Bxt], [Bjk, Bst], accum_out=st[:, 0:1])
        act(P, st[:, 1:2], st[:, 0:1], AF.Sqrt, [Bst], [Bst], scale=1.0 / D, bias=EPS)
        recip(P, st[:, 2:3], st[:, 1:2], [Bst], [Bst])
        xn, Bxn = xnrot.next()
        tsc(P, "dve", xn[:], xt[:], st[:, 2:3], None, ALU.mult, None, [Bxt, Bst], [Bxn])
        pt, Bpt = ptr.next()
        for k in range(8):
            tr(P, pt[:, k, :], xn[:, k * 128:(k + 1) * 128], identb[:], [Bxn, Bidb], [Bpt])
        for k in range(8):
            dst = hT[:, k, i * 128:(i + 1) * 128]
            if i % 2 == 0:
                P.op("dve", lambda e, dst=dst, pt=pt, k=k, j=j: e.tensor_scalar(
                    dst, pt[:, k, :], Acol[:, k, j:j + 1], modT[:, k, j:j + 1], ALU.mult, ALU.add),
                    [Bpt, BA, BmodT], [BhT[i]], join=(k > 0))
            else:
                P.op("act", lambda e, dst=dst, pt=pt, k=k, j=j: e.activation(
                    dst, pt[:, k, :], AF.Identity, bias=modT[:, k, j:j + 1], scale=Acol[:, k, j:j + 1]),
                    [Bpt, BA, BmodT], [BhT[i]], join=(k > 0))

    if io.get("_stop") == 2:
        P.build(); return
    ttiles = [(0, 512), (512, 512), (1024, 512), (1536, 512), (2048, 256)]

    def hbufs(t0, n):
        return BhT[t0 // 128:(t0 + n) // 128]

    ostage = Rot(P, "sb", "ost", [128, TT], BF16, 2)
    t1rot = Rot(P, "sb", "t1", [128, 512], F32, 1)
    t2rot = Rot(P, "sb", "t2", [128, 512], F32, 1)
    dcount = [0]

    def dq():
        dcount[0] += 1
        return "sp" if dcount[0] % 2 else "pool"


    def fm_job(c0, M, dst_ap, w_t, Bw_l, rhs_fn, nk, rope=None, scale_bc=None, post=None):
        og, Bog = ostage.next()
        first = True
        for (t0, n) in ttiles:
            ps, Bps = psA.next()
            for k in range(nk):
                rhs, rb = rhs_fn(k, t0, n)
                mm(P, ps[0:M, 0:n], w_t[:, k, c0:c0 + M], rhs, k == 0, k == nk - 1, Bw_l + rb, [Bps])
            if rope is not None:
                wr_t, Bwr_l, rc0 = rope
                ps2, Bps2 = psB.next()
                for k in range(nk):
                    rhs, rb = rhs_fn(k, t0, n)
                    mm(P, ps2[0:M, 0:n], wr_t[:, k, rc0:rc0 + M], rhs, k == 0, k == nk - 1, Bwr_l + rb, [Bps2])
                t1, Bt1 = t1rot.next()
                t2, Bt2 = t2rot.next()
                tt(P, "dve", t1[0:M, 0:n], ps[0:M, 0:n], cosT[0:M, t0:t0 + n], ALU.mult, [Bps, Bcos], [Bt1])
                tt(P, "dve", t2[0:M, 0:n], ps2[0:M, 0:n], sinT[0:M, t0:t0 + n], ALU.mult, [Bps2, Bsin], [Bt2])
                if scale_bc is None:
                    P.op("pool", lambda e, og=og, t1=t1, t2=t2, t0=t0, n=n: e.tensor_tensor(
                        og[0:M, t0:t0 + n], t1[0:M, 0:n], t2[0:M, 0:n], ALU.add), [Bt1, Bt2], [Bog], join=not first)
                else:
                    sbt, Bsb = scale_bc
                    tt(P, "pool", t1[0:M, 0:n], t1[0:M, 0:n], t2[0:M, 0:n], ALU.add, [Bt1, Bt2], [Bt1])
                    P.op("pool", lambda e, og=og, t1=t1, t0=t0, n=n, sbt=sbt: e.tensor_tensor(
                        og[0:M, t0:t0 + n], t1[0:M, 0:n], sbt[0:M, t0:t0 + n], ALU.mult), [Bt1, Bsb], [Bog],
                        join=not first)
            elif post is not None:
                post(ps, Bps, og, Bog, t0, n, first)
            elif scale_bc is not None:
                sbt, Bsb = scale_bc
                P.op("dve", lambda e, og=og, ps=ps, t0=t0, n=n, sbt=sbt: e.tensor_tensor(
                    og[0:M, t0:t0 + n], ps[0:M, 0:n], sbt[0:M, t0:t0 + n], ALU.mult), [Bps, Bsb], [Bog],
                    join=not first)
            else:
                P.op("act", lambda e, og=og, ps=ps, t0=t0, n=n: e.copy(og[0:M, t0:t0 + n], ps[0:M, 0:n]),
                     [Bps], [Bog], join=not first)
            first = False
        if dst_ap is not None:
            P.dma(dq(), dst_ap, og[0:M, :], [Bog], [Bfmout], join=True)
        return og, Bog

    Bfmout = P.buf("fmout"); Bvout = P.buf("vout"); Bgout = P.buf("gout")

    def h_rhs(k, t0, n):
        return hT[:, k, t0:t0 + n], hbufs(t0, n)

    for (c0, M, ridx, dname, drow) in lay.fm:
        og, Bog = fm_job(c0, M, io[dname][drow:drow + M, :], wbf, Bw, h_rhs, 8,
                         rope=None if ridx is None else (wrp, Bwr, ridx * 128))
        hp = lay.halo.get((dname, drow))
        if hp is not None:
            hrow, hw, hrows = hp
            P.dma(dq(), io["hk"][hrow:hrow + M, :], og[0:M, 0:hw], [Bog], [Bfmout], join=True)
            P.dma(dq(), io["hk"][hrows + hrow:hrows + hrow + M, :], og[0:M, T - hw:T], [Bog], [Bfmout], join=True)

    if io.get("_stop") == 3:
        P.build(); return
    vst = Rot(P, "sb", "vst", [128, lay.VCOLS], BF16, 2)
    gst = Rot(P, "sb", "gst", [128, 1024], F32, 2)
    for i in range(2):
        mset(P, "pool", vst.t[i][:], 1.0, [vst.b[i]])

    def tm_mm(i, c0, ncols):
        ps, Bps = psA.next()
        for k in range(8):
            mm(P, ps[:, 0:ncols], hT[:, k, i * 128:(i + 1) * 128], wbf[:, k, c0:c0 + ncols], k == 0, k == 7,
               [BhT[i]] + Bw, [Bps])
        return ps, Bps

    if lay.idx == 1:
        gkvb = P.sbuf("gkvb", [128, 128], F32); Bgkvb = P.buf("gkvb")
        P.dma("sp", gkvb[:], bcast_rows(io["kvng_row"], 128), [], [Bgkvb])
        ckT = P.sbuf("ckT", [128, TT], BF16); BckT = P.buf("ckT")
        st2 = Rot(P, "sb", "st2", [128, 4], F32, 3)
        jk2 = Rot(P, "sb", "jk2", [128, 128], F32, 2)
        ptc = ptr

    for i in range(TT // 128):
        vt, Bvt = vst.next()
        wfirst = True
        for (c0, ncols, nh, e, dcol) in lay.tmv:
            ps, Bps = tm_mm(i, c0, ncols)
            dstv = vt[:, dcol:dcol + nh * (e + 1)].rearrange("p (h e) -> p h e", e=e + 1)[:, :, 0:e]
            srcv = ps[:, 0:ncols].rearrange("p (h e) -> p h e", e=e)
            P.op("act", lambda en, dstv=dstv, srcv=srcv: en.copy(dstv, srcv), [Bps], [Bvt], join=not wfirst)
            wfirst = False
        if lay.idx == 1:
            ps, Bps = tm_mm(i, 256, 128)
            s2, Bs2 = st2.next()
            j2, Bj2 = jk2.next()
            act(P, j2[:], ps[:, 0:128], AF.Square, [Bps], [Bj2, Bs2], accum_out=s2[:, 0:1])
            act(P, s2[:, 1:2], s2[:, 0:1], AF.Sqrt, [Bs2], [Bs2], scale=1.0 / 128, bias=EPS)
            recip(P, s2[:, 2:3], s2[:, 1:2], [Bs2], [Bs2])
            P.op("dve", lambda en, vt=vt, ps=ps, s2=s2: en.scalar_tensor_tensor(
                vt[:, 0:128], ps[:, 0:128], s2[:, 2:3], gkvb[:], ALU.mult, ALU.mult), [Bps, Bs2, Bgkvb], [Bvt], join=True)
            pc, Bpc = ptc.next()
            tr(P, pc[:, 0, :], vt[:, 0:128], identb[:], [Bvt, Bidb], [Bpc])
            P.op("act", lambda en, pc=pc, i=i: en.copy(ckT[:, i * 128:(i + 1) * 128], pc[:, 0, :]), [Bpc], [BckT], join=True)
        P.dma(dq(), io["v"][i * 128:(i + 1) * 128, :], vt[:], [Bvt], [Bvout], join=True)
        gt, Bgt = gst.next()
        for hh in range(2):
            ps, Bps = tm_mm(i, lay.gcol + 512 * hh, 512)
            P.op("act", lambda en, gt=gt, ps=ps, hh=hh: en.activation(gt[:, hh * 512:(hh + 1) * 512], ps[:], AF.Silu),
                 [Bps], [Bgt], join=(hh == 1))
        P.dma(dq(), io["sg"][i * 128:(i + 1) * 128, :], gt[:], [Bgt], [Bgout], join=True)

    if lay.idx == 1:
        P.dma(dq(), io["kt"][0:128, :], ckT[:], [BckT], [Bfmout], join=True)
        cqT = P.sbuf("cqT", [128, 2, TT], BF16); BcqT = P.buf("cqT")
        sqr = Rot(P, "sb", "sqr", [128, 2, 512], F32, 1)
        rsq = P.sbuf("rsq", [128, TT], F32); Brsq = P.buf("rsq")
        onesf = P.sbuf("onesf", [128, 128], F32); Bones = P.buf("onesf")
        mset(P, "pool", onesf[:], 1.0, [Bones])
        first = True
        for (t0, n) in ttiles:
            sq, Bsq = sqr.next()
            for kk in range(2):
                ps, Bps = psA.next()
                for k in range(8):
                    mm(P, ps[:, 0:n], wbf[:, k, kk * 128:(kk + 1) * 128], hT[:, k, t0:t0 + n], k == 0, k == 7,
                       Bw + hbufs(t0, n), [Bps])
                P.op("act", lambda e, ps=ps, kk=kk, t0=t0, n=n: e.copy(cqT[:, kk, t0:t0 + n], ps[:, 0:n]),
                     [Bps], [BcqT], join=not (first and kk == 0))
                P.op("dve", lambda e, ps=ps, sq=sq, kk=kk, n=n: e.tensor_copy(sq[:, kk, 0:n], ps[:, 0:n]),
                     [Bps], [Bsq], join=(kk == 1))
            P.op("pool", lambda e, sq=sq, n=n: e.tensor_tensor(sq[:, :, 0:n], sq[:, :, 0:n], sq[:, :, 0:n], ALU.mult),
                 [Bsq], [Bsq])
            ps, Bps = psB.next()
            for kk in range(2):
                mm(P, ps[:, 0:n], onesf[:], sq[:, kk, 0:n], kk == 0, kk == 1, [Bones, Bsq], [Bps])
            P.op("act", lambda e, ps=ps, t0=t0, n=n: e.activation(rsq[:, t0:t0 + n], ps[:, 0:n], AF.Sqrt,
                                                                scale=1.0 / 256, bias=EPS), [Bps], [Brsq], join=not first)
            first = False
        recip(P, rsq[:], rsq[:], [Brsq], [Brsq])
        wqf = P.sbuf("wqf", [128, 1024], F32); Bwqf = P.buf("wqf")
        qng = P.sbuf("qng", [128, 2], F32); Bqng = P.buf("qng")
        P.dma("sp", qng[:], io["qng_col"], [], [Bqng])
        wqb = P.sbuf("wqb", [128, 2, 1024], BF16); Bwqb = P.buf("wqb")
        for kk in range(2):
            P.dma("sp", wqf[:], io["w_qb_all"][kk * 128:(kk + 1) * 128, :], [], [Bwqf])
            P.op("dve", lambda e, kk=kk: e.tensor_scalar(wqb[:, kk, :], wqf[:], qng[:, kk:kk + 1], None, ALU.mult),
                 [Bwqf, Bqng], [Bwqb], join=(kk == 1))
        wkT = P.sbuf("wkT", [128, 4, 128], BF16); BwkT = P.buf("wkT")
        P.dma("pool", wkT[:], io["wkT"], [], [BwkT])

        def cq_rhs(k, t0, n):
            return cqT[:, k, t0:t0 + n], [BcqT]

        qn_rot = Rot(P, "sb", "qn", [128, 512], BF16, 2)
        for h in range(4):
            def post(ps, Bps, og, Bog, t0, n, first, h=h):
                qn, Bqn = qn_rot.next()
                tt(P, "dve", qn[:, 0:n], ps[:, 0:n], rsq[:, t0:t0 + n], ALU.mult, [Bps, Brsq], [Bqn])
                ps3, Bps3 = psB.next()
                mm(P, ps3[:, 0:n], wkT[:, h, :], qn[:, 0:n], True, True, [BwkT, Bqn], [Bps3])
                P.op("act", lambda e, og=og, ps3=ps3, t0=t0, n=n: e.copy(og[:, t0:t0 + n], ps3[:, 0:n]),
                     [Bps3], [Bog], join=not first)
            fm_job(192 * h, 128, io["q"][128 * h:128 * h + 128, :], wqb, [Bwqb], cq_rhs, 2, post=post)
            fm_job(192 * h + 128, 64, io["q"][512 + 64 * h:512 + 64 * h + 64, :], wqb, [Bwqb], cq_rhs, 2,
                   rope=(wqb, [Bwqb], 768 + 64 * h), scale_bc=(rsq, Brsq))
    P.build()


def load_common_B(P, io):
    identb = P.sbuf("identb", [128, 128], BF16); Bidb = P.buf("identb")
    P.dma("pool", identb[:], io["ident"], [], [Bidb])
    return identb, Bidb


def phase_BA0(nc, io, tag):
    P = Prog(nc, tag)
    scale = 64 ** -0.5
    qaT = P.sbuf("qaT", [64, 8, TT], BF16); Bqa = P.buf("qaT")
    P.dma("sp", qaT[:], io["q"][0:512, :].rearrange("(h d) t -> d h t", d=64), [], [Bqa])
    kaT = P.sbuf("kaT", [64, 2, TT], BF16); Bka = P.buf("kaT")
    P.dma("pool", kaT[:], io["kt"][0:128, :].rearrange("(j d) t -> d j t", d=64), [], [Bka])
    vaX = P.sbuf("vaX", [128, 18, 130], BF16); Bva = P.buf("vaX")
    P.dma("sp", vaX[:], io["v"][:, 0:130].rearrange("(c p) e -> p c e", p=128), [], [Bva])
    hidx = P.sbuf("hidx", [128, NIDX], I32); Bhidx = P.buf("hidx")
    P.dma("sp", hidx[:], io["hidx"], [], [Bhidx])
    kaH = P.sbuf("kaH", [64, 2, 2, 128], BF16); BkaH = P.buf("kaH")
    vaH = P.sbuf("vaH", [128, 2, 646], BF16); BvaH = P.buf("vaH")
    for side in range(2):
        for j in range(2):
            P.gather(kaH[:, j, side, :], io["hk_all"], hidx[0:64, 2 * side + j:2 * side + j + 1], [Bhidx], [BkaH],
                     join=not (side == 0 and j == 0))
        P.gather(vaH[:, side, :], io["v_all"], hidx[:, 4 + side:5 + side], [Bhidx], [BvaH], join=(side > 0))
    mk = P.sbuf("mk", [128, 4, 512], BF16); Bmk = P.buf("mk")
    P.dma("pool", mk[:], io["maskA"].rearrange("m p f -> p m f"), [], [Bmk])
    sk = P.sbuf("sk", [128, 8], F32); Bsk = P.buf("sk")
    P.dma("sp", sk[:], bcast_rows(io["a_sink"], 128), [], [Bsk])
    esk = P.sbuf("esk", [128, 8], F32); Besk = P.buf("esk")
    act(P, esk[:], sk[:], AF.Exp, [Bsk], [Besk])
    psS = Rot(P, "ps", "psS", [128, 512], F32, 2)
    accr = Rot(P, "ps", "acc", [128, 512], F32, 2)
    ptr_ = Rot(P, "sb", "pT", [128, 512], BF16, 3)
    ytr = Rot(P, "sb", "yt", [128, 512], F32, 2)
    ztr = Rot(P, "sb", "zt", [128, 8], F32, 2)
    By = P.bufs_n("y", 5)
    for n in range(TT // 128):
        own = n < NTB
        q0 = n * 128
        yt, Byt = ytr.next()
        for j in range(2):
            def kown(c, j=j):
                return (kaT[:, j, c * 128:(c + 1) * 128], vaX[:, c, 65 * j:65 * j + 65], [Bka, Bva])

            def khalo(side, j=j):
                return (kaH[:, j, side, :], vaH[:, side, 65 * j:65 * j + 65], [BkaH, BvaH])
            if own:
                chunks = [(khalo(0) if n == 0 else kown(n - 1), 0 if n == 0 else 1), (kown(n), None),
                          (khalo(1) if n == NTB - 1 else kown(n + 1), 3 if n == NTB - 1 else 2),
                          (kown(16), None), (kown(17), None)]
            else:
                chunks = [(kown(16), None), (kown(17), None)]
            acc_, Bacc = accr.next()
            acc = acc_[:, 0:260].rearrange("p (g e) -> p g e", g=4)
            for ci, ((kap, vap, kvb), m) in enumerate(chunks):
                ps, Bps = psS.next()
                mm(P, ps[:, :].rearrange("p (g q) -> p g q", g=4), kap,
                   qaT[:, 4 * j:4 * j + 4, q0:q0 + 128], True, True, kvb + [Bqa], [Bps])
                pT, BpT = ptr_.next()
                act(P, pT[:], ps[:], AF.Exp, [Bps], [BpT], scale=scale)
                if m is not None:
                    tt(P, "pool", pT[:], pT[:], mk[:, m, :], ALU.mult, [BpT, Bmk], [BpT])
                for g in range(4):
                    mm(P, acc[:, g, :], pT[:, g * 128:(g + 1) * 128], vap,
                       ci == 0 and g == 0, ci == len(chunks) - 1, [BpT] + kvb, [Bacc])
            zt, Bzt = ztr.next()
            tt(P, "dve", zt[:, 0:4], acc[:, :, 64], esk[:, 4 * j:4 * j + 4], ALU.add, [Bacc, Besk], [Bzt])
            recip(P, zt[:, 4:8], zt[:, 0:4], [Bzt], [Bzt])
            for g in range(4):
                hd = 4 * j + g
                P.op("dve", lambda e, yt=yt, acc=acc, zt=zt, g=g, hd=hd: e.tensor_scalar(
                    yt[:, hd * 64:(hd + 1) * 64], acc[:, g, 0:64], zt[:, 4 + g:5 + g], None, ALU.mult),
                    [Bacc, Bzt], [Byt], join=not (j == 0 and g == 0))
        P.dma("sp", io["y"][q0:q0 + 128, 0:512], yt[:], [Byt], [By[n // 4]], join=True)
    P.build()


def full_attn_pass(P, qk_list, nq, chunks, vaug_fn, vbufs, scale, psS, accs, ptr_, E1):
    nb = nq // 128
    a0, Ba0, a1, Ba1 = accs
    nchk = len(chunks)

    def pv(ci, ch, pT, BpT):
        for b in range(nb):
            at, Bat = (a0, Ba0) if b < 2 else (a1, Ba1)
            mm(P, at[:, b % 2, :], pT[:, b * 128:(b + 1) * 128], vaug_fn(ch), ci == 0 and b % 2 == 0, ci == nchk - 1,
               [BpT] + vbufs, [Bat])

    pend = None
    for ci, ch in enumerate(chunks):
        ps, Bps = psS.next()
        for qi, (kT_fn, qT, bl) in enumerate(qk_list):
            mm(P, ps[:, 0:nq], kT_fn(ch), qT, qi == 0, qi == len(qk_list) - 1, bl, [Bps])
        pT, BpT = ptr_.next()
        act(P, pT[:, 0:nq], ps[:, 0:nq], AF.Exp, [Bps], [BpT], scale=scale)
        if pend is not None:
            pv(*pend)
        pend = (ci, ch, pT, BpT)
    pv(*pend)


def phase_BB0(nc, io, tag):
    P = Prog(nc, tag)
    scale = 64 ** -0.5
    lam_init = 0.8 - 0.6 * math.exp(-0.3 * 0)
    NCH = S // 128 + 2
    qbT = P.sbuf("qbT", [128, 4, TT], BF16); Bqb = P.buf("qbT")
    P.dma("sp", qbT[:], io["q"][512:1024, :].rearrange("(h r) t -> r h t", r=128), [], [Bqb])
    lb = P.sbuf("lb", [128, 256], F32); Blb = P.buf("lb")
    P.dma("sp", lb[:], bcast_rows(io["b_lambda"], 128), [], [Blb])
    lt = P.sbuf("lt", [128, 128], F32); Blt = P.buf("lt")
    ls = P.sbuf("ls", [128, 8], F32); Bls = P.buf("ls")
    tt(P, "dve", lt[:].rearrange("p (a b) -> p a b", a=2), lb[:].rearrange("p (a b c) -> p a b c", a=2, b=2)[:, :, 0, :],
       lb[:].rearrange("p (a b c) -> p a b c", a=2, b=2)[:, :, 1, :], ALU.mult, [Blb], [Blt])
    P.op("dve", lambda e: e.reduce_sum(ls[:, 0:2], lt[:].rearrange("p (a b) -> p a b", a=2), AX.X), [Blt], [Bls])
    act(P, ls[:, 2:4], ls[:, 0:2], AF.Exp, [Bls], [Bls])
    tt(P, "dve", ls[:, 4:5], ls[:, 3:4], ls[:, 2:3], ALU.subtract, [Bls], [Bls])
    tsc(P, "dve", ls[:, 5:6], ls[:, 4:5], -lam_init, None, ALU.add, None, [Bls], [Bls])
    sgl = P.sbuf("sgl", [128, 128], F32); Bsgl = P.buf("sgl")
    P.dma("sp", sgl[:], bcast_rows(io["subln_g"], 128), [], [Bsgl])
    tsc(P, "dve", sgl[:], sgl[:], 1.0 - lam_init, None, ALU.mult, None, [Bsgl], [Bsgl])

    kbr = Rot(P, "sb", "kb", [128, NCH * 128], BF16, 2)
    vbr = Rot(P, "sb", "vb", [128, NCH, 129], BF16, 2)
    psS = Rot(P, "ps", "psS", [128, 512], F32, 3)
    acc_f = [P.psum(f"acc{i}", [128, 512], F32) for i in range(4)]
    acc_t = [a[:, 0:258].rearrange("p (b e) -> p b e", b=2) for a in acc_f]
    acc_b = [P.buf(f"acc{i}", excl=True) for i in range(4)]
    ptr_ = Rot(P, "sb", "pT", [128, 512], BF16, 3)
    Or = [P.sbuf(f"O{t}", [128, 4, 129], F32) for t in range(2)]
    BO = [P.buf(f"O{t}") for t in range(2)]
    ur = Rot(P, "sb", "u", [128, 128], F32, 2)
    jr = Rot(P, "sb", "jk", [128, 128], F32, 2)
    sr = Rot(P, "sb", "s", [128, 8], F32, 3)
    ybr = Rot(P, "sb", "yb", [128, 4, 128], F32, 2)
    By = P.bufs_n("y", 5)
    qtiles = [(0, 512), (512, 512), (1024, 512), (1536, 512), (2048, 256)]
    for h in range(4):
        kb, Bkb = kbr.next()
        vb, Bvb = vbr.next()
        for r in range(NCORES):
            P.dma("sp" if r % 2 else "pool", kb[:, r * T:(r + 1) * T], io["kt_all"][r, 128 + 128 * h:256 + 128 * h, 0:T],
                  [], [Bkb], join=(r > 0))
            P.dma("pool" if r % 2 else "sp", vb[:, r * 16:(r + 1) * 16, :],
                  io["v_all"][r, 0:T, 130 + 129 * h:259 + 129 * h].rearrange("(c p) e -> p c e", p=128),
                  [], [Bvb], join=(r > 0))
        P.dma("sp", kb[:, S:S + L], io["kt_all"][0, 128 + 128 * h:256 + 128 * h, T:TT], [], [Bkb], join=True)
        P.dma("pool", vb[:, 128:130, :], io["v_all"][0, T:TT, 130 + 129 * h:259 + 129 * h].rearrange("(c p) e -> p c e", p=128),
              [], [Bvb], join=True)
        for qi_, (q0, nq) in enumerate(qtiles):
            nb = nq // 128
            chunks = list(range(NCH)) if q0 < T else [128, 129]
            for t in range(2):
                accs = (acc_t[2 * t], acc_b[2 * t], acc_t[2 * t + 1], acc_b[2 * t + 1])
                full_attn_pass(P, [(lambda ch, t=t, kb=kb: kb[64 * t:64 * t + 64, ch * 128:(ch + 1) * 128],
                                    qbT[64 * t:64 * t + 64, h, q0:q0 + nq], [Bkb, Bqb])],
                               nq, chunks, lambda ch, vb=vb: vb[:, ch, :], [Bvb], scale, psS, accs, ptr_, 129)
                P.op("act", lambda e, t=t: e.copy(Or[t][:, 0:2, :], acc_t[2 * t]), [acc_b[2 * t]], [BO[t]])
                if nb > 2:
                    P.op("act", lambda e, t=t: e.copy(Or[t][:, 2:4, :], acc_t[2 * t + 1]), [acc_b[2 * t + 1]], [BO[t]], join=True)
            yb, Byb = ybr.next()
            for b in range(nb):
                s_, Bs = sr.next()
                P.op("dve", lambda e, s_=s_, b=b: e.reciprocal(s_[:, 0:1], Or[0][:, b, 128:129]), [BO[0]], [Bs])
                P.op("dve", lambda e, s_=s_, b=b: e.reciprocal(s_[:, 1:2], Or[1][:, b, 128:129]), [BO[1]], [Bs])
                tt(P, "dve", s_[:, 2:3], s_[:, 1:2], ls[:, 5:6], ALU.mult, [Bs, Bls], [Bs])
                u, Bu = ur.next()
                P.op("dve", lambda e, u=u, s_=s_, b=b: e.tensor_scalar(u[:], Or[0][:, b, 0:128], s_[:, 0:1], None, ALU.mult),
                     [BO[0], Bs], [Bu])
                P.op("dve", lambda e, u=u, s_=s_, b=b: e.scalar_tensor_tensor(u[:], Or[1][:, b, 0:128], s_[:, 2:3], u[:],
                                                                               ALU.mult, ALU.add), [BO[1], Bs, Bu], [Bu])
                jk, Bjk = jr.next()
                act(P, jk[:], u[:], AF.Square, [Bu], [Bjk, Bs], accum_out=s_[:, 3:4])
                act(P, s_[:, 4:5], s_[:, 3:4], AF.Sqrt, [Bs], [Bs], scale=1.0 / 128, bias=EPS)
                recip(P, s_[:, 5:6], s_[:, 4:5], [Bs], [Bs])
                P.op("dve", lambda e, yb=yb, u=u, s_=s_, b=b: e.scalar_tensor_tensor(
                    yb[:, b, :], u[:], s_[:, 5:6], sgl[:], ALU.mult, ALU.mult), [Bu, Bs, Bsgl], [Byb], join=(b > 0))
            P.dma("sp", io["y"][q0:q0 + nq, 512 + 128 * h:640 + 128 * h].rearrange("(b p) c -> p b c", p=128),
                  yb[:, 0:nb, :], [Byb], [By[qi_]], join=True)
    P.build()


def phase_BO(nc, io, tag, last):
    P = Prog(nc, tag)
    identb, Bidb = load_common_B(P, io)
    wo = P.sbuf("wo", [128, 8, 1024], BF16); Bwo = P.buf("wo")
    P.dma("pool", wo[:], io["w_out"].rearrange("(k p) c -> p k c", p=128), [], [Bwo])
    gate = P.sbuf("gate", [128, 2, 1024], F32); Bgate = P.buf("gate")
    for j in range(2):
        P.dma("sp", gate[:, j, :], bcast_rows(io["mod"][j:j + 1, 2048:3072], 128), [], [Bgate], join=(j > 0))
    if last:
        fng = P.sbuf("fng", [128, 1024], F32); Bfng = P.buf("fng")
        P.dma("sp", fng[:], bcast_rows(io["final_g"], 128), [], [Bfng])
    yr = Rot(P, "sb", "yt", [128, 1024], F32, 2)
    gr = Rot(P, "sb", "gt", [128, 1024], F32, 2)
    xr = Rot(P, "sb", "xt", [128, 1024], F32, 2)
    ygr = Rot(P, "sb", "yg", [128, 1024], BF16, 2)
    ptr_ = Rot(P, "ps", "pt", [128, 8, 128], BF16, 2)
    ygTr = Rot(P, "sb", "ygT", [128, 8, 128], BF16, 2)
    pso = Rot(P, "ps", "pso", [128, 512], F32, 3)
    tmr = Rot(P, "sb", "tm", [128, 1024], F32, 2)
    x1r = Rot(P, "sb", "x1", [128, 1024], F32, 2)
    sr = Rot(P, "sb", "s", [128, 4], F32, 3)
    jr = Rot(P, "sb", "jk", [128, 1024], BF16, 2)
    Bout = P.buf("outd")
    nblk = NTB if last else TT // 128
    for i in range(nblk):
        own = i < NTB
        j = 0 if own else 1
        r0 = i * 128
        yt, Byt = yr.next(); gt, Bgt = gr.next(); xt, Bxt = xr.next()
        P.dma("sp", yt[:], io["y"][r0:r0 + 128, :], [], [Byt])
        P.dma("pool", gt[:], io["sg"][r0:r0 + 128, :], [], [Bgt])
        P.dma("sp", xt[:], io["x"][r0:r0 + 128, :] if own else io["xc"][r0 - T:r0 - T + 128, :], [], [Bxt])
        yg, Byg = ygr.next()
        tt(P, "pool", yg[:], yt[:], gt[:], ALU.mult, [Byt, Bgt], [Byg])
        pt, Bpt = ptr_.next()
        for k in range(8):
            tr(P, pt[:, k, :], yg[:, k * 128:(k + 1) * 128], identb[:], [Byg, Bidb], [Bpt])
        ygT, BygT = ygTr.next()
        if i % 2 == 0:
            P.op("act", lambda e, ygT=ygT, pt=pt: e.copy(ygT[:], pt[:]), [Bpt], [BygT])
        else:
            P.op("dve", lambda e, ygT=ygT, pt=pt: e.tensor_copy(ygT[:], pt[:]), [Bpt], [BygT])
        tm, Btm = tmr.next()
        x1, Bx1 = x1r.next()
        for ct in range(2):
            ps, Bps = pso.next()
            for k in range(8):
                mm(P, ps[:], ygT[:, k, :], wo[:, k, ct * 512:(ct + 1) * 512], k == 0, k == 7, [BygT, Bwo], [Bps])
            P.op("dve", lambda e, tm=tm, ps=ps, ct=ct, j=j: e.tensor_tensor(
                tm[:, ct * 512:(ct + 1) * 512], ps[:], gate[:, j, ct * 512:(ct + 1) * 512], ALU.mult),
                [Bps, Bgate], [Btm], join=(ct == 1))
        tt(P, "pool", x1[:], xt[:], tm[:], ALU.add, [Bxt, Btm], [Bx1])
        if last:
            s_, Bs = sr.next()
            jk, Bjk = jr.next()
            act(P, jk[:], x1[:], AF.Square, [Bx1], [Bjk, Bs], accum_out=s_[:, 0:1])
            act(P, s_[:, 1:2], s_[:, 0:1], AF.Sqrt, [Bs], [Bs], scale=1.0 / D, bias=EPS)
            recip(P, s_[:, 2:3], s_[:, 1:2], [Bs], [Bs])
            P.op("dve", lambda e, x1=x1, s_=s_, tm=tm: e.scalar_tensor_tensor(
                tm[:], x1[:], s_[:, 2:3], fng[:], ALU.mult, ALU.mult), [Bx1, Bs, Bfng], [Btm])
            P.dma("sp", io["out"][r0:r0 + 128, :], tm[:], [Btm], [Bout], join=True)
        else:
            dst = io["x1"][r0:r0 + 128, :] if own else io["xc1"][r0 - T:r0 - T + 128, :]
            P.dma("sp", dst, x1[:], [Bx1], [Bout], join=True)
    P.build()


def phase_BC1(nc, io, tag):
    P = Prog(nc, tag)
    scale = 192 ** -0.5
    NCH = S // 128 + 2
    identb, Bidb = load_common_B(P, io)
    qlT = P.sbuf("qlT", [128, 4, T], BF16); Bql = P.buf("qlT")
    P.dma("sp", qlT[:], io["q"][0:512, 0:T].rearrange("(h r) t -> r h t", r=128), [], [Bql])
    qpT = P.sbuf("qpT", [64, 4, T], BF16); Bqp = P.buf("qpT")
    P.dma("pool", qpT[:], io["q"][512:768, 0:T].rearrange("(h r) t -> r h t", r=64), [], [Bqp])
    wv = P.sbuf("wv", [128, 4, 128], BF16); Bwv = P.buf("wv")
    P.dma("pool", wv[:], io["wv"], [], [Bwv])
    kl = P.sbuf("kl", [128, NCH * 128], BF16); Bkl = P.buf("kl")
    kp = P.sbuf("kp", [64, NCH * 128], BF16); Bkp = P.buf("kp")
    vl = P.sbuf("vl", [128, NCH, 129], BF16); Bvl = P.buf("vl")
    for r in range(NCORES):
        P.dma("sp" if r % 2 else "pool", kl[:, r * T:(r + 1) * T], io["kt_all"][r, 0:128, 0:T], [], [Bkl], join=(r > 0))
        P.dma("pool" if r % 2 else "sp", kp[:, r * T:(r + 1) * T], io["kt_all"][r, 128:192, 0:T], [], [Bkp], join=(r > 0))
        P.dma("sp", vl[:, r * 16:(r + 1) * 16, :], io["v_all"][r, 0:T, 0:129].rearrange("(c p) e -> p c e", p=128),
              [], [Bvl], join=(r > 0))
    P.dma("sp", kl[:, S:S + L], io["kt_all"][0, 0:128, T:TT], [], [Bkl], join=True)
    P.dma("sp", kp[:, S:S + L], io["kt_all"][0, 128:192, T:TT], [], [Bkp], join=True)
    P.dma("pool", vl[:, 128:130, :], io["v_all"][0, T:TT, 0:129].rearrange("(c p) e -> p c e", p=128), [], [Bvl], join=True)
    psS = Rot(P, "ps", "psS", [128, 512], F32, 2)
    acc_f = [P.psum(f"acc{i}", [128, 512], F32) for i in range(4)]
    acc_t = [a[:, 0:258].rearrange("p (b e) -> p b e", b=2) for a in acc_f]
    acc_b = [P.buf(f"acc{i}", excl=True) for i in range(4)]
    ptr_ = Rot(P, "sb", "pT", [128, 512], BF16, 3)
    pst = Rot(P, "ps", "pst", [128, 8, 128], BF16, 1)
    pso = Rot(P, "ps", "pso", [128, 512], F32, 1)
    Or = Rot(P, "sb", "O", [128, 4, 129], F32, 2)
    sr = Rot(P, "sb", "s", [128, 4], F32, 3)
    ur = Rot(P, "sb", "u", [128, 128], BF16, 2)
    uTr = Rot(P, "sb", "uT", [128, 128], BF16, 2)
    ybr = Rot(P, "sb", "yb", [128, 4, 128], F32, 2)
    By = P.bufs_n("y", 5)
    par = 0
    for h in range(4):
        for qi_ in range(4):
            q0 = qi_ * 512
            accs = (acc_t[2 * par], acc_b[2 * par], acc_t[2 * par + 1], acc_b[2 * par + 1])
            full_attn_pass(P, [(lambda ch: kl[:, ch * 128:(ch + 1) * 128], qlT[:, h, q0:q0 + 512], [Bkl, Bql]),
                               (lambda ch: kp[:, ch * 128:(ch + 1) * 128], qpT[:, h, q0:q0 + 512], [Bkp, Bqp])],
                           512, list(range(NCH)), lambda ch: vl[:, ch, :], [Bvl], scale, psS, accs, ptr_, 129)
            O, BO = Or.next()
            P.op("act", lambda e, O=O, par=par: e.copy(O[:, 0:2, :], acc_t[2 * par]), [acc_b[2 * par]], [BO])
            P.op("act", lambda e, O=O, par=par: e.copy(O[:, 2:4, :], acc_t[2 * par + 1]), [acc_b[2 * par + 1]], [BO], join=True)
            par ^= 1
            yb, Byb = ybr.next()
            for b in range(4):
                s_, Bs = sr.next()
                P.op("dve", lambda e, s_=s_, O=O, b=b: e.reciprocal(s_[:, 0:1], O[:, b, 128:129]), [BO], [Bs])
                u, Bu = ur.next()
                P.op("dve", lambda e, u=u, s_=s_, O=O, b=b: e.tensor_scalar(u[:], O[:, b, 0:128], s_[:, 0:1], None, ALU.mult),
                     [BO, Bs], [Bu])
                pt, Bpt = pst.next()
                tr(P, pt[:, 0, :], u[:], identb[:], [Bu, Bidb], [Bpt])
                uT, BuT = uTr.next()
                P.op("act", lambda e, uT=uT, pt=pt: e.copy(uT[:], pt[:, 0, :]), [Bpt], [BuT])
                po, Bpo = pso.next()
                mm(P, po[:, 0:128], uT[:], wv[:, h, :], True, True, [BuT, Bwv], [Bpo])
                P.op("dve", lambda e, yb=yb, po=po, b=b: e.tensor_copy(yb[:, b, :], po[:, 0:128]), [Bpo], [Byb], join=(b > 0))
            P.dma("sp", io["y"][q0:q0 + 512, 128 * h:128 * h + 128].rearrange("(b p) c -> p b c", p=128),
                  yb[:], [Byb], [By[qi_]], join=True)
    P.build()


def bcast_mid(ap2d, n):
    a = [list(x) for x in ap2d.ap]
    return bass.AP(ap2d.tensor, ap2d.offset, [a[0], [0, n]] + a[1:])


def phase_BD1(nc, io, tag):
    P = Prog(nc, tag)
    scale = 64 ** -0.5
    NX = 22
    qdT = P.sbuf("qdT", [64, 8, T], BF16); Bqd = P.buf("qdT")
    P.dma("sp", qdT[:], io["q"][768:1280, 0:T].rearrange("(h d) t -> d h t", d=64), [], [Bqd])
    kdX = P.sbuf("kdX", [64, 8, TT], BF16); Bkd = P.buf("kdX")
    P.dma("pool", kdX[:], io["kt"][192:704, :].rearrange("(h d) t -> d h t", d=64), [], [Bkd])
    vdX = P.sbuf("vdX", [128, 18, 520], BF16); Bvd = P.buf("vdX")
    P.dma("sp", vdX[:], io["v"][:, 129:649].rearrange("(c p) e -> p c e", p=128), [], [Bvd])
    hidx = P.sbuf("hidx", [128, NIDX], I32); Bhidx = P.buf("hidx")
    P.dma("sp", hidx[:], io["hidx"], [], [Bhidx])
    kdH = P.sbuf("kdH", [64, 8, 2, 256], BF16); BkdH = P.buf("kdH")
    vdH = P.sbuf("vdH", [128, 4, 649], BF16); BvdH = P.buf("vdH")
    for side in range(2):
        for h in range(8):
            c = 6 + 8 * side + h
            P.gather(kdH[:, h, side, :], io["hk_all"], hidx[0:64, c:c + 1], [Bhidx], [BkdH], join=not (side == 0 and h == 0))
        for c2 in range(2):
            c = 22 + 2 * side + c2
            P.gather(vdH[:, 2 * side + c2, :], io["v_all"], hidx[:, c:c + 1], [Bhidx], [BvdH], join=not (side == 0 and c2 == 0))
    cm = P.sbuf("cm", [128, 64], F32); Bcm = P.buf("cm")
    P.dma("sp", cm[:], io["colmask"], [], [Bcm])
    vm = P.sbuf("vm", [128, NTB, 6, 2], F32); Bvm = P.buf("vm")
    P.dma("sp", vm[:], io["vmD"], [], [Bvm])
    TB = P.sbuf("TB", [128, 120, 64], F32); BTB = P.buf("TB")
    rp = io["rp_pad"]
    TBr = P.sbuf("TBr", [128, 120 * 64], F32); BTBr = P.buf("TBr")
    for a in range(2):
        src = bass.AP(rp.tensor, rp.offset, [[1, 64], [128, 120], [1, 64]])
        P.dma("sp" if a else "pool", TBr[64 * a:64 * a + 64, :].rearrange("p (r q) -> p r q", q=64), src, [], [BTBr], join=(a > 0))
    J2 = P.sbuf("J2", [128, 128], F32); BJ2 = P.buf("J2")
    P.dma("sp", J2[:], io["antidiag"], [], [BJ2])
    psT = Rot(P, "ps", "psT", [128, 512], F32, 2)
    TBf = TB[:].rearrange("p r q -> p (r q)")
    for cchunk in range(15):
        ps, Bps = psT.next()
        mm(P, ps[:], J2[:], TBr[:, cchunk * 512:(cchunk + 1) * 512], True, True, [BJ2, BTBr], [Bps])
        P.op("act", lambda e, ps=ps, cchunk=cchunk: e.activation(TBf[:, cchunk * 512:(cchunk + 1) * 512], ps[:], AF.Exp),
             [Bps], [BTB], join=(cchunk > 0))
    tt(P, "dve", TB[:], TB[:], bcast_mid(cm[:], 120), ALU.mult, [BTB, Bcm], [BTB])
    TB4 = TB[:].rearrange("p (h r) q -> p h r q", h=8)
    psS = Rot(P, "ps", "psS", [128, 512], F32, 2)
    accr = Rot(P, "ps", "acc", [128, 512], F32, 4)
    ptr_ = Rot(P, "sb", "pT", [128, 512], BF16, 3)
    ebr = Rot(P, "sb", "eb", [128, 8, 128], BF16, 3)
    ytr = Rot(P, "sb", "yt", [128, 512], F32, 2)
    ztr = Rot(P, "sb", "zt", [128, 8], F32, 2)
    By = P.bufs_n("y", 5)
    for n in range(NTB):
        q0 = n * 128
        cis = list(range(0, 6)) if n == 0 else (list(range(-1, 5)) if n == NTB - 1 else list(range(0, 5)))
        def ksrc(oc):
            if oc < 0:
                return (lambda h, oc=oc: kdH[:, h, 0, (oc + 2) * 128:(oc + 3) * 128],
                        lambda h, oc=oc: vdH[:, oc + 2, 129 + 65 * h:129 + 65 * h + 65], [BkdH, BvdH])
            if oc >= NTB and oc < NTB + 2:
                return (lambda h, oc=oc: kdH[:, h, 1, (oc - NTB) * 128:(oc - NTB + 1) * 128],
                        lambda h, oc=oc: vdH[:, 2 + oc - NTB, 129 + 65 * h:129 + 65 * h + 65], [BkdH, BvdH])
            if oc >= 100:
                oc = NTB + (oc - 100)
            return (lambda h, oc=oc: kdX[:, h, oc * 128:(oc + 1) * 128],
                    lambda h, oc=oc: vdX[:, oc, 65 * h:65 * h + 65], [Bkd, Bvd])
        chunks = [(ksrc(n + ci - 2), ci) for ci in cis] + [(ksrc(100), None), (ksrc(101), None)]
        accs = [accr.next() for _ in range(2)]
        for idx, ((kfn, vfn, kvb), ci) in enumerate(chunks):
            eb = None
            if ci is not None:
                eb, Beb = ebr.next()
                slot = ci - cis[0]
                first = True
                for a in range(2):
                    for b in range(2):
                        dr = 3 + 2 * ci + a - b
                        P.op("pool", lambda e, eb=eb, a=a, b=b, dr=dr, n=n, slot=slot: e.tensor_scalar(
                            eb[64 * a:64 * a + 64, :, 64 * b:64 * b + 64], TB4[64 * a:64 * a + 64, :, dr, :],
                            vm[64 * a:64 * a + 64, n, slot, b:b + 1], None, ALU.mult), [BTB, Bvm], [Beb], join=not first)
                        first = False
            for hg in range(2):
                ps, Bps = psS.next()
                for hh in range(4):
                    h = 4 * hg + hh
                    mm(P, ps[:, hh * 128:(hh + 1) * 128], kfn(h), qdT[:, h, q0:q0 + 128],
                       True, True, kvb + [Bqd], [Bps])
                pT, BpT = ptr_.next()
                act(P, pT[:], ps[:], AF.Exp, [Bps], [BpT], scale=scale)
                if eb is not None:
                    P.op("dve", lambda e, pT=pT, eb=eb, hg=hg: e.tensor_tensor(
                        pT[:], pT[:], eb[:, 4 * hg:4 * hg + 4, :].rearrange("p h q -> p (h q)"), ALU.mult),
                        [BpT, Beb], [BpT])
                acc_, Bacc = accs[hg]
                acc = acc_[:, 0:260].rearrange("p (g e) -> p g e", g=4)
                for hh in range(4):
                    h = 4 * hg + hh
                    mm(P, acc[:, hh, :], pT[:, hh * 128:(hh + 1) * 128], vfn(h),
                       idx == 0 and hh == 0, idx == len(chunks) - 1, [BpT] + kvb, [Bacc])
        yt, Byt = ytr.next()
        for hg in range(2):
            acc_, Bacc = accs[hg]
            acc = acc_[:, 0:260].rearrange("p (g e) -> p g e", g=4)
            zt, Bzt = ztr.next()
            P.op("dve", lambda e, zt=zt, acc=acc: e.reciprocal(zt[:, 0:4], acc[:, :, 64]), [Bacc], [Bzt])
            for hh in range(4):
                h = 4 * hg + hh
                P.op("dve", lambda e, yt=yt, acc=acc, zt=zt, hh=hh, h=h: e.tensor_scalar(
                    yt[:, h * 64:(h + 1) * 64], acc[:, hh, 0:64], zt[:, hh:hh + 1], None, ALU.mult),
                    [Bacc, Bzt], [Byt], join=not (hg == 0 and hh == 0))
        P.dma("sp", io["y"][q0:q0 + 128, 512:1024], yt[:], [Byt], [By[n // 4]], join=True)
    P.build()


def _rope_tables():
    t = np.arange(S)
    row = (t // GRID_W).astype(np.float32)
    col = (t % GRID_W).astype(np.float32)
    inv = (np.float32(10000.0) ** (-np.arange(16, dtype=np.float32) / np.float32(16))).astype(np.float32)
    ang_r = row[:, None] * inv[None, :]
    ang_c = col[:, None] * inv[None, :]
    ang = np.concatenate([ang_r, ang_r, ang_c, ang_c], axis=-1).astype(np.float32)
    return np.cos(ang).astype(np.float32), np.sin(ang).astype(np.float32)


def _rope_core_tables(r, cos, sin):
    cT = np.ones((128, TT), np.float32)
    sT = np.zeros((128, TT), np.float32)
    c = cos[r * T:(r + 1) * T].T
    s_ = (sin[r * T:(r + 1) * T] * ROPE_SIGN[None, :]).T
    cT[0:64, 0:T] = c; cT[64:128, 0:T] = c
    sT[0:64, 0:T] = s_; sT[64:128, 0:T] = s_
    return cT, sT


class IO(dict):
    pass


def _mk(nc, specs):
    io = IO()
    for (name, shape, dt, kind) in specs:
        io[name] = nc.dram_tensor(name, list(shape), dt, kind=kind).ap()
    return io


def _dt(a):
    if a.dtype == np.float32:
        return F32
    if a.dtype == NPBF:
        return BF16
    if a.dtype == np.int32:
        return I32
    raise ValueError(a.dtype)


def _launch(build, in_maps, outs):
    nc = bass.Bass("TRN2", target_bir_lowering=False)
    specs = [(k, v.shape, _dt(v), "ExternalInput") for k, v in in_maps[0].items()]
    specs += [(k, shp, dt, "ExternalOutput") for (k, shp, dt) in outs]
    io = _mk(nc, specs)
    build(nc, io)
    res = run_bass_kernel_spmd(nc, in_maps, core_ids=list(range(NCORES)))
    return res.results


def _arr_col(v):
    return np.ascontiguousarray(v.reshape(8, 128).T)


def _maskA(r):
    kk = np.arange(128)[:, None]
    qq = np.arange(128)[None, :]
    tp = np.tile((qq <= kk).astype(np.float32), (1, 4))
    tn = np.tile((kk <= qq).astype(np.float32), (1, 4))
    z = np.zeros_like(tp)
    return np.stack([z if r == 0 else tp, tp, tn, z if r == NCORES - 1 else tn]).astype(NPBF)


def _colmask():
    qc = np.arange(64)
    cs = np.clip(qc - 8, 0, 48)
    kc = np.arange(64)[:, None]
    m = ((kc >= cs[None, :]) & (kc < cs[None, :] + 16)).astype(np.float32)
    return np.ascontiguousarray(np.concatenate([m, m], 0))


def _antidiag():
    j = np.zeros((128, 128), np.float32)
    for a in range(2):
        for kc in range(64):
            j[64 * a + 63 - kc, 64 * a + kc] = 1.0
    return j


def _vmD(r):
    vm = np.zeros((128, NTB, 6, 2), np.float32)
    for n in range(NTB):
        cis = list(range(0, 6)) if n == 0 else (list(range(-1, 5)) if n == NTB - 1 else list(range(0, 5)))
        for slot, ci in enumerate(cis):
            for a in range(2):
                for b in range(2):
                    gr = 32 * r + 2 * n + b
                    rs = min(max(gr - 4, 0), 248)
                    kr = 32 * r + 2 * n - 4 + 2 * ci + a
                    if rs <= kr <= rs + 7:
                        vm[64 * a:64 * a + 64, n, slot, b] = 1.0
    return vm


def phase_AG(nc, pairs, tag):
    P = Prog(nc, tag)
    prev = []
    for i, (own, allg) in enumerate(pairs):
        b = P.buf(f"ag{i}")
        P.collective("AllGather", own, allg, prev, [b])
        prev = [b]
    P.build()


def _hidx(r):
    rp, rn = max(r - 1, 0), min(r + 1, NCORES - 1)
    p = np.arange(128, dtype=np.int64)
    cols = []
    for j in range(2):
        cols.append(rp * 256 + 128 + 64 * j + p)
    for j in range(2):
        cols.append(rn * 256 + 0 + 64 * j + p)
    cols.append(rp * TT + (T - 128) + p)
    cols.append(rn * TT + 0 + p)
    for h in range(8):
        cols.append(rp * 1024 + 512 + 64 * h + p)
    for h in range(8):
        cols.append(rn * 1024 + 0 + 64 * h + p)
    for c2 in range(2):
        cols.append(rp * TT + (T - 256) + 128 * c2 + p)
    for c2 in range(2):
        cols.append(rn * TT + 128 * c2 + p)
    a = np.stack(cols, 1)
    a[64:, 0:4] = 0
    a[64:, 6:22] = 0
    return np.ascontiguousarray(a.astype(np.int32))


def _layer_inputs(lay, sfx, norm_g, w_ada, b_ada, w_in):
    cols = []
    for c0 in lay.rope_cols:
        for hh in range(2):
            cols.append(c0 + 64 * hh + ROPE_PERM)
    cols = np.concatenate(cols)
    if lay.idx == 1:
        cols = np.concatenate([384 + ROPE_PERM, 384 + ROPE_PERM])
    return {"norm_g" + sfx: _arr_col(norm_g), "b_ada2" + sfx: np.ascontiguousarray(np.tile(b_ada.reshape(1, -1), (2, 1))),
            "w_ada" + sfx: np.ascontiguousarray(w_ada), "w_in" + sfx: np.ascontiguousarray(w_in),
            "w_rope" + sfx: np.ascontiguousarray(w_in[:, cols])}


_STOP = [None]


def build_fused(nc, ext):
    def scr(name, shape, dt):
        return nc.dram_tensor(name, list(shape), dt, kind="Internal").ap()

    common = {k: ext[k] for k in ("cc2", "cosT", "sinT", "ident", "hidx")}
    y = scr("y_scr", (TT, 1024), F32)
    x1 = scr("x1_scr", (T, 1024), F32)
    xc1 = scr("xc1_scr", (L, 1024), F32)
    xin = [ext["x"], x1]
    xcin = [ext["xc"], xc1]
    for li, lay in enumerate((Lay0, Lay1)):
        sfx = str(li)
        q = scr("q" + sfx, (lay.QROWS, TT), BF16)
        kt = scr("kt" + sfx, (lay.KTROWS, TT), BF16)
        v = scr("v" + sfx, (TT, lay.VCOLS), BF16)
        hk = scr("hk" + sfx, lay.HKSHAPE, BF16)
        sg = scr("sg" + sfx, (TT, 1024), F32)
        mod = scr("mod" + sfx, (2, 3072), F32)
        kt_all = scr("kt_all" + sfx, (NCORES * lay.KTROWS, TT), BF16)
        v_all = scr("v_all" + sfx, (NCORES * TT, lay.VCOLS), BF16)
        hk_all = scr("hk_all" + sfx, (NCORES * lay.HKSHAPE[0], lay.HKSHAPE[1]), BF16)
        ioA = IO(common)
        ioA.update(x=xin[li], xc=xcin[li], q=q, kt=kt, v=v, hk=hk, sg=sg, mod=mod)
        for k in ("norm_g", "b_ada2", "w_ada", "w_in", "w_rope"):
            ioA[k] = ext[k + sfx]
        if li == 1:
            for k in ("kvng_row", "w_qb_all", "qng_col", "wkT"):
                ioA[k] = ext[k]
        phase_A(nc, lay, ioA, f"a{li}_")
        nc.all_engine_barrier()
        if _STOP[0] == "A":
            return
        phase_AG(nc, [(kt, kt_all), (v, v_all), (hk, hk_all)], f"g{li}_")
        nc.all_engine_barrier()
        ioB = IO(common)
        ioB.update(q=q, kt=kt, v=v, sg=sg, mod=mod, y=y, x=xin[li], xc=xcin[li], hk_all=hk_all, v_all=v_all,
                   kt_all=kt_all.rearrange("(r a) c -> r a c", r=NCORES),
                   w_out=ext["w_out" + sfx])
        ioB3 = IO(ioB)
        ioB3["v_all"] = v_all.rearrange("(r a) c -> r a c", r=NCORES)
        if li == 0:
            for k in ("maskA", "a_sink", "b_lambda", "subln_g"):
                ioB[k] = ext[k]; ioB3[k] = ext[k]
            ioB["x1"] = x1; ioB["xc1"] = xc1
            if _STOP[0] == "AG":
                return
            phase_BA0(nc, ioB, "ba_")
            nc.all_engine_barrier()
            if _STOP[0] == "BA":
                return
            phase_BB0(nc, ioB3, "bb_")
            nc.all_engine_barrier()
            if _STOP[0] == "BB":
                return
            phase_BO(nc, ioB, "bo0_", last=False)
            nc.all_engine_barrier()
            if _STOP[0] == "BO":
                return
        else:
            for k in ("wv", "rp_pad", "colmask", "vmD", "antidiag", "final_g"):
                ioB[k] = ext[k]; ioB3[k] = ext[k]
            ioB["out"] = ext["out"]
            phase_BC1(nc, ioB3, "bc_")
            nc.all_engine_barrier()
            phase_BD1(nc, ioB, "bd_")
            nc.all_engine_barrier()
            phase_BO(nc, ioB, "bo1_", last=True)


def kernel(**inputs):
    inp = {k: np.asarray(v) for k, v in inputs.items()}
    cos, sin = _rope_tables()
    x = inp["x"][0]
    ident = np.eye(128, dtype=np.float32)
    cc2 = np.ascontiguousarray(np.stack([_arr_col(inp["c"].reshape(-1)), _arr_col(inp["c_ctx"].reshape(-1))], axis=-1))
    shared = dict(xc=np.ascontiguousarray(inp["ctx"][0]), cc2=cc2, ident=ident)
    shared.update(_layer_inputs(Lay0, "0", inp["ev_norm_g"][0], inp["ev_w_ada"][0], inp["ev_b_ada"][0], inp["ev_w_in"][0]))
    shared.update(_layer_inputs(Lay1, "1", inp["od_norm_g"][0], inp["od_w_ada"][0], inp["od_b_ada"][0], inp["od_w_in"][0]))
    shared.update(a_sink=np.ascontiguousarray(inp["ev_a_sink"][0].reshape(1, 8)),
                  b_lambda=np.ascontiguousarray(inp["ev_b_lambda"][0].reshape(1, 256)),
                  subln_g=np.ascontiguousarray(inp["ev_b_subln_g"][0].reshape(1, 128)),
                  w_out0=np.ascontiguousarray(inp["ev_w_out"][0]), w_out1=np.ascontiguousarray(inp["od_w_out"][0]))
    w_qb = inp["od_c_w_qb"][0]
    pe_cols = np.concatenate([192 * h + 128 + ROPE_PERM for h in range(4)])
    w_kvb = inp["od_c_w_kvb"][0].reshape(128, 4, 256)
    rpb = inp["od_d_rpb"][0]
    rp = np.zeros((8, 15, 128), np.float32)
    rp[:, :, 48:79] = rpb[:, :, ::-1]
    shared.update(kvng_row=np.ascontiguousarray(inp["od_c_kv_norm_g"][0].reshape(1, 128)),
                  w_qb_all=np.ascontiguousarray(np.concatenate([w_qb, w_qb[:, pe_cols]], axis=1)),
                  qng_col=np.ascontiguousarray(inp["od_c_q_norm_g"][0].reshape(2, 128).T),
                  wkT=np.ascontiguousarray(np.transpose(w_kvb[:, :, 0:128], (2, 1, 0))),
                  wv=np.ascontiguousarray(w_kvb[:, :, 128:256]), rp_pad=np.ascontiguousarray(rp.reshape(120, 128)),
                  colmask=_colmask(), antidiag=_antidiag(),
                  final_g=np.ascontiguousarray(inp["final_norm_g"].reshape(1, 1024)))
    maps = []
    for r in range(NCORES):
        cT, sT = _rope_core_tables(r, cos, sin)
        m = dict(shared)
        m.update(x=np.ascontiguousarray(x[r * T:(r + 1) * T]), cosT=cT, sinT=sT, hidx=_hidx(r), maskA=_maskA(r), vmD=_vmD(r))
        maps.append(m)
    res = _launch(build_fused, maps, [("out", (T, 1024), F32)])
    out = np.concatenate([res[r]["out"] for r in range(NCORES)], axis=0)
    return out.reshape(1, S, D).astype(np.float32)
```

```python
import math
import numpy as np
from contextlib import ExitStack
import ml_dtypes
import concourse.bass as bass
import concourse.mybir as mybir
from concourse.bass_utils import run_bass_kernel_spmd

F32 = mybir.dt.float32
BF16 = mybir.dt.bfloat16
I32 = mybir.dt.int32
AF = mybir.ActivationFunctionType
ALU = mybir.AluOpType
AX = mybir.AxisListType
NPBF = ml_dtypes.bfloat16

NCORES = 8
S = 16384
T = 2048
NTB = 16
L = 256
TT = T + L
D = 1024
EPS = 1e-6
GRID_W = 64
NIDX = 26
BD1_SKEW = 0


class Buf:
    __slots__ = ("name", "writers", "readers", "dma_sem", "dma_cnt", "excl")

    def __init__(self, name, excl=False):
        self.name = name
        self.excl = excl
        self.writers = []
        self.readers = []
        self.dma_sem = None
        self.dma_cnt = 0


class Op:
    __slots__ = ("eng", "emit", "deps", "signal", "sigval", "is_dma", "dsem", "dval", "cc_inc")

    def __init__(self, eng, emit, is_dma=False):
        self.eng = eng
        self.emit = emit
        self.deps = []
        self.signal = False
        self.sigval = 0
        self.is_dma = is_dma
        self.dsem = None
        self.dval = 0
        self.cc_inc = 16


class Prog:
    ENGS = ("pe", "act", "dve", "pool", "sp")

    def __init__(self, nc, tag=""):
        self.nc = nc
        self.tag = tag
        self.ops = {e: [] for e in self.ENGS}
        self.stack = ExitStack()
        self.esem = {}
        self.bufs = []
        self.dma_bufs = []
        self.sems = []

    def sem(self, name):
        h = self.nc.alloc_semaphore(name=self.tag + name)
        self.sems.append(h)
        return h

    def sbuf(self, name, shape, dt):
        return self.stack.enter_context(self.nc.sbuf_tensor(self.tag + name, shape, dt))

    def psum(self, name, shape, dt):
        return self.stack.enter_context(self.nc.psum_tensor(self.tag + name, shape, dt))

    def buf(self, name, excl=False):
        b = Buf(name, excl)
        self.bufs.append(b)
        return b

    def bufs_n(self, name, n):
        return [self.buf(f"{name}{i}") for i in range(n)]

    def _add(self, op, reads, writes, join=False):
        xr = [b for b in reads if b.excl]
        reads = [b for b in reads if not b.excl]
        xw = [b for b in writes if b.excl]
        writes = [b for b in writes if not b.excl]
        deps = []
        for b in xr + xw:
            deps.extend(b.readers)
            deps.extend(b.writers)
        for b in reads:
            deps.extend(b.writers)
        for b in writes:
            deps.extend(b.readers)
            if not join:
                deps.extend(b.writers)
        op.deps = [d for d in deps if not (d.eng == "pe" and op.eng == "pe" and not d.is_dma and not op.is_dma)]
        for b in xr + xw:
            b.writers = [op]
            b.readers = []
        for b in reads:
            b.readers.append(op)
        for b in writes:
            if join:
                b.writers.append(op)
            else:
                b.writers = [op]
            b.readers = []
        self.ops[op.eng].append(op)
        return op

    def op(self, eng, emit, reads=(), writes=(), join=False):
        return self._add(Op(eng, emit), list(reads), list(writes), join)

    def dma(self, eng, out, in_, reads, writes, join=False, emit=None, sem_buf=None, **kw):
        assert len(writes) == 1
        b = sem_buf if sem_buf is not None else (reads[0] if (len(reads) == 1 and self.outbound(out)) else writes[0])
        if b.dma_sem is None:
            b.dma_sem = self.sem("d_" + b.name)
            self.dma_bufs.append(b)
        b.dma_cnt += 16
        if emit is None:
            emit = lambda e, out=out, in_=in_, kw=kw: e.dma_start(out=out, in_=in_, **kw)
        o = Op(eng, emit, is_dma=True)
        o.dsem = b.dma_sem
        o.dval = b.dma_cnt
        return self._add(o, list(reads), list(writes), join)

    @staticmethod
    def outbound(out_ap):
        try:
            return "DRam" in type(out_ap.tensor).__name__ or "Dram" in type(out_ap.tensor).__name__ or "DRAM" in type(out_ap.tensor).__name__
        except Exception:
            return False

    def gather(self, out, src2d, idx, reads, writes, join=False):
        def emit(e, out=out, src2d=src2d, idx=idx):
            return e.indirect_dma_start(out=out, out_offset=None, in_=src2d,
                                        in_offset=bass.IndirectOffsetOnAxis(ap=idx, axis=0))
        return self.dma("pool", None, None, reads, writes, join=join, emit=emit, sem_buf=writes[0])

    def collective(self, kind, in_ap, out_ap, reads, writes):
        def emit(e):
            return e.collective_compute(kind, ALU.bypass, replica_groups=[list(range(NCORES))], ins=[in_ap], outs=[out_ap])
        o = self.dma("pool", None, None, reads, writes, emit=emit, sem_buf=writes[0])
        b = writes[0]
        b.dma_cnt += 1 - 16
        o.dval = b.dma_cnt
        o.cc_inc = 1
        return o

    def wait_all(self, eng, bufs):
        return self._add(Op(eng, None), list(bufs), [])

    def build(self):
        nc = self.nc
        fin = Op("sp", None)
        fin.deps = []
        for b in self.dma_bufs:
            d = Op("sp", None, is_dma=True)
            d.dsem = b.dma_sem
            d.dval = b.dma_cnt
            fin.deps.append(d)
        self.ops["sp"].append(fin)
        for e in self.ENGS:
            self.esem[e] = self.sem("e_" + e)
        for e in self.ENGS:
            for o in self.ops[e]:
                for d in o.deps:
                    if not d.is_dma:
                        d.signal = True
        for e in self.ENGS:
            c = 0
            for o in self.ops[e]:
                if o.signal and not o.is_dma:
                    c += 1
                    o.sigval = c
        engobj = {"pe": "tensor", "act": "scalar", "dve": "vector", "pool": "gpsimd", "sp": "sync"}

        def make(e):
            def fn(eng):
                waited = {}
                for o in self.ops[e]:
                    need = {}
                    for d in o.deps:
                        if d.is_dma:
                            s, v = d.dsem, d.dval
                        else:
                            s, v = self.esem[d.eng], d.sigval
                        k = id(s)
                        if k not in need or need[k][1] < v:
                            need[k] = (s, v)
                    for k, (s, v) in need.items():
                        if waited.get(k, 0) >= v:
                            continue
                        waited[k] = v
                        eng.wait_ge(s, v)
                    if o.emit is None:
                        continue
                    inst = o.emit(eng)
                    if o.is_dma:
                        inst.then_inc(o.dsem, o.cc_inc)
                    elif o.signal:
                        inst.then_inc(self.esem[e], 1)
                last = max([o.sigval for o in self.ops[e]] + [0])
                if last > 0:
                    eng.wait_ge(self.esem[e], last)
            return fn

        with nc.Block() as block:
            for e in self.ENGS:
                if self.ops[e]:
                    getattr(block, engobj[e])(make(e))
        self.stack.close()
        nc.all_engine_barrier()
        nc.clear_and_free_semaphores(self.sems)
        nc.all_engine_barrier()


def mm(P, out, lhsT, rhs, start, stop, reads, writes):
    return P.op("pe", lambda e: e.matmul(out, lhsT, rhs, start=start, stop=stop, skip_group_check=True), reads, writes)


def tr(P, out, in_, ident, reads, writes):
    return P.op("pe", lambda e: e.transpose(out, in_, ident), reads, writes)


def act(P, out, in_, func, reads, writes, **kw):
    return P.op("act", lambda e: e.activation(out, in_, func, **kw), reads, writes)


def tsc(P, eng, out, in0, s1, s2, op0, op1, reads, writes):
    if s2 is None:
        return P.op(eng, lambda e: e.tensor_scalar(out, in0, s1, None, op0), reads, writes)
    return P.op(eng, lambda e: e.tensor_scalar(out, in0, s1, s2, op0, op1), reads, writes)


def tt(P, eng, out, in0, in1, op, reads, writes):
    return P.op(eng, lambda e: e.tensor_tensor(out, in0, in1, op), reads, writes)


def cp(P, eng, out, in_, reads, writes):
    if eng == "act":
        return P.op(eng, lambda e: e.copy(out, in_), reads, writes)
    return P.op(eng, lambda e: e.tensor_copy(out, in_), reads, writes)


def mset(P, eng, ap, val, writes):
    return P.op(eng, lambda e: e.memset(ap, val), [], writes)


def recip(P, out, in_, reads, writes):
    return P.op("dve", lambda e: e.reciprocal(out, in_), reads, writes)


def bcast_rows(ap_row, nparts):
    a = [list(x) for x in ap_row.ap]
    a[0] = [0, nparts]
    return bass.AP(ap_row.tensor, ap_row.offset, a)


class Rot:
    def __init__(self, P, kind, name, shape, dt, n):
        alloc = P.sbuf if kind == "sb" else P.psum
        self.t = [alloc(f"{name}{i}", shape, dt) for i in range(n)]
        self.b = [P.buf(f"{name}{i}", excl=(kind == "ps")) for i in range(n)]
        self.i = 0
        self.n = n

    def next(self):
        r = (self.t[self.i], self.b[self.i])
        self.i = (self.i + 1) % self.n
        return r


ROPE_PERM = np.concatenate([np.arange(16, 32), np.arange(0, 16), np.arange(48, 64), np.arange(32, 48)])
ROPE_SIGN = np.concatenate([-np.ones(16), np.ones(16), -np.ones(16), np.ones(16)]).astype(np.float32)


class Lay0:
    idx = 0
    C = 3328
    fm = ([(128 * i, 128, i, "q", 128 * i) for i in range(4)]
          + [(512, 128, 4, "kt", 0)]
          + [(768 + 128 * i, 128, 5 + i, "q", 512 + 128 * i) for i in range(4)]
          + [(1280 + 128 * i, 128, 9 + i, "kt", 128 + 128 * i) for i in range(4)])
    rope_cols = ([128 * i for i in range(4)] + [512] + [768 + 128 * i for i in range(4)]
                 + [1280 + 128 * i for i in range(4)])
    NR = 13
    QROWS = 1024
    KTROWS = 640
    VCOLS = 646
    tmv = [(640, 128, 2, 64, 0), (1792, 512, 4, 128, 130)]
    gcol = 2304
    halo = {("kt", 0): (0, 128, 128)}
    HKSHAPE = (256, 128)


class Lay1:
    idx = 1
    C = 3008
    fm = ([(448 + 128 * i, 128, None, "q", 768 + 128 * i) for i in range(4)]
          + [(960 + 128 * i, 128, None, "kt", 192 + 128 * i) for i in range(4)]
          + [(384, 64, 0, "kt", 128)])
    rope_cols = [384]
    NR = 1
    QROWS = 1280
    KTROWS = 704
    VCOLS = 649
    tmv = [(1472, 512, 8, 64, 129)]
    gcol = 1984
    halo = {("kt", 192 + 128 * i): (128 * i, 256, 512) for i in range(4)}
    HKSHAPE = (1024, 256)


def phase_A(nc, lay, io, tag):
    P = Prog(nc, tag)
    C = lay.C
    NRC = lay.NR * 128
    identb = P.sbuf("identb", [128, 128], BF16); Bidb = P.buf("identb")
    P.dma("pool", identb[:], io["ident"], [], [Bidb])
    identf = P.sbuf("identf", [128, 128], F32); Bidf = P.buf("identf")
    P.dma("sp", identf[:], io["ident"], [], [Bidf])
    wbf = P.sbuf("wbf", [128, 8, C], BF16); Bw = P.bufs_n("w", 8)
    wrp = P.sbuf("wrp", [128, 8, NRC], BF16); Bwr = P.bufs_n("wr", 8)
    cosT = P.sbuf("cosT", [128, TT], F32); Bcos = P.buf("cos")
    sinT = P.sbuf("sinT", [128, TT], F32); Bsin = P.buf("sin")
    P.dma("sp", cosT[:], io["cosT"], [], [Bcos])
    P.dma("sp", sinT[:], io["sinT"], [], [Bsin])
    cc = P.sbuf("cc", [128, 8, 2], F32); Bcc = P.buf("cc")
    P.dma("sp", cc[:], io["cc2"], [], [Bcc])
    ng = P.sbuf("ng", [128, 8], F32); Bng = P.buf("ng")
    P.dma("sp", ng[:], io["norm_g"], [], [Bng])
    sc = P.sbuf("sc", [128, 8, 2], F32); Bsc = P.buf("sc")
    act(P, sc[:], cc[:], AF.Silu, [Bcc], [Bsc])
    warot = Rot(P, "sb", "wada", [128, 8, 256], F32, 2)
    barot = Rot(P, "sb", "bada", [2, 256], F32, 2)
    mrrot = Rot(P, "sb", "mr", [2, 256], F32, 2)
    psA = Rot(P, "ps", "psA", [128, 512], F32, 2)
    psB = Rot(P, "ps", "psB", [128, 512], F32, 2)
    psm = psA
    pscol = P.psum("pscol", [128, 512], F32); Bpscol = P.buf("pscol", excl=True)
    w_ada_v = io["w_ada"].rearrange("(k p) c -> p k c", p=128)
    Bmodd = P.buf("mod_dram")
    for ct in range(12):
        wa, Bwa = warot.next()
        P.dma("sp", wa[:], w_ada_v[:, :, ct * 256:(ct + 1) * 256], [], [Bwa])
        ba, Bba = barot.next()
        P.dma("sp", ba[:], io["b_ada2"][:, ct * 256:(ct + 1) * 256], [], [Bba])
        ps, Bps = psm.next()
        for k in range(8):
            mm(P, ps[0:2, 0:256], sc[:, k, :], wa[:, k, :], k == 0, k == 7, [Bsc, Bwa], [Bps])
        mr, Bmr = mrrot.next()
        tt(P, "dve", mr[:], ps[0:2, 0:256], ba[:], ALU.add, [Bps, Bba], [Bmr])
        P.dma("sp", io["mod"][:, ct * 256:(ct + 1) * 256], mr[:], [Bmr], [Bmodd], join=True)
        if ct < 8:
            for cc_ in range(2):
                ch = 2 * ct + cc_
                mm(P, pscol[:, 2 * ch:2 * ch + 2], mr[0:2, cc_ * 128:(cc_ + 1) * 128], identf[0:2, 0:2], True, True,
                   [Bmr, Bidf], [Bpscol])
    for k in range(8):
        P.dma("pool", wbf[:, k, :], io["w_in"][k * 128:(k + 1) * 128, :], [], [Bw[k]])
    for k in range(8):
        P.dma("pool", wrp[:, k, :], io["w_rope"][k * 128:(k + 1) * 128, :], [], [Bwr[k]])
    ps, Bps = pscol, Bpscol
    modT = P.sbuf("modT", [128, 16, 2], F32); BmodT = P.buf("modT")
    cp(P, "dve", modT[:].rearrange("p a b -> p (a b)"), ps[:, 0:32], [Bps], [BmodT])
    Acol = P.sbuf("Acol", [128, 8, 2], F32); BA = P.buf("Acol")
    tsc(P, "dve", Acol[:].rearrange("p a b -> p (a b)"), modT[:, 8:16, :].rearrange("p a b -> p (a b)"), 1.0, None,
        ALU.add, None, [BmodT], [BA])
    for j in range(2):
        P.op("dve", lambda e, j=j: e.tensor_tensor(Acol[:, :, j], Acol[:, :, j], ng[:, :], ALU.mult), [BA, Bng], [BA])

    if io.get("_stop") == 1:
        P.build(); return
    hT = P.sbuf("hT", [128, 8, TT], BF16); BhT = P.bufs_n("hT", TT // 128)
    xrot = Rot(P, "sb", "xt", [128, 1024], F32, 2)
    junk = Rot(P, "sb", "junk", [128, 1024], BF16, 1)
    xnrot = Rot(P, "sb", "xn", [128, 1024], BF16, 2)
    strot = Rot(P, "sb", "st", [128, 4], F32, 3)
    ptr = Rot(P, "ps", "ptr", [128, 8, 128], BF16, 2)
    for i in range(io.get("_ntiles", TT // 128)):
        j = 0 if i < NTB else 1
        src = io["x"][i * 128:(i + 1) * 128, :] if i < NTB else io["xc"][(i - NTB) * 128:(i - NTB + 1) * 128, :]
        xt, Bxt = xrot.next()
        P.dma("sp", xt[:], src, [], [Bxt])
        jk, Bjk = junk.next()
        st, Bst = strot.next()
        act(P, jk[:], xt[:], AF.Square, [Bxt], [Bjk, Bst], accum_out=st[:, 0:1])
        act(P, st[:, 1:2], st[:, 0:1], AF.Sqrt, [Bst], [Bst], scale=1.0 / D, bias=EPS)
        recip(P, st[:, 2:3], st[:, 1:2], [Bst], [Bst])
        xn, Bxn = xnrot.next()
        tsc(P, "dve", xn[:], xt[:], st[:, 2:3], None, ALU.mult, None, [Bxt, Bst], [Bxn])
        pt, Bpt = ptr.next()
        for k in range(8):
            tr(P, pt[:, k, :], xn[:, k * 128:(k + 1) * 128], identb[:], [Bxn, Bidb], [Bpt])
        for k in range(8):
            dst = hT[:, k, i * 128:(i + 1) * 128]
            if i % 2 == 0:
                P.op("dve", lambda e, dst=dst, pt=pt, k=k, j=j: e.tensor_scalar(
                    dst, pt[:, k, :], Acol[:, k, j:j + 1], modT[:, k, j:j + 1], ALU.mult, ALU.add),
                    [Bpt, BA, BmodT], [BhT[i]], join=(k > 0))
            else:
                P.op("act", lambda e, dst=dst, pt=pt, k=k, j=j: e.activation(
                    dst, pt[:, k, :], AF.Identity, bias=modT[:, k, j:j + 1], scale=Acol[:, k, j:j + 1]),
                    [Bpt, BA, BmodT], [BhT[i]], join=(k > 0))

    if io.get("_stop") == 2:
        P.build(); return
    ttiles = [(0, 512), (512, 512), (1024, 512), (1536, 512), (2048, 256)]

    def hbufs(t0, n):
        return BhT[t0 // 128:(t0 + n) // 128]

    ostage = Rot(P, "sb", "ost", [128, TT], BF16, 2)
    t1rot = Rot(P, "sb", "t1", [128, 512], F32, 1)
    t2rot = Rot(P, "sb", "t2", [128, 512], F32, 1)
    dcount = [0]

    def dq():
        dcount[0] += 1
        return "sp" if dcount[0] % 2 else "pool"


    def fm_job(c0, M, dst_ap, w_t, Bw_l, rhs_fn, nk, rope=None, scale_bc=None, post=None):
        og, Bog = ostage.next()
        first = True
        for (t0, n) in ttiles:
            ps, Bps = psA.next()
            for k in range(nk):
                rhs, rb = rhs_fn(k, t0, n)
                mm(P, ps[0:M, 0:n], w_t[:, k, c0:c0 + M], rhs, k == 0, k == nk - 1, Bw_l + rb, [Bps])
            if rope is not None:
                wr_t, Bwr_l, rc0 = rope
                ps2, Bps2 = psB.next()
                for k in range(nk):
                    rhs, rb = rhs_fn(k, t0, n)
                    mm(P, ps2[0:M, 0:n], wr_t[:, k, rc0:rc0 + M], rhs, k == 0, k == nk - 1, Bwr_l + rb, [Bps2])
                t1, Bt1 = t1rot.next()
                t2, Bt2 = t2rot.next()
                tt(P, "dve", t1[0:M, 0:n], ps[0:M, 0:n], cosT[0:M, t0:t0 + n], ALU.mult, [Bps, Bcos], [Bt1])
                tt(P, "dve", t2[0:M, 0:n], ps2[0:M, 0:n], sinT[0:M, t0:t0 + n], ALU.mult, [Bps2, Bsin], [Bt2])
                if scale_bc is None:
                    P.op("pool", lambda e, og=og, t1=t1, t2=t2, t0=t0, n=n: e.tensor_tensor(
                        og[0:M, t0:t0 + n], t1[0:M, 0:n], t2[0:M, 0:n], ALU.add), [Bt1, Bt2], [Bog], join=not first)
                else:
                    sbt, Bsb = scale_bc
                    tt(P, "pool", t1[0:M, 0:n], t1[0:M, 0:n], t2[0:M, 0:n], ALU.add, [Bt1, Bt2], [Bt1])
                    P.op("pool", lambda e, og=og, t1=t1, t0=t0, n=n, sbt=sbt: e.tensor_tensor(
                        og[0:M, t0:t0 + n], t1[0:M, 0:n], sbt[0:M, t0:t0 + n], ALU.mult), [Bt1, Bsb], [Bog],
                        join=not first)
            elif post is not None:
                post(ps, Bps, og, Bog, t0, n, first)
            elif scale_bc is not None:
                sbt, Bsb = scale_bc
                P.op("dve", lambda e, og=og, ps=ps, t0=t0, n=n, sbt=sbt: e.tensor_tensor(
                    og[0:M, t0:t0 + n], ps[0:M, 0:n], sbt[0:M, t0:t0 + n], ALU.mult), [Bps, Bsb], [Bog],
                    join=not first)
            else:
                P.op("act", lambda e, og=og, ps=ps, t0=t0, n=n: e.copy(og[0:M, t0:t0 + n], ps[0:M, 0:n]),
                     [Bps], [Bog], join=not first)
            first = False
        if dst_ap is not None:
            P.dma(dq(), dst_ap, og[0:M, :], [Bog], [Bfmout], join=True)
        return og, Bog

    Bfmout = P.buf("fmout"); Bvout = P.buf("vout"); Bgout = P.buf("gout")

    def h_rhs(k, t0, n):
        return hT[:, k, t0:t0 + n], hbufs(t0, n)

    for (c0, M, ridx, dname, drow) in lay.fm:
        og, Bog = fm_job(c0, M, io[dname][drow:drow + M, :], wbf, Bw, h_rhs, 8,
                         rope=None if ridx is None else (wrp, Bwr, ridx * 128))
        hp = lay.halo.get((dname, drow))
        if hp is not None:
            hrow, hw, hrows = hp
            P.dma(dq(), io["hk"][hrow:hrow + M, :], og[0:M, 0:hw], [Bog], [Bfmout], join=True)
            P.dma(dq(), io["hk"][hrows + hrow:hrows + hrow + M, :], og[0:M, T - hw:T], [Bog], [Bfmout], join=True)

    if io.get("_stop") == 3:
        P.build(); return
    vst = Rot(P, "sb", "vst", [128, lay.VCOLS], BF16, 2)
    gst = Rot(P, "sb", "gst", [128, 1024], F32, 2)
    for i in range(2):
        mset(P, "pool", vst.t[i][:], 1.0, [vst.b[i]])

    def tm_mm(i, c0, ncols):
        ps, Bps = psA.next()
        for k in range(8):
            mm(P, ps[:, 0:ncols], hT[:, k, i * 128:(i + 1) * 128], wbf[:, k, c0:c0 + ncols], k == 0, k == 7,
               [BhT[i]] + Bw, [Bps])
        return ps, Bps

    if lay.idx == 1:
        gkvb = P.sbuf("gkvb", [128, 128], F32); Bgkvb = P.buf("gkvb")
        P.dma("sp", gkvb[:], bcast_rows(io["kvng_row"], 128), [], [Bgkvb])
        ckT = P.sbuf("ckT", [128, TT], BF16); BckT = P.buf("ckT")
        st2 = Rot(P, "sb", "st2", [128, 4], F32, 3)
        jk2 = Rot(P, "sb", "jk2", [128, 128], F32, 2)
        ptc = ptr

    for i in range(TT // 128):
        vt, Bvt = vst.next()
        wfirst = True
        for (c0, ncols, nh, e, dcol) in lay.tmv:
            ps, Bps = tm_mm(i, c0, ncols)
            dstv = vt[:, dcol:dcol + nh * (e + 1)].rearrange("p (h e) -> p h e", e=e + 1)[:, :, 0:e]
            srcv = ps[:, 0:ncols].rearrange("p (h e) -> p h e", e=e)
            P.op("act", lambda en, dstv=dstv, srcv=srcv: en.copy(dstv, srcv), [Bps], [Bvt], join=not wfirst)
            wfirst = False
        if lay.idx == 1:
            ps, Bps = tm_mm(i, 256, 128)
            s2, Bs2 = st2.next()
            j2, Bj2 = jk2.next()
            act(P, j2[:], ps[:, 0:128], AF.Square, [Bps], [Bj2, Bs2], accum_out=s2[:, 0:1])
            act(P, s2[:, 1:2], s2[:, 0:1], AF.Sqrt, [Bs2], [Bs2], scale=1.0 / 128, bias=EPS)
            recip(P, s2[:, 2:3], s2[:, 1:2], [Bs2], [Bs2])
            P.op("dve", lambda en, vt=vt, ps=ps, s2=s2: en.scalar_tensor_tensor(
                vt[:, 0:128], ps[:, 0:128], s2[:, 2:3], gkvb[:], ALU.mult, ALU.mult), [Bps, Bs2, Bgkvb], [Bvt], join=True)
            pc, Bpc = ptc.next()
            tr(P, pc[:, 0, :], vt[:, 0:128], identb[:], [Bvt, Bidb], [Bpc])
            P.op("act", lambda en, pc=pc, i=i: en.copy(ckT[:, i * 128:(i + 1) * 128], pc[:, 0, :]), [Bpc], [BckT], join=True)
        P.dma(dq(), io["v"][i * 128:(i + 1) * 128, :], vt[:], [Bvt], [Bvout], join=True)
        gt, Bgt = gst.next()
        for hh in range(2):
            ps, Bps = tm_mm(i, lay.gcol + 512 * hh, 512)
            P.op("act", lambda en, gt=gt, ps=ps, hh=hh: en.activation(gt[:, hh * 512:(hh + 1) * 512], ps[:], AF.Silu),
                 [Bps], [Bgt], join=(hh == 1))
        P.dma(dq(), io["sg"][i * 128:(i + 1) * 128, :], gt[:], [Bgt], [Bgout], join=True)

    if lay.idx == 1:
        P.dma(dq(), io["kt"][0:128, :], ckT[:], [BckT], [Bfmout], join=True)
        cqT = P.sbuf("cqT", [128, 2, TT], BF16); BcqT = P.buf("cqT")
        sqr = Rot(P, "sb", "sqr", [128, 2, 512], F32, 1)
        rsq = P.sbuf("rsq", [128, TT], F32); Brsq = P.buf("rsq")
        onesf = P.sbuf("onesf", [128, 128], F32); Bones = P.buf("onesf")
        mset(P, "pool", onesf[:], 1.0, [Bones])
        first = True
        for (t0, n) in ttiles:
            sq, Bsq = sqr.next()
            for kk in range(2):
                ps, Bps = psA.next()
                for k in range(8):
                    mm(P, ps[:, 0:n], wbf[:, k, kk * 128:(kk + 1) * 128], hT[:, k, t0:t0 + n], k == 0, k == 7,
                       Bw + hbufs(t0, n), [Bps])
                P.op("act", lambda e, ps=ps, kk=kk, t0=t0, n=n: e.copy(cqT[:, kk, t0:t0 + n], ps[:, 0:n]),
                     [Bps], [BcqT], join=not (first and kk == 0))
                P.op("dve", lambda e, ps=ps, sq=sq, kk=kk, n=n: e.tensor_copy(sq[:, kk, 0:n], ps[:, 0:n]),
                     [Bps], [Bsq], join=(kk == 1))
            P.op("pool", lambda e, sq=sq, n=n: e.tensor_tensor(sq[:, :, 0:n], sq[:, :, 0:n], sq[:, :, 0:n], ALU.mult),
                 [Bsq], [Bsq])
            ps, Bps = psB.next()
            for kk in range(2):
                mm(P, ps[:, 0:n], onesf[:], sq[:, kk, 0:n], kk == 0, kk == 1, [Bones, Bsq], [Bps])
            P.op("act", lambda e, ps=ps, t0=t0, n=n: e.activation(rsq[:, t0:t0 + n], ps[:, 0:n], AF.Sqrt,
                                                                scale=1.0 / 256, bias=EPS), [Bps], [Brsq], join=not first)
            first = False
        recip(P, rsq[:], rsq[:], [Brsq], [Brsq])
        wqf = P.sbuf("wqf", [128, 1024], F32); Bwqf = P.buf("wqf")
        qng = P.sbuf("qng", [128, 2], F32); Bqng = P.buf("qng")
        P.dma("sp", qng[:], io["qng_col"], [], [Bqng])
        wqb = P.sbuf("wqb", [128, 2, 1024], BF16); Bwqb = P.buf("wqb")
        for kk in range(2):
            P.dma("sp", wqf[:], io["w_qb_all"][kk * 128:(kk + 1) * 128, :], [], [Bwqf])
            P.op("dve", lambda e, kk=kk: e.tensor_scalar(wqb[:, kk, :], wqf[:], qng[:, kk:kk + 1], None, ALU.mult),
                 [Bwqf, Bqng], [Bwqb], join=(kk == 1))
        wkT = P.sbuf("wkT", [128, 4, 128], BF16); BwkT = P.buf("wkT")
        P.dma("pool", wkT[:], io["wkT"], [], [BwkT])

        def cq_rhs(k, t0, n):
            return cqT[:, k, t0:t0 + n], [BcqT]

        qn_rot = Rot(P, "sb", "qn", [128, 512], BF16, 2)
        for h in range(4):
            def post(ps, Bps, og, Bog, t0, n, first, h=h):
                qn, Bqn = qn_rot.next()
                tt(P, "dve", qn[:, 0:n], ps[:, 0:n], rsq[:, t0:t0 + n], ALU.mult, [Bps, Brsq], [Bqn])
                ps3, Bps3 = psB.next()
                mm(P, ps3[:, 0:n], wkT[:, h, :], qn[:, 0:n], True, True, [BwkT, Bqn], [Bps3])
                P.op("act", lambda e, og=og, ps3=ps3, t0=t0, n=n: e.copy(og[:, t0:t0 + n], ps3[:, 0:n]),
                     [Bps3], [Bog], join=not first)
            fm_job(192 * h, 128, io["q"][128 * h:128 * h + 128, :], wqb, [Bwqb], cq_rhs, 2, post=post)
            fm_job(192 * h + 128, 64, io["q"][512 + 64 * h:512 + 64 * h + 64, :], wqb, [Bwqb], cq_rhs, 2,
                   rope=(wqb, [Bwqb], 768 + 64 * h), scale_bc=(rsq, Brsq))
    P.build()


def load_common_B(P, io):
    identb = P.sbuf("identb", [128, 128], BF16); Bidb = P.buf("identb")
    P.dma("pool", identb[:], io["ident"], [], [Bidb])
    return identb, Bidb


def phase_BA0(nc, io, tag):
    P = Prog(nc, tag)
    scale = 64 ** -0.5
    qaT = P.sbuf("qaT", [64, 8, TT], BF16); Bqa = P.buf("qaT")
    P.dma("sp", qaT[:], io["q"][0:512, :].rearrange("(h d) t -> d h t", d=64), [], [Bqa])
    kaT = P.sbuf("kaT", [64, 2, TT], BF16); Bka = P.buf("kaT")
    P.dma("pool", kaT[:], io["kt"][0:128, :].rearrange("(j d) t -> d j t", d=64), [], [Bka])
    vaX = P.sbuf("vaX", [128, 18, 130], BF16); Bva = P.buf("vaX")
    P.dma("sp", vaX[:], io["v"][:, 0:130].rearrange("(c p) e -> p c e", p=128), [], [Bva])
    hidx = P.sbuf("hidx", [128, NIDX], I32); Bhidx = P.buf("hidx")
    P.dma("sp", hidx[:], io["hidx"], [], [Bhidx])
    kaH = P.sbuf("kaH", [64, 2, 2, 128], BF16); BkaH = P.buf("kaH")
    vaH = P.sbuf("vaH", [128, 2, 646], BF16); BvaH = P.buf("vaH")
    for side in range(2):
        for j in range(2):
            P.gather(kaH[:, j, side, :], io["hk_all"], hidx[0:64, 2 * side + j:2 * side + j + 1], [Bhidx], [BkaH],
                     join=not (side == 0 and j == 0))
        P.gather(vaH[:, side, :], io["v_all"], hidx[:, 4 + side:5 + side], [Bhidx], [BvaH], join=(side > 0))
    mk = P.sbuf("mk", [128, 4, 512], BF16); Bmk = P.buf("mk")
    P.dma("pool", mk[:], io["maskA"].rearrange("m p f -> p m f"), [], [Bmk])
    sk = P.sbuf("sk", [128, 8], F32); Bsk = P.buf("sk")
    P.dma("sp", sk[:], bcast_rows(io["a_sink"], 128), [], [Bsk])
    esk = P.sbuf("esk", [128, 8], F32); Besk = P.buf("esk")
    act(P, esk[:], sk[:], AF.Exp, [Bsk], [Besk])
    psS = Rot(P, "ps", "psS", [128, 512], F32, 2)
    accr = Rot(P, "ps", "acc", [128, 512], F32, 2)
    ptr_ = Rot(P, "sb", "pT", [128, 512], BF16, 3)
    ytr = Rot(P, "sb", "yt", [128, 512], F32, 2)
    ztr = Rot(P, "sb", "zt", [128, 8], F32, 2)
    By = P.bufs_n("y", 5)
    for n in range(TT // 128):
        own = n < NTB
        q0 = n * 128
        yt, Byt = ytr.next()
        for j in range(2):
            def kown(c, j=j):
                return (kaT[:, j, c * 128:(c + 1) * 128], vaX[:, c, 65 * j:65 * j + 65], [Bka, Bva])

            def khalo(side, j=j):
                return (kaH[:, j, side, :], vaH[:, side, 65 * j:65 * j + 65], [BkaH, BvaH])
            if own:
                chunks = [(khalo(0) if n == 0 else kown(n - 1), 0 if n == 0 else 1), (kown(n), None),
                          (khalo(1) if n == NTB - 1 else kown(n + 1), 3 if n == NTB - 1 else 2),
                          (kown(16), None), (kown(17), None)]
            else:
                chunks = [(kown(16), None), (kown(17), None)]
            acc_, Bacc = accr.next()
            acc = acc_[:, 0:260].rearrange("p (g e) -> p g e", g=4)
            for ci, ((kap, vap, kvb), m) in enumerate(chunks):
                ps, Bps = psS.next()
                mm(P, ps[:, :].rearrange("p (g q) -> p g q", g=4), kap,
                   qaT[:, 4 * j:4 * j + 4, q0:q0 + 128], True, True, kvb + [Bqa], [Bps])
                pT, BpT = ptr_.next()
                act(P, pT[:], ps[:], AF.Exp, [Bps], [BpT], scale=scale)
                if m is not None:
                    tt(P, "pool", pT[:], pT[:], mk[:, m, :], ALU.mult, [BpT, Bmk], [BpT])
                for g in range(4):
                    mm(P, acc[:, g, :], pT[:, g * 128:(g + 1) * 128], vap,
                       ci == 0 and g == 0, ci == len(chunks) - 1, [BpT] + kvb, [Bacc])
            zt, Bzt = ztr.next()
            tt(P, "dve", zt[:, 0:4], acc[:, :, 64], esk[:, 4 * j:4 * j + 4], ALU.add, [Bacc, Besk], [Bzt])
            recip(P, zt[:, 4:8], zt[:, 0:4], [Bzt], [Bzt])
            for g in range(4):
                hd = 4 * j + g
                P.op("dve", lambda e, yt=yt, acc=acc, zt=zt, g=g, hd=hd: e.tensor_scalar(
                    yt[:, hd * 64:(hd + 1) * 64], acc[:, g, 0:64], zt[:, 4 + g:5 + g], None, ALU.mult),
                    [Bacc, Bzt], [Byt], join=not (j == 0 and g == 0))
        P.dma("sp", io["y"][q0:q0 + 128, 0:512], yt[:], [Byt], [By[n // 4]], join=True)
    P.build()


def full_attn_pass(P, qk_list, nq, chunks, vaug_fn, vbufs, scale, psS, accs, ptr_, E1):
    nb = nq // 128
    a0, Ba0, a1, Ba1 = accs
    nchk = len(chunks)

    def pv(ci, ch, pT, BpT):
        for b in range(nb):
            at, Bat = (a0, Ba0) if b < 2 else (a1, Ba1)
            mm(P, at[:, b % 2, :], pT[:, b * 128:(b + 1) * 128], vaug_fn(ch), ci == 0 and b % 2 == 0, ci == nchk - 1,
               [BpT] + vbufs, [Bat])

    skew = max(1, psS.n - 1)
    pend = []
    for ci, ch in enumerate(chunks):
        ps, Bps = psS.next()
        for qi, (kT_fn, qT, bl) in enumerate(qk_list):
            mm(P, ps[:, 0:nq], kT_fn(ch), qT, qi == 0, qi == len(qk_list) - 1, bl, [Bps])
        pT, BpT = ptr_.next()
        act(P, pT[:, 0:nq], ps[:, 0:nq], AF.Exp, [Bps], [BpT], scale=scale)
        pend.append((ci, ch, pT, BpT))
        if len(pend) > skew:
            pv(*pend.pop(0))
    while pend:
        pv(*pend.pop(0))


def phase_BB0(nc, io, tag):
    P = Prog(nc, tag)
    scale = 64 ** -0.5
    lam_init = 0.8 - 0.6 * math.exp(-0.3 * 0)
    NCH = S // 128 + 2
    qbT = P.sbuf("qbT", [128, 4, TT], BF16); Bqb = P.buf("qbT")
    P.dma("sp", qbT[:], io["q"][512:1024, :].rearrange("(h r) t -> r h t", r=128), [], [Bqb])
    lb = P.sbuf("lb", [128, 256], F32); Blb = P.buf("lb")
    P.dma("sp", lb[:], bcast_rows(io["b_lambda"], 128), [], [Blb])
    lt = P.sbuf("lt", [128, 128], F32); Blt = P.buf("lt")
    ls = P.sbuf("ls", [128, 8], F32); Bls = P.buf("ls")
    tt(P, "dve", lt[:].rearrange("p (a b) -> p a b", a=2), lb[:].rearrange("p (a b c) -> p a b c", a=2, b=2)[:, :, 0, :],
       lb[:].rearrange("p (a b c) -> p a b c", a=2, b=2)[:, :, 1, :], ALU.mult, [Blb], [Blt])
    P.op("dve", lambda e: e.reduce_sum(ls[:, 0:2], lt[:].rearrange("p (a b) -> p a b", a=2), AX.X), [Blt], [Bls])
    act(P, ls[:, 2:4], ls[:, 0:2], AF.Exp, [Bls], [Bls])
    tt(P, "dve", ls[:, 4:5], ls[:, 3:4], ls[:, 2:3], ALU.subtract, [Bls], [Bls])
    tsc(P, "dve", ls[:, 5:6], ls[:, 4:5], -lam_init, None, ALU.add, None, [Bls], [Bls])
    sgl = P.sbuf("sgl", [128, 128], F32); Bsgl = P.buf("sgl")
    P.dma("sp", sgl[:], bcast_rows(io["subln_g"], 128), [], [Bsgl])
    tsc(P, "dve", sgl[:], sgl[:], 1.0 - lam_init, None, ALU.mult, None, [Bsgl], [Bsgl])

    kbr = Rot(P, "sb", "kb", [128, NCH * 128], BF16, 2)
    vbr = Rot(P, "sb", "vb", [128, NCH, 129], BF16, 2)
    psS = Rot(P, "ps", "psS", [128, 512], F32, 3)
    acc_f = [P.psum(f"acc{i}", [128, 512], F32) for i in range(4)]
    acc_t = [a[:, 0:258].rearrange("p (b e) -> p b e", b=2) for a in acc_f]
    acc_b = [P.buf(f"acc{i}", excl=True) for i in range(4)]
    ptr_ = Rot(P, "sb", "pT", [128, 512], BF16, 4)
    Or = [P.sbuf(f"O{t}", [128, 4, 129], F32) for t in range(2)]
    BO = [P.buf(f"O{t}") for t in range(2)]
    ur = Rot(P, "sb", "u", [128, 128], F32, 2)
    jr = Rot(P, "sb", "jk", [128, 128], F32, 2)
    sr = Rot(P, "sb", "s", [128, 8], F32, 3)
    ybr = Rot(P, "sb", "yb", [128, 4, 128], F32, 2)
    By = P.bufs_n("y", 5)
    qtiles = [(0, 512), (512, 512), (1024, 512), (1536, 512), (2048, 256)]
    for h in range(4):
        kb, Bkb = kbr.next()
        vb, Bvb = vbr.next()
        for r in range(NCORES):
            P.dma("sp" if r % 2 else "pool", kb[:, r * T:(r + 1) * T], io["kt_all"][r, 128 + 128 * h:256 + 128 * h, 0:T],
                  [], [Bkb], join=(r > 0))
            P.dma("pool" if r % 2 else "sp", vb[:, r * 16:(r + 1) * 16, :],
                  io["v_all"][r, 0:T, 130 + 129 * h:259 + 129 * h].rearrange("(c p) e -> p c e", p=128),
                  [], [Bvb], join=(r > 0))
        P.dma("sp", kb[:, S:S + L], io["kt_all"][0, 128 + 128 * h:256 + 128 * h, T:TT], [], [Bkb], join=True)
        P.dma("pool", vb[:, 128:130, :], io["v_all"][0, T:TT, 130 + 129 * h:259 + 129 * h].rearrange("(c p) e -> p c e", p=128),
              [], [Bvb], join=True)
        for qi_, (q0, nq) in enumerate(qtiles):
            nb = nq // 128
            chunks = list(range(NCH)) if q0 < T else [128, 129]
            for t in range(2):
                accs = (acc_t[2 * t], acc_b[2 * t], acc_t[2 * t + 1], acc_b[2 * t + 1])
                full_attn_pass(P, [(lambda ch, t=t, kb=kb: kb[64 * t:64 * t + 64, ch * 128:(ch + 1) * 128],
                                    qbT[64 * t:64 * t + 64, h, q0:q0 + nq], [Bkb, Bqb])],
                               nq, chunks, lambda ch, vb=vb: vb[:, ch, :], [Bvb], scale, psS, accs, ptr_, 129)
                P.op("act", lambda e, t=t: e.copy(Or[t][:, 0:2, :], acc_t[2 * t]), [acc_b[2 * t]], [BO[t]])
                if nb > 2:
                    P.op("act", lambda e, t=t: e.copy(Or[t][:, 2:4, :], acc_t[2 * t + 1]), [acc_b[2 * t + 1]], [BO[t]], join=True)
            yb, Byb = ybr.next()
            for b in range(nb):
                s_, Bs = sr.next()
                P.op("dve", lambda e, s_=s_, b=b: e.reciprocal(s_[:, 0:1], Or[0][:, b, 128:129]), [BO[0]], [Bs])
                P.op("dve", lambda e, s_=s_, b=b: e.reciprocal(s_[:, 1:2], Or[1][:, b, 128:129]), [BO[1]], [Bs])
                tt(P, "dve", s_[:, 2:3], s_[:, 1:2], ls[:, 5:6], ALU.mult, [Bs, Bls], [Bs])
                u, Bu = ur.next()
                P.op("dve", lambda e, u=u, s_=s_, b=b: e.tensor_scalar(u[:], Or[0][:, b, 0:128], s_[:, 0:1], None, ALU.mult),
                     [BO[0], Bs], [Bu])
                P.op("dve", lambda e, u=u, s_=s_, b=b: e.scalar_tensor_tensor(u[:], Or[1][:, b, 0:128], s_[:, 2:3], u[:],
                                                                               ALU.mult, ALU.add), [BO[1], Bs, Bu], [Bu])
                jk, Bjk = jr.next()
                act(P, jk[:], u[:], AF.Square, [Bu], [Bjk, Bs], accum_out=s_[:, 3:4])
                act(P, s_[:, 4:5], s_[:, 3:4], AF.Sqrt, [Bs], [Bs], scale=1.0 / 128, bias=EPS)
                recip(P, s_[:, 5:6], s_[:, 4:5], [Bs], [Bs])
                P.op("dve", lambda e, yb=yb, u=u, s_=s_, b=b: e.scalar_tensor_tensor(
                    yb[:, b, :], u[:], s_[:, 5:6], sgl[:], ALU.mult, ALU.mult), [Bu, Bs, Bsgl], [Byb], join=(b > 0))
            P.dma("sp", io["y"][q0:q0 + nq, 512 + 128 * h:640 + 128 * h].rearrange("(b p) c -> p b c", p=128),
                  yb[:, 0:nb, :], [Byb], [By[qi_]], join=True)
    P.build()


def phase_BO(nc, io, tag, last):
    P = Prog(nc, tag)
    identb, Bidb = load_common_B(P, io)
    wo = P.sbuf("wo", [128, 8, 1024], BF16); Bwo = P.buf("wo")
    P.dma("pool", wo[:], io["w_out"].rearrange("(k p) c -> p k c", p=128), [], [Bwo])
    gate = P.sbuf("gate", [128, 2, 1024], F32); Bgate = P.buf("gate")
    for j in range(2):
        P.dma("sp", gate[:, j, :], bcast_rows(io["mod"][j:j + 1, 2048:3072], 128), [], [Bgate], join=(j > 0))
    if last:
        fng = P.sbuf("fng", [128, 1024], F32); Bfng = P.buf("fng")
        P.dma("sp", fng[:], bcast_rows(io["final_g"], 128), [], [Bfng])
    yr = Rot(P, "sb", "yt", [128, 1024], F32, 2)
    gr = Rot(P, "sb", "gt", [128, 1024], F32, 2)
    xr = Rot(P, "sb", "xt", [128, 1024], F32, 2)
    ygr = Rot(P, "sb", "yg", [128, 1024], BF16, 2)
    ptr_ = Rot(P, "ps", "pt", [128, 8, 128], BF16, 2)
    ygTr = Rot(P, "sb", "ygT", [128, 8, 128], BF16, 2)
    pso = Rot(P, "ps", "pso", [128, 512], F32, 3)
    tmr = Rot(P, "sb", "tm", [128, 1024], F32, 2)
    x1r = Rot(P, "sb", "x1", [128, 1024], F32, 2)
    sr = Rot(P, "sb", "s", [128, 4], F32, 3)
    jr = Rot(P, "sb", "jk", [128, 1024], BF16, 2)
    Bout = P.buf("outd")
    nblk = NTB if last else TT // 128
    for i in range(nblk):
        own = i < NTB
        j = 0 if own else 1
        r0 = i * 128
        yt, Byt = yr.next(); gt, Bgt = gr.next(); xt, Bxt = xr.next()
        P.dma("sp", yt[:], io["y"][r0:r0 + 128, :], [], [Byt])
        P.dma("pool", gt[:], io["sg"][r0:r0 + 128, :], [], [Bgt])
        P.dma("sp", xt[:], io["x"][r0:r0 + 128, :] if own else io["xc"][r0 - T:r0 - T + 128, :], [], [Bxt])
        yg, Byg = ygr.next()
        tt(P, "pool", yg[:], yt[:], gt[:], ALU.mult, [Byt, Bgt], [Byg])
        pt, Bpt = ptr_.next()
        for k in range(8):
            tr(P, pt[:, k, :], yg[:, k * 128:(k + 1) * 128], identb[:], [Byg, Bidb], [Bpt])
        ygT, BygT = ygTr.next()
        if i % 2 == 0:
            P.op("act", lambda e, ygT=ygT, pt=pt: e.copy(ygT[:], pt[:]), [Bpt], [BygT])
        else:
            P.op("dve", lambda e, ygT=ygT, pt=pt: e.tensor_copy(ygT[:], pt[:]), [Bpt], [BygT])
        tm, Btm = tmr.next()
        x1, Bx1 = x1r.next()
        for ct in range(2):
            ps, Bps = pso.next()
            for k in range(8):
                mm(P, ps[:], ygT[:, k, :], wo[:, k, ct * 512:(ct + 1) * 512], k == 0, k == 7, [BygT, Bwo], [Bps])
            P.op("dve", lambda e, tm=tm, ps=ps, ct=ct, j=j: e.tensor_tensor(
                tm[:, ct * 512:(ct + 1) * 512], ps[:], gate[:, j, ct * 512:(ct + 1) * 512], ALU.mult),
                [Bps, Bgate], [Btm], join=(ct == 1))
        tt(P, "pool", x1[:], xt[:], tm[:], ALU.add, [Bxt, Btm], [Bx1])
        if last:
            s_, Bs = sr.next()
            jk, Bjk = jr.next()
            act(P, jk[:], x1[:], AF.Square, [Bx1], [Bjk, Bs], accum_out=s_[:, 0:1])
            act(P, s_[:, 1:2], s_[:, 0:1], AF.Sqrt, [Bs], [Bs], scale=1.0 / D, bias=EPS)
            recip(P, s_[:, 2:3], s_[:, 1:2], [Bs], [Bs])
            P.op("dve", lambda e, x1=x1, s_=s_, tm=tm: e.scalar_tensor_tensor(
                tm[:], x1[:], s_[:, 2:3], fng[:], ALU.mult, ALU.mult), [Bx1, Bs, Bfng], [Btm])
            P.dma("sp", io["out"][r0:r0 + 128, :], tm[:], [Btm], [Bout], join=True)
        else:
            dst = io["x1"][r0:r0 + 128, :] if own else io["xc1"][r0 - T:r0 - T + 128, :]
            P.dma("sp", dst, x1[:], [Bx1], [Bout], join=True)
    P.build()


def phase_BC1(nc, io, tag):
    P = Prog(nc, tag)
    scale = 192 ** -0.5
    NCH = S // 128 + 2
    identb, Bidb = load_common_B(P, io)
    qlT = P.sbuf("qlT", [128, 4, T], BF16); Bql = P.buf("qlT")
    P.dma("sp", qlT[:], io["q"][0:512, 0:T].rearrange("(h r) t -> r h t", r=128), [], [Bql])
    qpT = P.sbuf("qpT", [64, 4, T], BF16); Bqp = P.buf("qpT")
    P.dma("pool", qpT[:], io["q"][512:768, 0:T].rearrange("(h r) t -> r h t", r=64), [], [Bqp])
    wv = P.sbuf("wv", [128, 4, 128], BF16); Bwv = P.buf("wv")
    P.dma("pool", wv[:], io["wv"], [], [Bwv])
    kl = P.sbuf("kl", [128, NCH * 128], BF16); Bkl = P.buf("kl")
    kp = P.sbuf("kp", [64, NCH * 128], BF16); Bkp = P.buf("kp")
    vl = P.sbuf("vl", [128, NCH, 129], BF16); Bvl = P.buf("vl")
    for r in range(NCORES):
        P.dma("sp" if r % 2 else "pool", kl[:, r * T:(r + 1) * T], io["kt_all"][r, 0:128, 0:T], [], [Bkl], join=(r > 0))
        P.dma("pool" if r % 2 else "sp", kp[:, r * T:(r + 1) * T], io["kt_all"][r, 128:192, 0:T], [], [Bkp], join=(r > 0))
        P.dma("sp", vl[:, r * 16:(r + 1) * 16, :], io["v_all"][r, 0:T, 0:129].rearrange("(c p) e -> p c e", p=128),
              [], [Bvl], join=(r > 0))
    P.dma("sp", kl[:, S:S + L], io["kt_all"][0, 0:128, T:TT], [], [Bkl], join=True)
    P.dma("sp", kp[:, S:S + L], io["kt_all"][0, 128:192, T:TT], [], [Bkp], join=True)
    P.dma("pool", vl[:, 128:130, :], io["v_all"][0, T:TT, 0:129].rearrange("(c p) e -> p c e", p=128), [], [Bvl], join=True)
    psS = Rot(P, "ps", "psS", [128, 512], F32, 2)
    acc_f = [P.psum(f"acc{i}", [128, 512], F32) for i in range(4)]
    acc_t = [a[:, 0:258].rearrange("p (b e) -> p b e", b=2) for a in acc_f]
    acc_b = [P.buf(f"acc{i}", excl=True) for i in range(4)]
    ptr_ = Rot(P, "sb", "pT", [128, 512], BF16, 3)
    pst = Rot(P, "ps", "pst", [128, 8, 128], BF16, 1)
    pso = Rot(P, "ps", "pso", [128, 512], F32, 1)
    Or = Rot(P, "sb", "O", [128, 4, 129], F32, 2)
    sr = Rot(P, "sb", "s", [128, 4], F32, 3)
    ur = Rot(P, "sb", "u", [128, 128], BF16, 2)
    uTr = Rot(P, "sb", "uT", [128, 128], BF16, 2)
    ybr = Rot(P, "sb", "yb", [128, 4, 128], F32, 2)
    By = P.bufs_n("y", 5)
    par = 0
    for h in range(4):
        for qi_ in range(4):
            q0 = qi_ * 512
            accs = (acc_t[2 * par], acc_b[2 * par], acc_t[2 * par + 1], acc_b[2 * par + 1])
            full_attn_pass(P, [(lambda ch: kl[:, ch * 128:(ch + 1) * 128], qlT[:, h, q0:q0 + 512], [Bkl, Bql]),
                               (lambda ch: kp[:, ch * 128:(ch + 1) * 128], qpT[:, h, q0:q0 + 512], [Bkp, Bqp])],
                           512, list(range(NCH)), lambda ch: vl[:, ch, :], [Bvl], scale, psS, accs, ptr_, 129)
            O, BO = Or.next()
            P.op("act", lambda e, O=O, par=par: e.copy(O[:, 0:2, :], acc_t[2 * par]), [acc_b[2 * par]], [BO])
            P.op("act", lambda e, O=O, par=par: e.copy(O[:, 2:4, :], acc_t[2 * par + 1]), [acc_b[2 * par + 1]], [BO], join=True)
            par ^= 1
            yb, Byb = ybr.next()
            for b in range(4):
                s_, Bs = sr.next()
                P.op("dve", lambda e, s_=s_, O=O, b=b: e.reciprocal(s_[:, 0:1], O[:, b, 128:129]), [BO], [Bs])
                u, Bu = ur.next()
                P.op("dve", lambda e, u=u, s_=s_, O=O, b=b: e.tensor_scalar(u[:], O[:, b, 0:128], s_[:, 0:1], None, ALU.mult),
                     [BO, Bs], [Bu])
                pt, Bpt = pst.next()
                tr(P, pt[:, 0, :], u[:], identb[:], [Bu, Bidb], [Bpt])
                uT, BuT = uTr.next()
                P.op("act", lambda e, uT=uT, pt=pt: e.copy(uT[:], pt[:, 0, :]), [Bpt], [BuT])
                po, Bpo = pso.next()
                mm(P, po[:, 0:128], uT[:], wv[:, h, :], True, True, [BuT, Bwv], [Bpo])
                P.op("dve", lambda e, yb=yb, po=po, b=b: e.tensor_copy(yb[:, b, :], po[:, 0:128]), [Bpo], [Byb], join=(b > 0))
            P.dma("sp", io["y"][q0:q0 + 512, 128 * h:128 * h + 128].rearrange("(b p) c -> p b c", p=128),
                  yb[:], [Byb], [By[qi_]], join=True)
    P.build()


def bcast_mid(ap2d, n):
    a = [list(x) for x in ap2d.ap]
    return bass.AP(ap2d.tensor, ap2d.offset, [a[0], [0, n]] + a[1:])


def phase_BD1(nc, io, tag):
    P = Prog(nc, tag)
    scale = 64 ** -0.5
    NX = 22
    qdT = P.sbuf("qdT", [64, 8, T], BF16); Bqd = P.buf("qdT")
    P.dma("sp", qdT[:], io["q"][768:1280, 0:T].rearrange("(h d) t -> d h t", d=64), [], [Bqd])
    kdX = P.sbuf("kdX", [64, 8, TT], BF16); Bkd = P.buf("kdX")
    P.dma("pool", kdX[:], io["kt"][192:704, :].rearrange("(h d) t -> d h t", d=64), [], [Bkd])
    vdX = P.sbuf("vdX", [128, 18, 520], BF16); Bvd = P.buf("vdX")
    P.dma("sp", vdX[:], io["v"][:, 129:649].rearrange("(c p) e -> p c e", p=128), [], [Bvd])
    hidx = P.sbuf("hidx", [128, NIDX], I32); Bhidx = P.buf("hidx")
    P.dma("sp", hidx[:], io["hidx"], [], [Bhidx])
    kdH = P.sbuf("kdH", [64, 8, 2, 256], BF16); BkdH = P.buf("kdH")
    vdH = P.sbuf("vdH", [128, 4, 649], BF16); BvdH = P.buf("vdH")
    for side in range(2):
        for h in range(8):
            c = 6 + 8 * side + h
            P.gather(kdH[:, h, side, :], io["hk_all"], hidx[0:64, c:c + 1], [Bhidx], [BkdH], join=not (side == 0 and h == 0))
        for c2 in range(2):
            c = 22 + 2 * side + c2
            P.gather(vdH[:, 2 * side + c2, :], io["v_all"], hidx[:, c:c + 1], [Bhidx], [BvdH], join=not (side == 0 and c2 == 0))
    cm = P.sbuf("cm", [128, 64], F32); Bcm = P.buf("cm")
    P.dma("sp", cm[:], io["colmask"], [], [Bcm])
    vm = P.sbuf("vm", [128, NTB, 6, 2], F32); Bvm = P.buf("vm")
    P.dma("sp", vm[:], io["vmD"], [], [Bvm])
    TB = P.sbuf("TB", [128, 120, 64], F32); BTB = P.buf("TB")
    rp = io["rp_pad"]
    TBr = P.sbuf("TBr", [128, 120 * 64], F32); BTBr = P.buf("TBr")
    for a in range(2):
        src = bass.AP(rp.tensor, rp.offset, [[1, 64], [128, 120], [1, 64]])
        P.dma("sp" if a else "pool", TBr[64 * a:64 * a + 64, :].rearrange("p (r q) -> p r q", q=64), src, [], [BTBr], join=(a > 0))
    J2 = P.sbuf("J2", [128, 128], F32); BJ2 = P.buf("J2")
    P.dma("sp", J2[:], io["antidiag"], [], [BJ2])
    psS = Rot(P, "ps", "psS", [128, 512], F32, 3)
    psT = psS
    TBf = TB[:].rearrange("p r q -> p (r q)")
    for cchunk in range(15):
        ps, Bps = psT.next()
        mm(P, ps[:], J2[:], TBr[:, cchunk * 512:(cchunk + 1) * 512], True, True, [BJ2, BTBr], [Bps])
        P.op("act", lambda e, ps=ps, cchunk=cchunk: e.activation(TBf[:, cchunk * 512:(cchunk + 1) * 512], ps[:], AF.Exp),
             [Bps], [BTB], join=(cchunk > 0))
    tt(P, "dve", TB[:], TB[:], bcast_mid(cm[:], 120), ALU.mult, [BTB, Bcm], [BTB])
    TB4 = TB[:].rearrange("p (h r) q -> p h r q", h=8)
    accr = Rot(P, "ps", "acc", [128, 512], F32, 4)
    ptr_ = Rot(P, "sb", "pT", [128, 512], BF16, 5)
    ebr = Rot(P, "sb", "eb", [128, 8, 128], BF16, 3)
    ytr = Rot(P, "sb", "yt", [128, 512], F32, 2)
    ztr = Rot(P, "sb", "zt", [128, 8], F32, 2)
    By = P.bufs_n("y", 5)
    ebstd_t = P.sbuf("ebstd", [128, 5, 8, 128], BF16); Bebstd = P.buf("ebstd")
    ebstd = [ebstd_t[:, ci, :, :] for ci in range(5)]
    mset(P, "pool", ebstd_t[:], 0.0, [Bebstd])
    efirst = True
    for ci in range(5):
        for a in range(2):
            for b in range(2):
                dlt = -4 + 2 * ci + a - b
                if -4 <= dlt <= 3:
                    dr = dlt + 7
                    P.op("dve", lambda e, ci=ci, a=a, b=b, dr=dr: e.tensor_copy(
                        ebstd_t[64 * a:64 * a + 64, ci, :, 64 * b:64 * b + 64], TB4[64 * a:64 * a + 64, :, dr, :]),
                        [BTB, Bebstd] if efirst else [BTB], [Bebstd], join=not efirst)
                    efirst = False
    for n in range(NTB):
        q0 = n * 128
        cis = list(range(0, 6)) if n == 0 else (list(range(-1, 5)) if n == NTB - 1 else list(range(0, 5)))
        def ksrc(oc):
            if oc < 0:
                return (lambda h, oc=oc: kdH[:, h, 0, (oc + 2) * 128:(oc + 3) * 128],
                        lambda h, oc=oc: vdH[:, oc + 2, 129 + 65 * h:129 + 65 * h + 65], [BkdH, BvdH])
            if oc >= NTB and oc < NTB + 2:
                return (lambda h, oc=oc: kdH[:, h, 1, (oc - NTB) * 128:(oc - NTB + 1) * 128],
                        lambda h, oc=oc: vdH[:, 2 + oc - NTB, 129 + 65 * h:129 + 65 * h + 65], [BkdH, BvdH])
            if oc >= 100:
                oc = NTB + (oc - 100)
            return (lambda h, oc=oc: kdX[:, h, oc * 128:(oc + 1) * 128],
                    lambda h, oc=oc: vdX[:, oc, 65 * h:65 * h + 65], [Bkd, Bvd])
        chunks = [(ksrc(n + ci - 2), ci) for ci in cis] + [(ksrc(100), None), (ksrc(101), None)]
        accs = [accr.next() for _ in range(2)]
        nchk = len(chunks)
        pend = []

        def pvD(idx, hg, pT, BpT, vfn, kvb, accs=accs, nchk=nchk):
            acc_, Bacc = accs[hg]
            acc = acc_[:, 0:260].rearrange("p (g e) -> p g e", g=4)
            for hh in range(4):
                h = 4 * hg + hh
                mm(P, acc[:, hh, :], pT[:, hh * 128:(hh + 1) * 128], vfn(h),
                   idx == 0 and hh == 0, idx == nchk - 1, [BpT] + kvb, [Bacc])
        for idx, ((kfn, vfn, kvb), ci) in enumerate(chunks):
            eb = None
            if ci is not None and 2 <= n <= NTB - 3:
                eb, Beb = ebstd[ci], Bebstd
            elif ci is not None:
                eb, Beb = ebr.next()
                slot = ci - cis[0]
                first = True
                for a in range(2):
                    for b in range(2):
                        dr = 3 + 2 * ci + a - b
                        P.op("dve" if (a + b) % 2 else "pool", lambda e, eb=eb, a=a, b=b, dr=dr, n=n, slot=slot: e.tensor_scalar(
                            eb[64 * a:64 * a + 64, :, 64 * b:64 * b + 64], TB4[64 * a:64 * a + 64, :, dr, :],
                            vm[64 * a:64 * a + 64, n, slot, b:b + 1], None, ALU.mult), [BTB, Bvm], [Beb], join=not first)
                        first = False
            for hg in range(2):
                ps, Bps = psS.next()
                for hh in range(4):
                    h = 4 * hg + hh
                    mm(P, ps[:, hh * 128:(hh + 1) * 128], kfn(h), qdT[:, h, q0:q0 + 128],
                       True, True, kvb + [Bqd], [Bps])
                pT, BpT = ptr_.next()
                act(P, pT[:], ps[:], AF.Exp, [Bps], [BpT], scale=scale)
                if eb is not None:
                    P.op("dve", lambda e, pT=pT, eb=eb, hg=hg: e.tensor_tensor(
                        pT[:], pT[:], eb[:, 4 * hg:4 * hg + 4, :].rearrange("p h q -> p (h q)"), ALU.mult),
                        [BpT, Beb], [BpT])
                pend.append((idx, hg, pT, BpT, vfn, kvb))
                if len(pend) > BD1_SKEW:
                    pvD(*pend.pop(0))
        while pend:
            pvD(*pend.pop(0))
        yt, Byt = ytr.next()
        for hg in range(2):
            acc_, Bacc = accs[hg]
            acc = acc_[:, 0:260].rearrange("p (g e) -> p g e", g=4)
            zt, Bzt = ztr.next()
            P.op("dve", lambda e, zt=zt, acc=acc: e.reciprocal(zt[:, 0:4], acc[:, :, 64]), [Bacc], [Bzt])
            for hh in range(4):
                h = 4 * hg + hh
                P.op("dve", lambda e, yt=yt, acc=acc, zt=zt, hh=hh, h=h: e.tensor_scalar(
                    yt[:, h * 64:(h + 1) * 64], acc[:, hh, 0:64], zt[:, hh:hh + 1], None, ALU.mult),
                    [Bacc, Bzt], [Byt], join=not (hg == 0 and hh == 0))
        P.dma("sp", io["y"][q0:q0 + 128, 512:1024], yt[:], [Byt], [By[n // 4]], join=True)
    P.build()


def _rope_tables():
    t = np.arange(S)
    row = (t // GRID_W).astype(np.float32)
    col = (t % GRID_W).astype(np.float32)
    inv = (np.float32(10000.0) ** (-np.arange(16, dtype=np.float32) / np.float32(16))).astype(np.float32)
    ang_r = row[:, None] * inv[None, :]
    ang_c = col[:, None] * inv[None, :]
    ang = np.concatenate([ang_r, ang_r, ang_c, ang_c], axis=-1).astype(np.float32)
    return np.cos(ang).astype(np.float32), np.sin(ang).astype(np.float32)


def _rope_core_tables(r, cos, sin):
    cT = np.ones((128, TT), np.float32)
    sT = np.zeros((128, TT), np.float32)
    c = cos[r * T:(r + 1) * T].T
    s_ = (sin[r * T:(r + 1) * T] * ROPE_SIGN[None, :]).T
    cT[0:64, 0:T] = c; cT[64:128, 0:T] = c
    sT[0:64, 0:T] = s_; sT[64:128, 0:T] = s_
    return cT, sT


class IO(dict):
    pass


def _mk(nc, specs):
    io = IO()
    for (name, shape, dt, kind) in specs:
        io[name] = nc.dram_tensor(name, list(shape), dt, kind=kind).ap()
    return io


def _dt(a):
    if a.dtype == np.float32:
        return F32
    if a.dtype == NPBF:
        return BF16
    if a.dtype == np.int32:
        return I32
    raise ValueError(a.dtype)


def _launch(build, in_maps, outs):
    nc = bass.Bass("TRN2", target_bir_lowering=False)
    specs = [(k, v.shape, _dt(v), "ExternalInput") for k, v in in_maps[0].items()]
    specs += [(k, shp, dt, "ExternalOutput") for (k, shp, dt) in outs]
    io = _mk(nc, specs)
    build(nc, io)
    res = run_bass_kernel_spmd(nc, in_maps, core_ids=list(range(NCORES)))
    return res.results


def _arr_col(v):
    return np.ascontiguousarray(v.reshape(8, 128).T)


def _maskA(r):
    kk = np.arange(128)[:, None]
    qq = np.arange(128)[None, :]
    tp = np.tile((qq <= kk).astype(np.float32), (1, 4))
    tn = np.tile((kk <= qq).astype(np.float32), (1, 4))
    z = np.zeros_like(tp)
    return np.stack([z if r == 0 else tp, tp, tn, z if r == NCORES - 1 else tn]).astype(NPBF)


def _colmask():
    qc = np.arange(64)
    cs = np.clip(qc - 8, 0, 48)
    kc = np.arange(64)[:, None]
    m = ((kc >= cs[None, :]) & (kc < cs[None, :] + 16)).astype(np.float32)
    return np.ascontiguousarray(np.concatenate([m, m], 0))


def _antidiag():
    j = np.zeros((128, 128), np.float32)
    for a in range(2):
        for kc in range(64):
            j[64 * a + 63 - kc, 64 * a + kc] = 1.0
    return j


def _vmD(r):
    vm = np.zeros((128, NTB, 6, 2), np.float32)
    for n in range(NTB):
        cis = list(range(0, 6)) if n == 0 else (list(range(-1, 5)) if n == NTB - 1 else list(range(0, 5)))
        for slot, ci in enumerate(cis):
            for a in range(2):
                for b in range(2):
                    gr = 32 * r + 2 * n + b
                    rs = min(max(gr - 4, 0), 248)
                    kr = 32 * r + 2 * n - 4 + 2 * ci + a
                    if rs <= kr <= rs + 7:
                        vm[64 * a:64 * a + 64, n, slot, b] = 1.0
    return vm


def phase_AG(nc, pairs, tag):
    P = Prog(nc, tag)
    prev = []
    for i, (own, allg) in enumerate(pairs):
        b = P.buf(f"ag{i}")
        P.collective("AllGather", own, allg, prev, [b])
        prev = [b]
    P.build()


def _hidx(r):
    rp, rn = max(r - 1, 0), min(r + 1, NCORES - 1)
    p = np.arange(128, dtype=np.int64)
    cols = []
    for j in range(2):
        cols.append(rp * 256 + 128 + 64 * j + p)
    for j in range(2):
        cols.append(rn * 256 + 0 + 64 * j + p)
    cols.append(rp * TT + (T - 128) + p)
    cols.append(rn * TT + 0 + p)
    for h in range(8):
        cols.append(rp * 1024 + 512 + 64 * h + p)
    for h in range(8):
        cols.append(rn * 1024 + 0 + 64 * h + p)
    for c2 in range(2):
        cols.append(rp * TT + (T - 256) + 128 * c2 + p)
    for c2 in range(2):
        cols.append(rn * TT + 128 * c2 + p)
    a = np.stack(cols, 1)
    a[64:, 0:4] = 0
    a[64:, 6:22] = 0
    return np.ascontiguousarray(a.astype(np.int32))


def _layer_inputs(lay, sfx, norm_g, w_ada, b_ada, w_in):
    cols = []
    for c0 in lay.rope_cols:
        for hh in range(2):
            cols.append(c0 + 64 * hh + ROPE_PERM)
    cols = np.concatenate(cols)
    if lay.idx == 1:
        cols = np.concatenate([384 + ROPE_PERM, 384 + ROPE_PERM])
    return {"norm_g" + sfx: _arr_col(norm_g), "b_ada2" + sfx: np.ascontiguousarray(np.tile(b_ada.reshape(1, -1), (2, 1))),
            "w_ada" + sfx: np.ascontiguousarray(w_ada), "w_in" + sfx: np.ascontiguousarray(w_in),
            "w_rope" + sfx: np.ascontiguousarray(w_in[:, cols])}


_STOP = [None]


def build_fused(nc, ext):
    def scr(name, shape, dt):
        return nc.dram_tensor(name, list(shape), dt, kind="Internal").ap()

    common = {k: ext[k] for k in ("cc2", "cosT", "sinT", "ident", "hidx")}
    y = scr("y_scr", (TT, 1024), F32)
    x1 = scr("x1_scr", (T, 1024), F32)
    xc1 = scr("xc1_scr", (L, 1024), F32)
    xin = [ext["x"], x1]
    xcin = [ext["xc"], xc1]
    for li, lay in enumerate((Lay0, Lay1)):
        sfx = str(li)
        q = scr("q" + sfx, (lay.QROWS, TT), BF16)
        kt = scr("kt" + sfx, (lay.KTROWS, TT), BF16)
        v = scr("v" + sfx, (TT, lay.VCOLS), BF16)
        hk = scr("hk" + sfx, lay.HKSHAPE, BF16)
        sg = scr("sg" + sfx, (TT, 1024), F32)
        mod = scr("mod" + sfx, (2, 3072), F32)
        kt_all = scr("kt_all" + sfx, (NCORES * lay.KTROWS, TT), BF16)
        v_all = scr("v_all" + sfx, (NCORES * TT, lay.VCOLS), BF16)
        hk_all = scr("hk_all" + sfx, (NCORES * lay.HKSHAPE[0], lay.HKSHAPE[1]), BF16)
        ioA = IO(common)
        ioA.update(x=xin[li], xc=xcin[li], q=q, kt=kt, v=v, hk=hk, sg=sg, mod=mod)
        for k in ("norm_g", "b_ada2", "w_ada", "w_in", "w_rope"):
            ioA[k] = ext[k + sfx]
        if li == 1:
            for k in ("kvng_row", "w_qb_all", "qng_col", "wkT"):
                ioA[k] = ext[k]
        phase_A(nc, lay, ioA, f"a{li}_")
        nc.all_engine_barrier()
        if _STOP[0] == "A":
            return
        phase_AG(nc, [(kt, kt_all), (v, v_all), (hk, hk_all)], f"g{li}_")
        nc.all_engine_barrier()
        ioB = IO(common)
        ioB.update(q=q, kt=kt, v=v, sg=sg, mod=mod, y=y, x=xin[li], xc=xcin[li], hk_all=hk_all, v_all=v_all,
                   kt_all=kt_all.rearrange("(r a) c -> r a c", r=NCORES),
                   w_out=ext["w_out" + sfx])
        ioB3 = IO(ioB)
        ioB3["v_all"] = v_all.rearrange("(r a) c -> r a c", r=NCORES)
        if li == 0:
            for k in ("maskA", "a_sink", "b_lambda", "subln_g"):
                ioB[k] = ext[k]; ioB3[k] = ext[k]
            ioB["x1"] = x1; ioB["xc1"] = xc1
            if _STOP[0] == "AG":
                return
            phase_BA0(nc, ioB, "ba_")
            nc.all_engine_barrier()
            if _STOP[0] == "BA":
                return
            phase_BB0(nc, ioB3, "bb_")
            nc.all_engine_barrier()
            if _STOP[0] == "BB":
                return
            phase_BO(nc, ioB, "bo0_", last=False)
            nc.all_engine_barrier()
            if _STOP[0] == "BO":
                return
        else:
            for k in ("wv", "rp_pad", "colmask", "vmD", "antidiag", "final_g"):
                ioB[k] = ext[k]; ioB3[k] = ext[k]
            ioB["out"] = ext["out"]
            phase_BC1(nc, ioB3, "bc_")
            nc.all_engine_barrier()
            phase_BD1(nc, ioB, "bd_")
            nc.all_engine_barrier()
            phase_BO(nc, ioB, "bo1_", last=True)


def kernel(**inputs):
    inp = {k: np.asarray(v) for k, v in inputs.items()}
    cos, sin = _rope_tables()
    x = inp["x"][0]
    ident = np.eye(128, dtype=np.float32)
    cc2 = np.ascontiguousarray(np.stack([_arr_col(inp["c"].reshape(-1)), _arr_col(inp["c_ctx"].reshape(-1))], axis=-1))
    shared = dict(xc=np.ascontiguousarray(inp["ctx"][0]), cc2=cc2, ident=ident)
    shared.update(_layer_inputs(Lay0, "0", inp["ev_norm_g"][0], inp["ev_w_ada"][0], inp["ev_b_ada"][0], inp["ev_w_in"][0]))
    shared.update(_layer_inputs(Lay1, "1", inp["od_norm_g"][0], inp["od_w_ada"][0], inp["od_b_ada"][0], inp["od_w_in"][0]))
    shared.update(a_sink=np.ascontiguousarray(inp["ev_a_sink"][0].reshape(1, 8)),
                  b_lambda=np.ascontiguousarray(inp["ev_b_lambda"][0].reshape(1, 256)),
                  subln_g=np.ascontiguousarray(inp["ev_b_subln_g"][0].reshape(1, 128)),
                  w_out0=np.ascontiguousarray(inp["ev_w_out"][0]), w_out1=np.ascontiguousarray(inp["od_w_out"][0]))
    w_qb = inp["od_c_w_qb"][0]
    pe_cols = np.concatenate([192 * h + 128 + ROPE_PERM for h in range(4)])
    w_kvb = inp["od_c_w_kvb"][0].reshape(128, 4, 256)
    rpb = inp["od_d_rpb"][0]
    rp = np.zeros((8, 15, 128), np.float32)
    rp[:, :, 48:79] = rpb[:, :, ::-1]
    shared.update(kvng_row=np.ascontiguousarray(inp["od_c_kv_norm_g"][0].reshape(1, 128)),
                  w_qb_all=np.ascontiguousarray(np.concatenate([w_qb, w_qb[:, pe_cols]], axis=1)),
                  qng_col=np.ascontiguousarray(inp["od_c_q_norm_g"][0].reshape(2, 128).T),
                  wkT=np.ascontiguousarray(np.transpose(w_kvb[:, :, 0:128], (2, 1, 0))),
                  wv=np.ascontiguousarray(w_kvb[:, :, 128:256]), rp_pad=np.ascontiguousarray(rp.reshape(120, 128)),
                  colmask=_colmask(), antidiag=_antidiag(),
                  final_g=np.ascontiguousarray(inp["final_norm_g"].reshape(1, 1024)))
    maps = []
    for r in range(NCORES):
        cT, sT = _rope_core_tables(r, cos, sin)
        m = dict(shared)
        m.update(x=np.ascontiguousarray(x[r * T:(r + 1) * T]), cosT=cT, sinT=sT, hidx=_hidx(r), maskA=_maskA(r), vmD=_vmD(r))
        maps.append(m)
    res = _launch(build_fused, maps, [("out", (T, 1024), F32)])
    out = np.concatenate([res[r]["out"] for r in range(NCORES)], axis=0)
    return out.reshape(1, S, D).astype(np.float32)
```

```python
import math
import numpy as np
from contextlib import ExitStack
import ml_dtypes
import concourse.bass as bass
import concourse.mybir as mybir
from concourse.bass_utils import run_bass_kernel_spmd

F32 = mybir.dt.float32
BF16 = mybir.dt.bfloat16
I32 = mybir.dt.int32
AF = mybir.ActivationFunctionType
ALU = mybir.AluOpType
AX = mybir.AxisListType
NPBF = ml_dtypes.bfloat16

NCORES = 8
S = 16384
T = 2048
NTB = 16
L = 256
TT = T + L
D = 1024
EPS = 1e-6
GRID_W = 64
NIDX = 26
BD1_SKEW = 0


class Buf:
    __slots__ = ("name", "writers", "readers", "dma_sem", "dma_cnt", "excl")

    def __init__(self, name, excl=False):
        self.name = name
        self.excl = excl
        self.writers = []
        self.readers = []
        self.dma_sem = None
        self.dma_cnt = 0


class Op:
    __slots__ = ("eng", "emit", "deps", "signal", "sigval", "is_dma", "dsem", "dval", "cc_inc")

    def __init__(self, eng, emit, is_dma=False):
        self.eng = eng
        self.emit = emit
        self.deps = []
        self.signal = False
        self.sigval = 0
        self.is_dma = is_dma
        self.dsem = None
        self.dval = 0
        self.cc_inc = 16


class Prog:
    ENGS = ("pe", "act", "dve", "pool", "sp")

    def __init__(self, nc, tag=""):
        self.nc = nc
        self.tag = tag
        self.ops = {e: [] for e in self.ENGS}
        self.stack = ExitStack()
        self.esem = {}
        self.bufs = []
        self.dma_bufs = []
        self.sems = []

    def sem(self, name):
        h = self.nc.alloc_semaphore(name=self.tag + name)
        self.sems.append(h)
        return h

    def sbuf(self, name, shape, dt):
        return self.stack.enter_context(self.nc.sbuf_tensor(self.tag + name, shape, dt))

    def psum(self, name, shape, dt):
        return self.stack.enter_context(self.nc.psum_tensor(self.tag + name, shape, dt))

    def buf(self, name, excl=False):
        b = Buf(name, excl)
        self.bufs.append(b)
        return b

    def bufs_n(self, name, n):
        return [self.buf(f"{name}{i}") for i in range(n)]

    def _add(self, op, reads, writes, join=False):
        xr = [b for b in reads if b.excl]
        reads = [b for b in reads if not b.excl]
        xw = [b for b in writes if b.excl]
        writes = [b for b in writes if not b.excl]
        deps = []
        for b in xr + xw:
            deps.extend(b.readers)
            deps.extend(b.writers)
        for b in reads:
            deps.extend(b.writers)
        for b in writes:
            deps.extend(b.readers)
            if not join:
                deps.extend(b.writers)
        op.deps = [d for d in deps if not (d.eng == "pe" and op.eng == "pe" and not d.is_dma and not op.is_dma)]
        for b in xr + xw:
            b.writers = [op]
            b.readers = []
        for b in reads:
            b.readers.append(op)
        for b in writes:
            if join:
                b.writers.append(op)
            else:
                b.writers = [op]
            b.readers = []
        self.ops[op.eng].append(op)
        return op

    def op(self, eng, emit, reads=(), writes=(), join=False):
        return self._add(Op(eng, emit), list(reads), list(writes), join)

    def dma(self, eng, out, in_, reads, writes, join=False, emit=None, sem_buf=None, **kw):
        assert len(writes) == 1
        b = sem_buf if sem_buf is not None else (reads[0] if (len(reads) == 1 and self.outbound(out)) else writes[0])
        if b.dma_sem is None:
            b.dma_sem = self.sem("d_" + b.name)
            self.dma_bufs.append(b)
        b.dma_cnt += 16
        if emit is None:
            emit = lambda e, out=out, in_=in_, kw=kw: e.dma_start(out=out, in_=in_, **kw)
        o = Op(eng, emit, is_dma=True)
        o.dsem = b.dma_sem
        o.dval = b.dma_cnt
        return self._add(o, list(reads), list(writes), join)

    @staticmethod
    def outbound(out_ap):
        try:
            return "DRam" in type(out_ap.tensor).__name__ or "Dram" in type(out_ap.tensor).__name__ or "DRAM" in type(out_ap.tensor).__name__
        except Exception:
            return False

    def gather(self, out, src2d, idx, reads, writes, join=False):
        def emit(e, out=out, src2d=src2d, idx=idx):
            return e.indirect_dma_start(out=out, out_offset=None, in_=src2d,
                                        in_offset=bass.IndirectOffsetOnAxis(ap=idx, axis=0))
        return self.dma("pool", None, None, reads, writes, join=join, emit=emit, sem_buf=writes[0])

    def collective(self, kind, in_ap, out_ap, reads, writes):
        def emit(e):
            return e.collective_compute(kind, ALU.bypass, replica_groups=[list(range(NCORES))], ins=[in_ap], outs=[out_ap])
        o = self.dma("pool", None, None, reads, writes, emit=emit, sem_buf=writes[0])
        b = writes[0]
        b.dma_cnt += 1 - 16
        o.dval = b.dma_cnt
        o.cc_inc = 1
        return o

    def wait_all(self, eng, bufs):
        return self._add(Op(eng, None), list(bufs), [])

    def build(self):
        nc = self.nc
        fin = Op("sp", None)
        fin.deps = []
        for b in self.dma_bufs:
            d = Op("sp", None, is_dma=True)
            d.dsem = b.dma_sem
            d.dval = b.dma_cnt
            fin.deps.append(d)
        self.ops["sp"].append(fin)
        for e in self.ENGS:
            self.esem[e] = self.sem("e_" + e)
        for e in self.ENGS:
            for o in self.ops[e]:
                for d in o.deps:
                    if not d.is_dma:
                        d.signal = True
        for e in self.ENGS:
            c = 0
            for o in self.ops[e]:
                if o.signal and not o.is_dma:
                    c += 1
                    o.sigval = c
        engobj = {"pe": "tensor", "act": "scalar", "dve": "vector", "pool": "gpsimd", "sp": "sync"}

        def make(e):
            def fn(eng):
                waited = {}
                for o in self.ops[e]:
                    need = {}
                    for d in o.deps:
                        if d.is_dma:
                            s, v = d.dsem, d.dval
                        else:
                            s, v = self.esem[d.eng], d.sigval
                        k = id(s)
                        if k not in need or need[k][1] < v:
                            need[k] = (s, v)
                    for k, (s, v) in need.items():
                        if waited.get(k, 0) >= v:
                            continue
                        waited[k] = v
                        eng.wait_ge(s, v)
                    if o.emit is None:
                        continue
                    inst = o.emit(eng)
                    if o.is_dma:
                        inst.then_inc(o.dsem, o.cc_inc)
                    elif o.signal:
                        inst.then_inc(self.esem[e], 1)
                last = max([o.sigval for o in self.ops[e]] + [0])
                if last > 0:
                    eng.wait_ge(self.esem[e], last)
            return fn

        with nc.Block() as block:
            for e in self.ENGS:
                if self.ops[e]:
                    getattr(block, engobj[e])(make(e))
        self.stack.close()
        nc.all_engine_barrier()
        nc.clear_and_free_semaphores(self.sems)
        nc.all_engine_barrier()


def mm(P, out, lhsT, rhs, start, stop, reads, writes):
    return P.op("pe", lambda e: e.matmul(out, lhsT, rhs, start=start, stop=stop, skip_group_check=True), reads, writes)


def tr(P, out, in_, ident, reads, writes):
    return P.op("pe", lambda e: e.transpose(out, in_, ident), reads, writes)


def act(P, out, in_, func, reads, writes, **kw):
    return P.op("act", lambda e: e.activation(out, in_, func, **kw), reads, writes)


def tsc(P, eng, out, in0, s1, s2, op0, op1, reads, writes):
    if s2 is None:
        return P.op(eng, lambda e: e.tensor_scalar(out, in0, s1, None, op0), reads, writes)
    return P.op(eng, lambda e: e.tensor_scalar(out, in0, s1, s2, op0, op1), reads, writes)


def tt(P, eng, out, in0, in1, op, reads, writes):
    return P.op(eng, lambda e: e.tensor_tensor(out, in0, in1, op), reads, writes)


def cp(P, eng, out, in_, reads, writes):
    if eng == "act":
        return P.op(eng, lambda e: e.copy(out, in_), reads, writes)
    return P.op(eng, lambda e: e.tensor_copy(out, in_), reads, writes)


def mset(P, eng, ap, val, writes):
    return P.op(eng, lambda e: e.memset(ap, val), [], writes)


def recip(P, out, in_, reads, writes):
    return P.op("dve", lambda e: e.reciprocal(out, in_), reads, writes)


def bcast_rows(ap_row, nparts):
    a = [list(x) for x in ap_row.ap]
    a[0] = [0, nparts]
    return bass.AP(ap_row.tensor, ap_row.offset, a)


class Rot:
    def __init__(self, P, kind, name, shape, dt, n):
        alloc = P.sbuf if kind == "sb" else P.psum
        self.t = [alloc(f"{name}{i}", shape, dt) for i in range(n)]
        self.b = [P.buf(f"{name}{i}", excl=(kind == "ps")) for i in range(n)]
        self.i = 0
        self.n = n

    def next(self):
        r = (self.t[self.i], self.b[self.i])
        self.i = (self.i + 1) % self.n
        return r


ROPE_PERM = np.concatenate([np.arange(16, 32), np.arange(0, 16), np.arange(48, 64), np.arange(32, 48)])
ROPE_SIGN = np.concatenate([-np.ones(16), np.ones(16), -np.ones(16), np.ones(16)]).astype(np.float32)


class Lay0:
    idx = 0
    C = 3328
    fm = ([(128 * i, 128, i, "q", 128 * i) for i in range(4)]
          + [(512, 128, 4, "kt", 0)]
          + [(768 + 128 * i, 128, 5 + i, "q", 512 + 128 * i) for i in range(4)]
          + [(1280 + 128 * i, 128, 9 + i, "kt", 128 + 128 * i) for i in range(4)])
    rope_cols = ([128 * i for i in range(4)] + [512] + [768 + 128 * i for i in range(4)]
                 + [1280 + 128 * i for i in range(4)])
    NR = 13
    QROWS = 1024
    KTROWS = 640
    VCOLS = 646
    tmv = [(640, 128, 2, 64, 0), (1792, 512, 4, 128, 130)]
    gcol = 2304
    halo = {("kt", 0): (0, 128, 128)}
    HKSHAPE = (256, 128)


class Lay1:
    idx = 1
    C = 3008
    fm = ([(448 + 128 * i, 128, None, "q", 768 + 128 * i) for i in range(4)]
          + [(960 + 128 * i, 128, None, "kt", 192 + 128 * i) for i in range(4)]
          + [(384, 64, 0, "kt", 128)])
    rope_cols = [384]
    NR = 1
    QROWS = 1280
    KTROWS = 704
    VCOLS = 649
    tmv = [(1472, 512, 8, 64, 129)]
    gcol = 1984
    halo = {("kt", 192 + 128 * i): (128 * i, 256, 512) for i in range(4)}
    HKSHAPE = (1024, 256)


def phase_A(nc, lay, io, tag):
    P = Prog(nc, tag)
    C = lay.C
    NRC = lay.NR * 128
    identb = P.sbuf("identb", [128, 128], BF16); Bidb = P.buf("identb")
    P.dma("pool", identb[:], io["ident"], [], [Bidb])
    identf = P.sbuf("identf", [128, 128], F32); Bidf = P.buf("identf")
    P.dma("sp", identf[:], io["ident"], [], [Bidf])
    wbf = P.sbuf("wbf", [128, 8, C], BF16); Bw = P.bufs_n("w", 8)
    wrp = P.sbuf("wrp", [128, 8, NRC], BF16); Bwr = P.bufs_n("wr", 8)
    cosT = P.sbuf("cosT", [128, TT], F32); Bcos = P.buf("cos")
    sinT = P.sbuf("sinT", [128, TT], F32); Bsin = P.buf("sin")
    P.dma("sp", cosT[:], io["cosT"], [], [Bcos])
    P.dma("sp", sinT[:], io["sinT"], [], [Bsin])
    cc = P.sbuf("cc", [128, 8, 2], F32); Bcc = P.buf("cc")
    P.dma("sp", cc[:], io["cc2"], [], [Bcc])
    ng = P.sbuf("ng", [128, 8], F32); Bng = P.buf("ng")
    P.dma("sp", ng[:], io["norm_g"], [], [Bng])
    sc = P.sbuf("sc", [128, 8, 2], F32); Bsc = P.buf("sc")
    act(P, sc[:], cc[:], AF.Silu, [Bcc], [Bsc])
    warot = Rot(P, "sb", "wada", [128, 8, 256], F32, 2)
    barot = Rot(P, "sb", "bada", [2, 256], F32, 2)
    mrrot = Rot(P, "sb", "mr", [2, 256], F32, 2)
    psA = Rot(P, "ps", "psA", [128, 512], F32, 2)
    psB = Rot(P, "ps", "psB", [128, 512], F32, 2)
    psm = psA
    pscol = P.psum("pscol", [128, 512], F32); Bpscol = P.buf("pscol", excl=True)
    w_ada_v = io["w_ada"].rearrange("(k p) c -> p k c", p=128)
    Bmodd = P.buf("mod_dram")
    for ct in range(12):
        wa, Bwa = warot.next()
        P.dma("sp", wa[:], w_ada_v[:, :, ct * 256:(ct + 1) * 256], [], [Bwa])
        ba, Bba = barot.next()
        P.dma("sp", ba[:], io["b_ada2"][:, ct * 256:(ct + 1) * 256], [], [Bba])
        ps, Bps = psm.next()
        for k in range(8):
            mm(P, ps[0:2, 0:256], sc[:, k, :], wa[:, k, :], k == 0, k == 7, [Bsc, Bwa], [Bps])
        mr, Bmr = mrrot.next()
        tt(P, "dve", mr[:], ps[0:2, 0:256], ba[:], ALU.add, [Bps, Bba], [Bmr])
        P.dma("sp", io["mod"][:, ct * 256:(ct + 1) * 256], mr[:], [Bmr], [Bmodd], join=True)
        if ct < 8:
            for cc_ in range(2):
                ch = 2 * ct + cc_
                mm(P, pscol[:, 2 * ch:2 * ch + 2], mr[0:2, cc_ * 128:(cc_ + 1) * 128], identf[0:2, 0:2], True, True,
                   [Bmr, Bidf], [Bpscol])
    for k in range(8):
        P.dma("pool", wbf[:, k, :], io["w_in"][k * 128:(k + 1) * 128, :], [], [Bw[k]])
    for k in range(8):
        P.dma("pool", wrp[:, k, :], io["w_rope"][k * 128:(k + 1) * 128, :], [], [Bwr[k]])
    ps, Bps = pscol, Bpscol
    modT = P.sbuf("modT", [128, 16, 2], F32); BmodT = P.buf("modT")
    cp(P, "dve", modT[:].rearrange("p a b -> p (a b)"), ps[:, 0:32], [Bps], [BmodT])
    Acol = P.sbuf("Acol", [128, 8, 2], F32); BA = P.buf("Acol")
    tsc(P, "dve", Acol[:].rearrange("p a b -> p (a b)"), modT[:, 8:16, :].rearrange("p a b -> p (a b)"), 1.0, None,
        ALU.add, None, [BmodT], [BA])
    for j in range(2):
        P.op("dve", lambda e, j=j: e.tensor_tensor(Acol[:, :, j], Acol[:, :, j], ng[:, :], ALU.mult), [BA, Bng], [BA])

    if io.get("_stop") == 1:
        P.build(); return
    hT = P.sbuf("hT", [128, 8, TT], BF16); BhT = P.bufs_n("hT", TT // 128)
    xrot = Rot(P, "sb", "xt", [128, 1024], F32, 2)
    junk = Rot(P, "sb", "junk", [128, 1024], BF16, 1)
    xnrot = Rot(P, "sb", "xn", [128, 1024], BF16, 2)
    strot = Rot(P, "sb", "st", [128, 4], F32, 3)
    ptr = Rot(P, "ps", "ptr", [128, 8, 128], BF16, 2)
    for i in range(io.get("_ntiles", TT // 128)):
        j = 0 if i < NTB else 1
        src = io["x"][i * 128:(i + 1) * 128, :] if i < NTB else io["xc"][(i - NTB) * 128:(i - NTB + 1) * 128, :]
        xt, Bxt = xrot.next()
        P.dma("sp", xt[:], src, [], [Bxt])
        jk, Bjk = junk.next()
        st, Bst = strot.next()
        act(P, jk[:], xt[:], AF.Square, [Bxt], [Bjk, Bst], accum_out=st[:, 0:1])
        act(P, st[:, 1:2], st[:, 0:1], AF.Sqrt, [Bst], [Bst], scale=1.0 / D, bias=EPS)
        recip(P, st[:, 2:3], st[:, 1:2], [Bst], [Bst])
        xn, Bxn = xnrot.next()
        tsc(P, "dve", xn[:], xt[:], st[:, 2:3], None, ALU.mult, None, [Bxt, Bst], [Bxn])
        pt, Bpt = ptr.next()
        for k in range(8):
            tr(P, pt[:, k, :], xn[:, k * 128:(k + 1) * 128], identb[:], [Bxn, Bidb], [Bpt])
        for k in range(8):
            dst = hT[:, k, i * 128:(i + 1) * 128]
            if i % 2 == 0:
                P.op("dve", lambda e, dst=dst, pt=pt, k=k, j=j: e.tensor_scalar(
                    dst, pt[:, k, :], Acol[:, k, j:j + 1], modT[:, k, j:j + 1], ALU.mult, ALU.add),
                    [Bpt, BA, BmodT], [BhT[i]], join=(k > 0))
            else:
                P.op("act", lambda e, dst=dst, pt=pt, k=k, j=j: e.activation(
                    dst, pt[:, k, :], AF.Identity, bias=modT[:, k, j:j + 1], scale=Acol[:, k, j:j + 1]),
                    [Bpt, BA, BmodT], [BhT[i]], join=(k > 0))

    if io.get("_stop") == 2:
        P.build(); return
    ttiles = [(0, 512), (512, 512), (1024, 512), (1536, 512), (2048, 256)]

    def hbufs(t0, n):
        return BhT[t0 // 128:(t0 + n) // 128]

    ostage = Rot(P, "sb", "ost", [128, TT], BF16, 2)
    t1rot = Rot(P, "sb", "t1", [128, 512], F32, 1)
    t2rot = Rot(P, "sb", "t2", [128, 512], F32, 1)
    dcount = [0]

    def dq():
        dcount[0] += 1
        return "sp" if dcount[0] % 2 else "pool"


    def fm_job(c0, M, dst_ap, w_t, Bw_l, rhs_fn, nk, rope=None, scale_bc=None, post=None):
        og, Bog = ostage.next()
        first = True
        for (t0, n) in ttiles:
            ps, Bps = psA.next()
            for k in range(nk):
                rhs, rb = rhs_fn(k, t0, n)
                mm(P, ps[0:M, 0:n], w_t[:, k, c0:c0 + M], rhs, k == 0, k == nk - 1, Bw_l + rb, [Bps])
            if rope is not None:
                wr_t, Bwr_l, rc0 = rope
                ps2, Bps2 = psB.next()
                for k in range(nk):
                    rhs, rb = rhs_fn(k, t0, n)
                    mm(P, ps2[0:M, 0:n], wr_t[:, k, rc0:rc0 + M], rhs, k == 0, k == nk - 1, Bwr_l + rb, [Bps2])
                t1, Bt1 = t1rot.next()
                t2, Bt2 = t2rot.next()
                tt(P, "dve", t1[0:M, 0:n], ps[0:M, 0:n], cosT[0:M, t0:t0 + n], ALU.mult, [Bps, Bcos], [Bt1])
                tt(P, "dve", t2[0:M, 0:n], ps2[0:M, 0:n], sinT[0:M, t0:t0 + n], ALU.mult, [Bps2, Bsin], [Bt2])
                if scale_bc is None:
                    P.op("pool", lambda e, og=og, t1=t1, t2=t2, t0=t0, n=n: e.tensor_tensor(
                        og[0:M, t0:t0 + n], t1[0:M, 0:n], t2[0:M, 0:n], ALU.add), [Bt1, Bt2], [Bog], join=not first)
                else:
                    sbt, Bsb = scale_bc
                    tt(P, "pool", t1[0:M, 0:n], t1[0:M, 0:n], t2[0:M, 0:n], ALU.add, [Bt1, Bt2], [Bt1])
                    P.op("pool", lambda e, og=og, t1=t1, t0=t0, n=n, sbt=sbt: e.tensor_tensor(
                        og[0:M, t0:t0 + n], t1[0:M, 0:n], sbt[0:M, t0:t0 + n], ALU.mult), [Bt1, Bsb], [Bog],
                        join=not first)
            elif post is not None:
                post(ps, Bps, og, Bog, t0, n, first)
            elif scale_bc is not None:
                sbt, Bsb = scale_bc
                P.op("dve", lambda e, og=og, ps=ps, t0=t0, n=n, sbt=sbt: e.tensor_tensor(
                    og[0:M, t0:t0 + n], ps[0:M, 0:n], sbt[0:M, t0:t0 + n], ALU.mult), [Bps, Bsb], [Bog],
                    join=not first)
            else:
                P.op("act", lambda e, og=og, ps=ps, t0=t0, n=n: e.copy(og[0:M, t0:t0 + n], ps[0:M, 0:n]),
                     [Bps], [Bog], join=not first)
            first = False
        if dst_ap is not None:
            P.dma(dq(), dst_ap, og[0:M, :], [Bog], [Bfmout], join=True)
        return og, Bog

    Bfmout = P.buf("fmout"); Bvout = P.buf("vout"); Bgout = P.buf("gout")

    def h_rhs(k, t0, n):
        return hT[:, k, t0:t0 + n], hbufs(t0, n)

    for (c0, M, ridx, dname, drow) in lay.fm:
        og, Bog = fm_job(c0, M, io[dname][drow:drow + M, :], wbf, Bw, h_rhs, 8,
                         rope=None if ridx is None else (wrp, Bwr, ridx * 128))
        hp = lay.halo.get((dname, drow))
        if hp is not None:
            hrow, hw, hrows = hp
            P.dma(dq(), io["hk"][hrow:hrow + M, :], og[0:M, 0:hw], [Bog], [Bfmout], join=True)
            P.dma(dq(), io["hk"][hrows + hrow:hrows + hrow + M, :], og[0:M, T - hw:T], [Bog], [Bfmout], join=True)

    if io.get("_stop") == 3:
        P.build(); return
    vst = Rot(P, "sb", "vst", [128, lay.VCOLS], BF16, 2)
    gst = Rot(P, "sb", "gst", [128, 1024], F32, 2)
    for i in range(2):
        mset(P, "pool", vst.t[i][:], 1.0, [vst.b[i]])

    def tm_mm(i, c0, ncols):
        ps, Bps = psA.next()
        for k in range(8):
            mm(P, ps[:, 0:ncols], hT[:, k, i * 128:(i + 1) * 128], wbf[:, k, c0:c0 + ncols], k == 0, k == 7,
               [BhT[i]] + Bw, [Bps])
        return ps, Bps

    if lay.idx == 1:
        gkvb = P.sbuf("gkvb", [128, 128], F32); Bgkvb = P.buf("gkvb")
        P.dma("sp", gkvb[:], bcast_rows(io["kvng_row"], 128), [], [Bgkvb])
        ckT = P.sbuf("ckT", [128, TT], BF16); BckT = P.buf("ckT")
        st2 = Rot(P, "sb", "st2", [128, 4], F32, 3)
        jk2 = Rot(P, "sb", "jk2", [128, 128], F32, 2)
        ptc = ptr

    for i in range(TT // 128):
        vt, Bvt = vst.next()
        wfirst = True
        for (c0, ncols, nh, e, dcol) in lay.tmv:
            ps, Bps = tm_mm(i, c0, ncols)
            dstv = vt[:, dcol:dcol + nh * (e + 1)].rearrange("p (h e) -> p h e", e=e + 1)[:, :, 0:e]
            srcv = ps[:, 0:ncols].rearrange("p (h e) -> p h e", e=e)
            P.op("act", lambda en, dstv=dstv, srcv=srcv: en.copy(dstv, srcv), [Bps], [Bvt], join=not wfirst)
            wfirst = False
        if lay.idx == 1:
            ps, Bps = tm_mm(i, 256, 128)
            s2, Bs2 = st2.next()
            j2, Bj2 = jk2.next()
            act(P, j2[:], ps[:, 0:128], AF.Square, [Bps], [Bj2, Bs2], accum_out=s2[:, 0:1])
            act(P, s2[:, 1:2], s2[:, 0:1], AF.Sqrt, [Bs2], [Bs2], scale=1.0 / 128, bias=EPS)
            recip(P, s2[:, 2:3], s2[:, 1:2], [Bs2], [Bs2])
            P.op("dve", lambda en, vt=vt, ps=ps, s2=s2: en.scalar_tensor_tensor(
                vt[:, 0:128], ps[:, 0:128], s2[:, 2:3], gkvb[:], ALU.mult, ALU.mult), [Bps, Bs2, Bgkvb], [Bvt], join=True)
            pc, Bpc = ptc.next()
            tr(P, pc[:, 0, :], vt[:, 0:128], identb[:], [Bvt, Bidb], [Bpc])
            P.op("act", lambda en, pc=pc, i=i: en.copy(ckT[:, i * 128:(i + 1) * 128], pc[:, 0, :]), [Bpc], [BckT], join=True)
        P.dma(dq(), io["v"][i * 128:(i + 1) * 128, :], vt[:], [Bvt], [Bvout], join=True)
        gt, Bgt = gst.next()
        for hh in range(2):
            ps, Bps = tm_mm(i, lay.gcol + 512 * hh, 512)
            P.op("act", lambda en, gt=gt, ps=ps, hh=hh: en.activation(gt[:, hh * 512:(hh + 1) * 512], ps[:], AF.Silu),
                 [Bps], [Bgt], join=(hh == 1))
        P.dma(dq(), io["sg"][i * 128:(i + 1) * 128, :], gt[:], [Bgt], [Bgout], join=True)

    if lay.idx == 1:
        P.dma(dq(), io["kt"][0:128, :], ckT[:], [BckT], [Bfmout], join=True)
        cqT = P.sbuf("cqT", [128, 2, TT], BF16); BcqT = P.buf("cqT")
        sqr = Rot(P, "sb", "sqr", [128, 2, 512], F32, 1)
        rsq = P.sbuf("rsq", [128, TT], F32); Brsq = P.buf("rsq")
        onesf = P.sbuf("onesf", [128, 128], F32); Bones = P.buf("onesf")
        mset(P, "pool", onesf[:], 1.0, [Bones])
        first = True
        for (t0, n) in ttiles:
            sq, Bsq = sqr.next()
            for kk in range(2):
                ps, Bps = psA.next()
                for k in range(8):
                    mm(P, ps[:, 0:n], wbf[:, k, kk * 128:(kk + 1) * 128], hT[:, k, t0:t0 + n], k == 0, k == 7,
                       Bw + hbufs(t0, n), [Bps])
                P.op("act", lambda e, ps=ps, kk=kk, t0=t0, n=n: e.copy(cqT[:, kk, t0:t0 + n], ps[:, 0:n]),
                     [Bps], [BcqT], join=not (first and kk == 0))
                P.op("dve", lambda e, ps=ps, sq=sq, kk=kk, n=n: e.tensor_copy(sq[:, kk, 0:n], ps[:, 0:n]),
                     [Bps], [Bsq], join=(kk == 1))
            P.op("pool", lambda e, sq=sq, n=n: e.tensor_tensor(sq[:, :, 0:n], sq[:, :, 0:n], sq[:, :, 0:n], ALU.mult),
                 [Bsq], [Bsq])
            ps, Bps = psB.next()
            for kk in range(2):
                mm(P, ps[:, 0:n], onesf[:], sq[:, kk, 0:n], kk == 0, kk == 1, [Bones, Bsq], [Bps])
            P.op("act", lambda e, ps=ps, t0=t0, n=n: e.activation(rsq[:, t0:t0 + n], ps[:, 0:n], AF.Sqrt,
                                                                scale=1.0 / 256, bias=EPS), [Bps], [Brsq], join=not first)
            first = False
        recip(P, rsq[:], rsq[:], [Brsq], [Brsq])
        wqf = P.sbuf("wqf", [128, 1024], F32); Bwqf = P.buf("wqf")
        qng = P.sbuf("qng", [128, 2], F32); Bqng = P.buf("qng")
        P.dma("sp", qng[:], io["qng_col"], [], [Bqng])
        wqb = P.sbuf("wqb", [128, 2, 1024], BF16); Bwqb = P.buf("wqb")
        for kk in range(2):
            P.dma("sp", wqf[:], io["w_qb_all"][kk * 128:(kk + 1) * 128, :], [], [Bwqf])
            P.op("dve", lambda e, kk=kk: e.tensor_scalar(wqb[:, kk, :], wqf[:], qng[:, kk:kk + 1], None, ALU.mult),
                 [Bwqf, Bqng], [Bwqb], join=(kk == 1))
        wkT = P.sbuf("wkT", [128, 4, 128], BF16); BwkT = P.buf("wkT")
        P.dma("pool", wkT[:], io["wkT"], [], [BwkT])

        def cq_rhs(k, t0, n):
            return cqT[:, k, t0:t0 + n], [BcqT]

        qn_rot = Rot(P, "sb", "qn", [128, 512], BF16, 2)
        for h in range(4):
            def post(ps, Bps, og, Bog, t0, n, first, h=h):
                qn, Bqn = qn_rot.next()
                tt(P, "dve", qn[:, 0:n], ps[:, 0:n], rsq[:, t0:t0 + n], ALU.mult, [Bps, Brsq], [Bqn])
                ps3, Bps3 = psB.next()
                mm(P, ps3[:, 0:n], wkT[:, h, :], qn[:, 0:n], True, True, [BwkT, Bqn], [Bps3])
                P.op("act", lambda e, og=og, ps3=ps3, t0=t0, n=n: e.copy(og[:, t0:t0 + n], ps3[:, 0:n]),
                     [Bps3], [Bog], join=not first)
            fm_job(192 * h, 128, io["q"][128 * h:128 * h + 128, :], wqb, [Bwqb], cq_rhs, 2, post=post)
            fm_job(192 * h + 128, 64, io["q"][512 + 64 * h:512 + 64 * h + 64, :], wqb, [Bwqb], cq_rhs, 2,
                   rope=(wqb, [Bwqb], 768 + 64 * h), scale_bc=(rsq, Brsq))
    P.build()


def load_common_B(P, io):
    identb = P.sbuf("identb", [128, 128], BF16); Bidb = P.buf("identb")
    P.dma("pool", identb[:], io["ident"], [], [Bidb])
    return identb, Bidb


def phase_BA0(nc, io, tag):
    P = Prog(nc, tag)
    scale = 64 ** -0.5
    qaT = P.sbuf("qaT", [64, 8, TT], BF16); Bqa = P.buf("qaT")
    P.dma("sp", qaT[:], io["q"][0:512, :].rearrange("(h d) t -> d h t", d=64), [], [Bqa])
    kaT = P.sbuf("kaT", [64, 2, TT], BF16); Bka = P.buf("kaT")
    P.dma("pool", kaT[:], io["kt"][0:128, :].rearrange("(j d) t -> d j t", d=64), [], [Bka])
    vaX = P.sbuf("vaX", [128, 18, 130], BF16); Bva = P.buf("vaX")
    P.dma("sp", vaX[:], io["v"][:, 0:130].rearrange("(c p) e -> p c e", p=128), [], [Bva])
    hidx = P.sbuf("hidx", [128, NIDX], I32); Bhidx = P.buf("hidx")
    P.dma("sp", hidx[:], io["hidx"], [], [Bhidx])
    kaH = P.sbuf("kaH", [64, 2, 2, 128], BF16); BkaH = P.buf("kaH")
    vaH = P.sbuf("vaH", [128, 2, 646], BF16); BvaH = P.buf("vaH")
    for side in range(2):
        for j in range(2):
            P.gather(kaH[:, j, side, :], io["hk_all"], hidx[0:64, 2 * side + j:2 * side + j + 1], [Bhidx], [BkaH],
                     join=not (side == 0 and j == 0))
        P.gather(vaH[:, side, :], io["v_all"], hidx[:, 4 + side:5 + side], [Bhidx], [BvaH], join=(side > 0))
    mk = P.sbuf("mk", [128, 4, 512], BF16); Bmk = P.buf("mk")
    P.dma("pool", mk[:], io["maskA"].rearrange("m p f -> p m f"), [], [Bmk])
    sk = P.sbuf("sk", [128, 8], F32); Bsk = P.buf("sk")
    P.dma("sp", sk[:], bcast_rows(io["a_sink"], 128), [], [Bsk])
    esk = P.sbuf("esk", [128, 8], F32); Besk = P.buf("esk")
    act(P, esk[:], sk[:], AF.Exp, [Bsk], [Besk])
    psS = Rot(P, "ps", "psS", [128, 512], F32, 2)
    accr = Rot(P, "ps", "acc", [128, 512], F32, 2)
    ptr_ = Rot(P, "sb", "pT", [128, 512], BF16, 3)
    ytr = Rot(P, "sb", "yt", [128, 512], F32, 2)
    ztr = Rot(P, "sb", "zt", [128, 8], F32, 2)
    By = P.bufs_n("y", 5)
    for n in range(TT // 128):
        own = n < NTB
        q0 = n * 128
        yt, Byt = ytr.next()
        for j in range(2):
            def kown(c, j=j):
                return (kaT[:, j, c * 128:(c + 1) * 128], vaX[:, c, 65 * j:65 * j + 65], [Bka, Bva])

            def khalo(side, j=j):
                return (kaH[:, j, side, :], vaH[:, side, 65 * j:65 * j + 65], [BkaH, BvaH])
            if own:
                chunks = [(khalo(0) if n == 0 else kown(n - 1), 0 if n == 0 else 1), (kown(n), None),
                          (khalo(1) if n == NTB - 1 else kown(n + 1), 3 if n == NTB - 1 else 2),
                          (kown(16), None), (kown(17), None)]
            else:
                chunks = [(kown(16), None), (kown(17), None)]
            acc_, Bacc = accr.next()
            acc = acc_[:, 0:260].rearrange("p (g e) -> p g e", g=4)
            for ci, ((kap, vap, kvb), m) in enumerate(chunks):
                ps, Bps = psS.next()
                mm(P, ps[:, :].rearrange("p (g q) -> p g q", g=4), kap,
                   qaT[:, 4 * j:4 * j + 4, q0:q0 + 128], True, True, kvb + [Bqa], [Bps])
                pT, BpT = ptr_.next()
                act(P, pT[:], ps[:], AF.Exp, [Bps], [BpT], scale=scale)
                if m is not None:
                    tt(P, "pool", pT[:], pT[:], mk[:, m, :], ALU.mult, [BpT, Bmk], [BpT])
                for g in range(4):
                    mm(P, acc[:, g, :], pT[:, g * 128:(g + 1) * 128], vap,
                       ci == 0 and g == 0, ci == len(chunks) - 1, [BpT] + kvb, [Bacc])
            zt, Bzt = ztr.next()
            tt(P, "dve", zt[:, 0:4], acc[:, :, 64], esk[:, 4 * j:4 * j + 4], ALU.add, [Bacc, Besk], [Bzt])
            recip(P, zt[:, 4:8], zt[:, 0:4], [Bzt], [Bzt])
            for g in range(4):
                hd = 4 * j + g
                P.op("dve", lambda e, yt=yt, acc=acc, zt=zt, g=g, hd=hd: e.tensor_scalar(
                    yt[:, hd * 64:(hd + 1) * 64], acc[:, g, 0:64], zt[:, 4 + g:5 + g], None, ALU.mult),
                    [Bacc, Bzt], [Byt], join=not (j == 0 and g == 0))
        P.dma("sp", io["y"][q0:q0 + 128, 0:512], yt[:], [Byt], [By[n // 4]], join=True)
    P.build()


def full_attn_pass(P, qk_list, nq, chunks, vaug_fn, vbufs, scale, psS, accs, ptr_, E1):
    nb = nq // 128
    a0, Ba0, a1, Ba1 = accs
    nchk = len(chunks)

    def pv(ci, ch, pT, BpT):
        for b in range(nb):
            at, Bat = (a0, Ba0) if b < 2 else (a1, Ba1)
            mm(P, at[:, b % 2, :], pT[:, b * 128:(b + 1) * 128], vaug_fn(ch), ci == 0 and b % 2 == 0, ci == nchk - 1,
               [BpT] + vbufs, [Bat])

    skew = max(1, psS.n - 1)
    pend = []
    for ci, ch in enumerate(chunks):
        ps, Bps = psS.next()
        for qi, (kT_fn, qT, bl) in enumerate(qk_list):
            mm(P, ps[:, 0:nq], kT_fn(ch), qT, qi == 0, qi == len(qk_list) - 1, bl, [Bps])
        pT, BpT = ptr_.next()
        act(P, pT[:, 0:nq], ps[:, 0:nq], AF.Exp, [Bps], [BpT], scale=scale)
        pend.append((ci, ch, pT, BpT))
        if len(pend) > skew:
            pv(*pend.pop(0))
    while pend:
        pv(*pend.pop(0))


def full_attn_multi(P, streams, nq, chunks, scale, psS, ptr_, skew=1):
    nb = nq // 128
    nchk = len(chunks)
    nqk = max(len(st["qk"]) for st in streams)

    def pv(ci, ch, pts):
        for st, (pT, BpT) in zip(streams, pts):
            a0, Ba0, a1, Ba1 = st["accs"]
            vfn, vb = st["v"]
            for b in range(nb):
                at, Bat = (a0, Ba0) if b < 2 else (a1, Ba1)
                mm(P, at[:, b % 2, :], pT[:, b * 128:(b + 1) * 128], vfn(ch), ci == 0 and b % 2 == 0, ci == nchk - 1,
                   [BpT] + vb, [Bat])

    pend = []
    for ci, ch in enumerate(chunks):
        pss = [psS.next() for _ in streams]
        for qi in range(nqk):
            for st, (ps, Bps) in zip(streams, pss):
                if qi < len(st["qk"]):
                    kT_fn, qT, bl = st["qk"][qi]
                    mm(P, ps[:, 0:nq], kT_fn(ch), qT, qi == 0, qi == len(st["qk"]) - 1, bl, [Bps])
        pts = []
        for (ps, Bps) in pss:
            pT, BpT = ptr_.next()
            act(P, pT[:, 0:nq], ps[:, 0:nq], AF.Exp, [Bps], [BpT], scale=scale)
            pts.append((pT, BpT))
        pend.append((ci, ch, pts))
        if len(pend) > skew:
            pv(*pend.pop(0))
    while pend:
        pv(*pend.pop(0))


def phase_BB0(nc, io, tag):
    P = Prog(nc, tag)
    scale = 64 ** -0.5
    lam_init = 0.8 - 0.6 * math.exp(-0.3 * 0)
    NCH = S // 128 + 2
    qbT = P.sbuf("qbT", [128, 4, TT], BF16); Bqb = P.buf("qbT")
    P.dma("sp", qbT[:], io["q"][512:1024, :].rearrange("(h r) t -> r h t", r=128), [], [Bqb])
    lb = P.sbuf("lb", [128, 256], F32); Blb = P.buf("lb")
    P.dma("sp", lb[:], bcast_rows(io["b_lambda"], 128), [], [Blb])
    lt = P.sbuf("lt", [128, 128], F32); Blt = P.buf("lt")
    ls = P.sbuf("ls", [128, 8], F32); Bls = P.buf("ls")
    tt(P, "dve", lt[:].rearrange("p (a b) -> p a b", a=2), lb[:].rearrange("p (a b c) -> p a b c", a=2, b=2)[:, :, 0, :],
       lb[:].rearrange("p (a b c) -> p a b c", a=2, b=2)[:, :, 1, :], ALU.mult, [Blb], [Blt])
    P.op("dve", lambda e: e.reduce_sum(ls[:, 0:2], lt[:].rearrange("p (a b) -> p a b", a=2), AX.X), [Blt], [Bls])
    act(P, ls[:, 2:4], ls[:, 0:2], AF.Exp, [Bls], [Bls])
    tt(P, "dve", ls[:, 4:5], ls[:, 3:4], ls[:, 2:3], ALU.subtract, [Bls], [Bls])
    tsc(P, "dve", ls[:, 5:6], ls[:, 4:5], -lam_init, None, ALU.add, None, [Bls], [Bls])
    sgl = P.sbuf("sgl", [128, 128], F32); Bsgl = P.buf("sgl")
    P.dma("sp", sgl[:], bcast_rows(io["subln_g"], 128), [], [Bsgl])
    tsc(P, "dve", sgl[:], sgl[:], 1.0 - lam_init, None, ALU.mult, None, [Bsgl], [Bsgl])

    kbr = Rot(P, "sb", "kb", [128, NCH * 128], BF16, 2)
    vbr = Rot(P, "sb", "vb", [128, NCH, 129], BF16, 2)
    psS = Rot(P, "ps", "psS", [128, 512], F32, 4)
    acc_f = [P.psum(f"acc{i}", [128, 512], F32) for i in range(4)]
    acc_t = [a[:, 0:258].rearrange("p (b e) -> p b e", b=2) for a in acc_f]
    acc_b = [P.buf(f"acc{i}", excl=True) for i in range(4)]
    ptr_ = Rot(P, "sb", "pT", [128, 512], BF16, 6)
    Or = [P.sbuf(f"O{t}", [128, 4, 129], F32) for t in range(2)]
    BO = [P.buf(f"O{t}") for t in range(2)]
    ur = Rot(P, "sb", "u", [128, 128], F32, 2)
    jr = Rot(P, "sb", "jk", [128, 128], F32, 2)
    sr = Rot(P, "sb", "s", [128, 8], F32, 3)
    ybr = Rot(P, "sb", "yb", [128, 4, 128], F32, 2)
    By = P.bufs_n("y", 5)
    qtiles = [(0, 512), (512, 512), (1024, 512), (1536, 512), (2048, 256)]
    for h in range(4):
        kb, Bkb = kbr.next()
        vb, Bvb = vbr.next()
        for r in range(NCORES):
            P.dma("sp" if r % 2 else "pool", kb[:, r * T:(r + 1) * T], io["kt_all"][r, 128 + 128 * h:256 + 128 * h, 0:T],
                  [], [Bkb], join=(r > 0))
            P.dma("pool" if r % 2 else "sp", vb[:, r * 16:(r + 1) * 16, :],
                  io["v_all"][r, 0:T, 130 + 129 * h:259 + 129 * h].rearrange("(c p) e -> p c e", p=128),
                  [], [Bvb], join=(r > 0))
        P.dma("sp", kb[:, S:S + L], io["kt_all"][0, 128 + 128 * h:256 + 128 * h, T:TT], [], [Bkb], join=True)
        P.dma("pool", vb[:, 128:130, :], io["v_all"][0, T:TT, 130 + 129 * h:259 + 129 * h].rearrange("(c p) e -> p c e", p=128),
              [], [Bvb], join=True)
        for qi_, (q0, nq) in enumerate(qtiles):
            nb = nq // 128
            chunks = list(range(NCH)) if q0 < T else [128, 129]
            streams = []
            for t in range(2):
                streams.append(dict(
                    qk=[(lambda ch, t=t, kb=kb: kb[64 * t:64 * t + 64, ch * 128:(ch + 1) * 128],
                         qbT[64 * t:64 * t + 64, h, q0:q0 + nq], [Bkb, Bqb])],
                    accs=(acc_t[2 * t], acc_b[2 * t], acc_t[2 * t + 1], acc_b[2 * t + 1]),
                    v=(lambda ch, vb=vb: vb[:, ch, :], [Bvb])))
            full_attn_multi(P, streams, nq, chunks, scale, psS, ptr_, skew=1)
            for t in range(2):
                P.op("act", lambda e, t=t: e.copy(Or[t][:, 0:2, :], acc_t[2 * t]), [acc_b[2 * t]], [BO[t]])
                if nb > 2:
                    P.op("act", lambda e, t=t: e.copy(Or[t][:, 2:4, :], acc_t[2 * t + 1]), [acc_b[2 * t + 1]], [BO[t]], join=True)
            yb, Byb = ybr.next()
            for b in range(nb):
                s_, Bs = sr.next()
                P.op("dve", lambda e, s_=s_, b=b: e.reciprocal(s_[:, 0:1], Or[0][:, b, 128:129]), [BO[0]], [Bs])
                P.op("dve", lambda e, s_=s_, b=b: e.reciprocal(s_[:, 1:2], Or[1][:, b, 128:129]), [BO[1]], [Bs])
                tt(P, "dve", s_[:, 2:3], s_[:, 1:2], ls[:, 5:6], ALU.mult, [Bs, Bls], [Bs])
                u, Bu = ur.next()
                P.op("dve", lambda e, u=u, s_=s_, b=b: e.tensor_scalar(u[:], Or[0][:, b, 0:128], s_[:, 0:1], None, ALU.mult),
                     [BO[0], Bs], [Bu])
                P.op("dve", lambda e, u=u, s_=s_, b=b: e.scalar_tensor_tensor(u[:], Or[1][:, b, 0:128], s_[:, 2:3], u[:],
                                                                               ALU.mult, ALU.add), [BO[1], Bs, Bu], [Bu])
                jk, Bjk = jr.next()
                act(P, jk[:], u[:], AF.Square, [Bu], [Bjk, Bs], accum_out=s_[:, 3:4])
                act(P, s_[:, 4:5], s_[:, 3:4], AF.Sqrt, [Bs], [Bs], scale=1.0 / 128, bias=EPS)
                recip(P, s_[:, 5:6], s_[:, 4:5], [Bs], [Bs])
                P.op("dve", lambda e, yb=yb, u=u, s_=s_, b=b: e.scalar_tensor_tensor(
                    yb[:, b, :], u[:], s_[:, 5:6], sgl[:], ALU.mult, ALU.mult), [Bu, Bs, Bsgl], [Byb], join=(b > 0))
            P.dma("sp", io["y"][q0:q0 + nq, 512 + 128 * h:640 + 128 * h].rearrange("(b p) c -> p b c", p=128),
                  yb[:, 0:nb, :], [Byb], [By[qi_]], join=True)
    P.build()


def phase_BO(nc, io, tag, last):
    P = Prog(nc, tag)
    identb, Bidb = load_common_B(P, io)
    wo = P.sbuf("wo", [128, 8, 1024], BF16); Bwo = P.buf("wo")
    P.dma("pool", wo[:], io["w_out"].rearrange("(k p) c -> p k c", p=128), [], [Bwo])
    gate = P.sbuf("gate", [128, 2, 1024], F32); Bgate = P.buf("gate")
    for j in range(2):
        P.dma("sp", gate[:, j, :], bcast_rows(io["mod"][j:j + 1, 2048:3072], 128), [], [Bgate], join=(j > 0))
    if last:
        fng = P.sbuf("fng", [128, 1024], F32); Bfng = P.buf("fng")
        P.dma("sp", fng[:], bcast_rows(io["final_g"], 128), [], [Bfng])
    yr = Rot(P, "sb", "yt", [128, 1024], F32, 2)
    gr = Rot(P, "sb", "gt", [128, 1024], F32, 2)
    xr = Rot(P, "sb", "xt", [128, 1024], F32, 2)
    ygr = Rot(P, "sb", "yg", [128, 1024], BF16, 2)
    ptr_ = Rot(P, "ps", "pt", [128, 8, 128], BF16, 2)
    ygTr = Rot(P, "sb", "ygT", [128, 8, 128], BF16, 2)
    pso = Rot(P, "ps", "pso", [128, 512], F32, 3)
    tmr = Rot(P, "sb", "tm", [128, 1024], F32, 2)
    x1r = Rot(P, "sb", "x1", [128, 1024], F32, 2)
    sr = Rot(P, "sb", "s", [128, 4], F32, 3)
    jr = Rot(P, "sb", "jk", [128, 1024], BF16, 2)
    Bout = P.buf("outd")
    nblk = NTB if last else TT // 128
    for i in range(nblk):
        own = i < NTB
        j = 0 if own else 1
        r0 = i * 128
        yt, Byt = yr.next(); gt, Bgt = gr.next(); xt, Bxt = xr.next()
        P.dma("sp", yt[:], io["y"][r0:r0 + 128, :], [], [Byt])
        P.dma("pool", gt[:], io["sg"][r0:r0 + 128, :], [], [Bgt])
        P.dma("sp", xt[:], io["x"][r0:r0 + 128, :] if own else io["xc"][r0 - T:r0 - T + 128, :], [], [Bxt])
        yg, Byg = ygr.next()
        tt(P, "pool", yg[:], yt[:], gt[:], ALU.mult, [Byt, Bgt], [Byg])
        pt, Bpt = ptr_.next()
        for k in range(8):
            tr(P, pt[:, k, :], yg[:, k * 128:(k + 1) * 128], identb[:], [Byg, Bidb], [Bpt])
        ygT, BygT = ygTr.next()
        if i % 2 == 0:
            P.op("act", lambda e, ygT=ygT, pt=pt: e.copy(ygT[:], pt[:]), [Bpt], [BygT])
        else:
            P.op("dve", lambda e, ygT=ygT, pt=pt: e.tensor_copy(ygT[:], pt[:]), [Bpt], [BygT])
        tm, Btm = tmr.next()
        x1, Bx1 = x1r.next()
        for ct in range(2):
            ps, Bps = pso.next()
            for k in range(8):
                mm(P, ps[:], ygT[:, k, :], wo[:, k, ct * 512:(ct + 1) * 512], k == 0, k == 7, [BygT, Bwo], [Bps])
            P.op("dve", lambda e, tm=tm, ps=ps, ct=ct, j=j: e.tensor_tensor(
                tm[:, ct * 512:(ct + 1) * 512], ps[:], gate[:, j, ct * 512:(ct + 1) * 512], ALU.mult),
                [Bps, Bgate], [Btm], join=(ct == 1))
        tt(P, "pool", x1[:], xt[:], tm[:], ALU.add, [Bxt, Btm], [Bx1])
        if last:
            s_, Bs = sr.next()
            jk, Bjk = jr.next()
            act(P, jk[:], x1[:], AF.Square, [Bx1], [Bjk, Bs], accum_out=s_[:, 0:1])
            act(P, s_[:, 1:2], s_[:, 0:1], AF.Sqrt, [Bs], [Bs], scale=1.0 / D, bias=EPS)
            recip(P, s_[:, 2:3], s_[:, 1:2], [Bs], [Bs])
            P.op("dve", lambda e, x1=x1, s_=s_, tm=tm: e.scalar_tensor_tensor(
                tm[:], x1[:], s_[:, 2:3], fng[:], ALU.mult, ALU.mult), [Bx1, Bs, Bfng], [Btm])
            P.dma("sp", io["out"][r0:r0 + 128, :], tm[:], [Btm], [Bout], join=True)
        else:
            dst = io["x1"][r0:r0 + 128, :] if own else io["xc1"][r0 - T:r0 - T + 128, :]
            P.dma("sp", dst, x1[:], [Bx1], [Bout], join=True)
    P.build()


def phase_BC1(nc, io, tag):
    P = Prog(nc, tag)
    scale = 192 ** -0.5
    NCH = S // 128 + 2
    identb, Bidb = load_common_B(P, io)
    qlT = P.sbuf("qlT", [128, 4, T], BF16); Bql = P.buf("qlT")
    P.dma("sp", qlT[:], io["q"][0:512, 0:T].rearrange("(h r) t -> r h t", r=128), [], [Bql])
    qpT = P.sbuf("qpT", [64, 4, T], BF16); Bqp = P.buf("qpT")
    P.dma("pool", qpT[:], io["q"][512:768, 0:T].rearrange("(h r) t -> r h t", r=64), [], [Bqp])
    wv = P.sbuf("wv", [128, 4, 128], BF16); Bwv = P.buf("wv")
    P.dma("pool", wv[:], io["wv"], [], [Bwv])
    kl = P.sbuf("kl", [128, NCH * 128], BF16); Bkl = P.buf("kl")
    kp = P.sbuf("kp", [64, NCH * 128], BF16); Bkp = P.buf("kp")
    vl = P.sbuf("vl", [128, NCH, 129], BF16); Bvl = P.buf("vl")
    for r in range(NCORES):
        P.dma("sp" if r % 2 else "pool", kl[:, r * T:(r + 1) * T], io["kt_all"][r, 0:128, 0:T], [], [Bkl], join=(r > 0))
        P.dma("pool" if r % 2 else "sp", kp[:, r * T:(r + 1) * T], io["kt_all"][r, 128:192, 0:T], [], [Bkp], join=(r > 0))
        P.dma("sp", vl[:, r * 16:(r + 1) * 16, :], io["v_all"][r, 0:T, 0:129].rearrange("(c p) e -> p c e", p=128),
              [], [Bvl], join=(r > 0))
    P.dma("sp", kl[:, S:S + L], io["kt_all"][0, 0:128, T:TT], [], [Bkl], join=True)
    P.dma("sp", kp[:, S:S + L], io["kt_all"][0, 128:192, T:TT], [], [Bkp], join=True)
    P.dma("pool", vl[:, 128:130, :], io["v_all"][0, T:TT, 0:129].rearrange("(c p) e -> p c e", p=128), [], [Bvl], join=True)
    psS = Rot(P, "ps", "psS", [128, 512], F32, 2)
    acc_f = [P.psum(f"acc{i}", [128, 512], F32) for i in range(4)]
    acc_t = [a[:, 0:258].rearrange("p (b e) -> p b e", b=2) for a in acc_f]
    acc_b = [P.buf(f"acc{i}", excl=True) for i in range(4)]
    ptr_ = Rot(P, "sb", "pT", [128, 512], BF16, 3)
    pst = Rot(P, "ps", "pst", [128, 8, 128], BF16, 1)
    pso = Rot(P, "ps", "pso", [128, 512], F32, 1)
    Or = Rot(P, "sb", "O", [128, 4, 129], F32, 2)
    sr = Rot(P, "sb", "s", [128, 4], F32, 3)
    ur = Rot(P, "sb", "u", [128, 128], BF16, 2)
    uTr = Rot(P, "sb", "uT", [128, 128], BF16, 2)
    ybr = Rot(P, "sb", "yb", [128, 4, 128], F32, 2)
    By = P.bufs_n("y", 5)
    par = 0
    for h in range(4):
        for qi_ in range(4):
            q0 = qi_ * 512
            accs = (acc_t[2 * par], acc_b[2 * par], acc_t[2 * par + 1], acc_b[2 * par + 1])
            full_attn_pass(P, [(lambda ch: kl[:, ch * 128:(ch + 1) * 128], qlT[:, h, q0:q0 + 512], [Bkl, Bql]),
                               (lambda ch: kp[:, ch * 128:(ch + 1) * 128], qpT[:, h, q0:q0 + 512], [Bkp, Bqp])],
                           512, list(range(NCH)), lambda ch: vl[:, ch, :], [Bvl], scale, psS, accs, ptr_, 129)
            O, BO = Or.next()
            P.op("act", lambda e, O=O, par=par: e.copy(O[:, 0:2, :], acc_t[2 * par]), [acc_b[2 * par]], [BO])
            P.op("act", lambda e, O=O, par=par: e.copy(O[:, 2:4, :], acc_t[2 * par + 1]), [acc_b[2 * par + 1]], [BO], join=True)
            par ^= 1
            yb, Byb = ybr.next()
            for b in range(4):
                s_, Bs = sr.next()
                P.op("dve", lambda e, s_=s_, O=O, b=b: e.reciprocal(s_[:, 0:1], O[:, b, 128:129]), [BO], [Bs])
                u, Bu = ur.next()
                P.op("dve", lambda e, u=u, s_=s_, O=O, b=b: e.tensor_scalar(u[:], O[:, b, 0:128], s_[:, 0:1], None, ALU.mult),
                     [BO, Bs], [Bu])
                pt, Bpt = pst.next()
                tr(P, pt[:, 0, :], u[:], identb[:], [Bu, Bidb], [Bpt])
                uT, BuT = uTr.next()
                P.op("act", lambda e, uT=uT, pt=pt: e.copy(uT[:], pt[:, 0, :]), [Bpt], [BuT])
                po, Bpo = pso.next()
                mm(P, po[:, 0:128], uT[:], wv[:, h, :], True, True, [BuT, Bwv], [Bpo])
                P.op("dve", lambda e, yb=yb, po=po, b=b: e.tensor_copy(yb[:, b, :], po[:, 0:128]), [Bpo], [Byb], join=(b > 0))
            P.dma("sp", io["y"][q0:q0 + 512, 128 * h:128 * h + 128].rearrange("(b p) c -> p b c", p=128),
                  yb[:], [Byb], [By[qi_]], join=True)
    P.build()


def bcast_mid(ap2d, n):
    a = [list(x) for x in ap2d.ap]
    return bass.AP(ap2d.tensor, ap2d.offset, [a[0], [0, n]] + a[1:])


def phase_BD1(nc, io, tag):
    P = Prog(nc, tag)
    scale = 64 ** -0.5
    NX = 22
    qdT = P.sbuf("qdT", [64, 8, T], BF16); Bqd = P.buf("qdT")
    P.dma("sp", qdT[:], io["q"][768:1280, 0:T].rearrange("(h d) t -> d h t", d=64), [], [Bqd])
    kdX = P.sbuf("kdX", [64, 8, TT], BF16); Bkd = P.buf("kdX")
    P.dma("pool", kdX[:], io["kt"][192:704, :].rearrange("(h d) t -> d h t", d=64), [], [Bkd])
    vdX = P.sbuf("vdX", [128, 18, 520], BF16); Bvd = P.buf("vdX")
    P.dma("sp", vdX[:], io["v"][:, 129:649].rearrange("(c p) e -> p c e", p=128), [], [Bvd])
    hidx = P.sbuf("hidx", [128, NIDX], I32); Bhidx = P.buf("hidx")
    P.dma("sp", hidx[:], io["hidx"], [], [Bhidx])
    kdH = P.sbuf("kdH", [64, 8, 2, 256], BF16); BkdH = P.buf("kdH")
    vdH = P.sbuf("vdH", [128, 4, 649], BF16); BvdH = P.buf("vdH")
    for side in range(2):
        for h in range(8):
            c = 6 + 8 * side + h
            P.gather(kdH[:, h, side, :], io["hk_all"], hidx[0:64, c:c + 1], [Bhidx], [BkdH], join=not (side == 0 and h == 0))
        for c2 in range(2):
            c = 22 + 2 * side + c2
            P.gather(vdH[:, 2 * side + c2, :], io["v_all"], hidx[:, c:c + 1], [Bhidx], [BvdH], join=not (side == 0 and c2 == 0))
    cm = P.sbuf("cm", [128, 64], F32); Bcm = P.buf("cm")
    P.dma("sp", cm[:], io["colmask"], [], [Bcm])
    vm = P.sbuf("vm", [128, NTB, 6, 2], F32); Bvm = P.buf("vm")
    P.dma("sp", vm[:], io["vmD"], [], [Bvm])
    TB = P.sbuf("TB", [128, 120, 64], F32); BTB = P.buf("TB")
    rp = io["rp_pad"]
    TBr = P.sbuf("TBr", [128, 120 * 64], F32); BTBr = P.buf("TBr")
    for a in range(2):
        src = bass.AP(rp.tensor, rp.offset, [[1, 64], [128, 120], [1, 64]])
        P.dma("sp" if a else "pool", TBr[64 * a:64 * a + 64, :].rearrange("p (r q) -> p r q", q=64), src, [], [BTBr], join=(a > 0))
    J2 = P.sbuf("J2", [128, 128], F32); BJ2 = P.buf("J2")
    P.dma("sp", J2[:], io["antidiag"], [], [BJ2])
    psS = Rot(P, "ps", "psS", [128, 512], F32, 3)
    psT = psS
    TBf = TB[:].rearrange("p r q -> p (r q)")
    for cchunk in range(15):
        ps, Bps = psT.next()
        mm(P, ps[:], J2[:], TBr[:, cchunk * 512:(cchunk + 1) * 512], True, True, [BJ2, BTBr], [Bps])
        P.op("act", lambda e, ps=ps, cchunk=cchunk: e.activation(TBf[:, cchunk * 512:(cchunk + 1) * 512], ps[:], AF.Exp),
             [Bps], [BTB], join=(cchunk > 0))
    tt(P, "dve", TB[:], TB[:], bcast_mid(cm[:], 120), ALU.mult, [BTB, Bcm], [BTB])
    TB4 = TB[:].rearrange("p (h r) q -> p h r q", h=8)
    accr = Rot(P, "ps", "acc", [128, 512], F32, 4)
    ptr_ = Rot(P, "sb", "pT", [128, 512], BF16, 5)
    ebr = Rot(P, "sb", "eb", [128, 8, 128], BF16, 3)
    ytr = Rot(P, "sb", "yt", [128, 512], F32, 2)
    ztr = Rot(P, "sb", "zt", [128, 8], F32, 2)
    By = P.bufs_n("y", 5)
    ebstd_t = P.sbuf("ebstd", [128, 5, 8, 128], BF16); Bebstd = P.buf("ebstd")
    ebstd = [ebstd_t[:, ci, :, :] for ci in range(5)]
    mset(P, "pool", ebstd_t[:], 0.0, [Bebstd])
    efirst = True
    for ci in range(5):
        for a in range(2):
            for b in range(2):
                dlt = -4 + 2 * ci + a - b
                if -4 <= dlt <= 3:
                    dr = dlt + 7
                    P.op("dve", lambda e, ci=ci, a=a, b=b, dr=dr: e.tensor_copy(
                        ebstd_t[64 * a:64 * a + 64, ci, :, 64 * b:64 * b + 64], TB4[64 * a:64 * a + 64, :, dr, :]),
                        [BTB, Bebstd] if efirst else [BTB], [Bebstd], join=not efirst)
                    efirst = False
    for n in range(NTB):
        q0 = n * 128
        cis = list(range(0, 6)) if n == 0 else (list(range(-1, 5)) if n == NTB - 1 else list(range(0, 5)))
        def ksrc(oc):
            if oc < 0:
                return (lambda h, oc=oc: kdH[:, h, 0, (oc + 2) * 128:(oc + 3) * 128],
                        lambda h, oc=oc: vdH[:, oc + 2, 129 + 65 * h:129 + 65 * h + 65], [BkdH, BvdH])
            if oc >= NTB and oc < NTB + 2:
                return (lambda h, oc=oc: kdH[:, h, 1, (oc - NTB) * 128:(oc - NTB + 1) * 128],
                        lambda h, oc=oc: vdH[:, 2 + oc - NTB, 129 + 65 * h:129 + 65 * h + 65], [BkdH, BvdH])
            if oc >= 100:
                oc = NTB + (oc - 100)
            return (lambda h, oc=oc: kdX[:, h, oc * 128:(oc + 1) * 128],
                    lambda h, oc=oc: vdX[:, oc, 65 * h:65 * h + 65], [Bkd, Bvd])
        chunks = [(ksrc(n + ci - 2), ci) for ci in cis] + [(ksrc(100), None), (ksrc(101), None)]
        accs = [accr.next() for _ in range(2)]
        nchk = len(chunks)
        pend = []

        def pvD(idx, hg, pT, BpT, vfn, kvb, accs=accs, nchk=nchk):
            acc_, Bacc = accs[hg]
            acc = acc_[:, 0:260].rearrange("p (g e) -> p g e", g=4)
            for hh in range(4):
                h = 4 * hg + hh
                mm(P, acc[:, hh, :], pT[:, hh * 128:(hh + 1) * 128], vfn(h),
                   idx == 0 and hh == 0, idx == nchk - 1, [BpT] + kvb, [Bacc])
        for idx, ((kfn, vfn, kvb), ci) in enumerate(chunks):
            eb = None
            if ci is not None and 2 <= n <= NTB - 3:
                eb, Beb = ebstd[ci], Bebstd
            elif ci is not None:
                eb, Beb = ebr.next()
                slot = ci - cis[0]
                first = True
                for a in range(2):
                    for b in range(2):
                        dr = 3 + 2 * ci + a - b
                        P.op("dve" if (a + b) % 2 else "pool", lambda e, eb=eb, a=a, b=b, dr=dr, n=n, slot=slot: e.tensor_scalar(
                            eb[64 * a:64 * a + 64, :, 64 * b:64 * b + 64], TB4[64 * a:64 * a + 64, :, dr, :],
                            vm[64 * a:64 * a + 64, n, slot, b:b + 1], None, ALU.mult), [BTB, Bvm], [Beb], join=not first)
                        first = False
            for hg in range(2):
                ps, Bps = psS.next()
                for hh in range(4):
                    h = 4 * hg + hh
                    mm(P, ps[:, hh * 128:(hh + 1) * 128], kfn(h), qdT[:, h, q0:q0 + 128],
                       True, True, kvb + [Bqd], [Bps])
                pT, BpT = ptr_.next()
                act(P, pT[:], ps[:], AF.Exp, [Bps], [BpT], scale=scale)
                if eb is not None:
                    P.op("dve", lambda e, pT=pT, eb=eb, hg=hg: e.tensor_tensor(
                        pT[:], pT[:], eb[:, 4 * hg:4 * hg + 4, :].rearrange("p h q -> p (h q)"), ALU.mult),
                        [BpT, Beb], [BpT])
                pend.append((idx, hg, pT, BpT, vfn, kvb))
                if len(pend) > BD1_SKEW:
                    pvD(*pend.pop(0))
        while pend:
            pvD(*pend.pop(0))
        yt, Byt = ytr.next()
        for hg in range(2):
            acc_, Bacc = accs[hg]
            acc = acc_[:, 0:260].rearrange("p (g e) -> p g e", g=4)
            zt, Bzt = ztr.next()
            P.op("dve", lambda e, zt=zt, acc=acc: e.reciprocal(zt[:, 0:4], acc[:, :, 64]), [Bacc], [Bzt])
            for hh in range(4):
                h = 4 * hg + hh
                P.op("dve", lambda e, yt=yt, acc=acc, zt=zt, hh=hh, h=h: e.tensor_scalar(
                    yt[:, h * 64:(h + 1) * 64], acc[:, hh, 0:64], zt[:, hh:hh + 1], None, ALU.mult),
                    [Bacc, Bzt], [Byt], join=not (hg == 0 and hh == 0))
        P.dma("sp", io["y"][q0:q0 + 128, 512:1024], yt[:], [Byt], [By[n // 4]], join=True)
    P.build()


def _rope_tables():
    t = np.arange(S)
    row = (t // GRID_W).astype(np.float32)
    col = (t % GRID_W).astype(np.float32)
    inv = (np.float32(10000.0) ** (-np.arange(16, dtype=np.float32) / np.float32(16))).astype(np.float32)
    ang_r = row[:, None] * inv[None, :]
    ang_c = col[:, None] * inv[None, :]
    ang = np.concatenate([ang_r, ang_r, ang_c, ang_c], axis=-1).astype(np.float32)
    return np.cos(ang).astype(np.float32), np.sin(ang).astype(np.float32)


def _rope_core_tables(r, cos, sin):
    cT = np.ones((128, TT), np.float32)
    sT = np.zeros((128, TT), np.float32)
    c = cos[r * T:(r + 1) * T].T
    s_ = (sin[r * T:(r + 1) * T] * ROPE_SIGN[None, :]).T
    cT[0:64, 0:T] = c; cT[64:128, 0:T] = c
    sT[0:64, 0:T] = s_; sT[64:128, 0:T] = s_
    return cT, sT


class IO(dict):
    pass


def _mk(nc, specs):
    io = IO()
    for (name, shape, dt, kind) in specs:
        io[name] = nc.dram_tensor(name, list(shape), dt, kind=kind).ap()
    return io


def _dt(a):
    if a.dtype == np.float32:
        return F32
    if a.dtype == NPBF:
        return BF16
    if a.dtype == np.int32:
        return I32
    raise ValueError(a.dtype)


def _launch(build, in_maps, outs):
    nc = bass.Bass("TRN2", target_bir_lowering=False)
    specs = [(k, v.shape, _dt(v), "ExternalInput") for k, v in in_maps[0].items()]
    specs += [(k, shp, dt, "ExternalOutput") for (k, shp, dt) in outs]
    io = _mk(nc, specs)
    build(nc, io)
    res = run_bass_kernel_spmd(nc, in_maps, core_ids=list(range(NCORES)))
    return res.results


def _arr_col(v):
    return np.ascontiguousarray(v.reshape(8, 128).T)


def _maskA(r):
    kk = np.arange(128)[:, None]
    qq = np.arange(128)[None, :]
    tp = np.tile((qq <= kk).astype(np.float32), (1, 4))
    tn = np.tile((kk <= qq).astype(np.float32), (1, 4))
    z = np.zeros_like(tp)
    return np.stack([z if r == 0 else tp, tp, tn, z if r == NCORES - 1 else tn]).astype(NPBF)


def _colmask():
    qc = np.arange(64)
    cs = np.clip(qc - 8, 0, 48)
    kc = np.arange(64)[:, None]
    m = ((kc >= cs[None, :]) & (kc < cs[None, :] + 16)).astype(np.float32)
    return np.ascontiguousarray(np.concatenate([m, m], 0))


def _antidiag():
    j = np.zeros((128, 128), np.float32)
    for a in range(2):
        for kc in range(64):
            j[64 * a + 63 - kc, 64 * a + kc] = 1.0
    return j


def _vmD(r):
    vm = np.zeros((128, NTB, 6, 2), np.float32)
    for n in range(NTB):
        cis = list(range(0, 6)) if n == 0 else (list(range(-1, 5)) if n == NTB - 1 else list(range(0, 5)))
        for slot, ci in enumerate(cis):
            for a in range(2):
                for b in range(2):
                    gr = 32 * r + 2 * n + b
                    rs = min(max(gr - 4, 0), 248)
                    kr = 32 * r + 2 * n - 4 + 2 * ci + a
                    if rs <= kr <= rs + 7:
                        vm[64 * a:64 * a + 64, n, slot, b] = 1.0
    return vm


def phase_AG(nc, pairs, tag):
    P = Prog(nc, tag)
    prev = []
    for i, (own, allg) in enumerate(pairs):
        b = P.buf(f"ag{i}")
        P.collective("AllGather", own, allg, prev, [b])
        prev = [b]
    P.build()


def _hidx(r):
    rp, rn = max(r - 1, 0), min(r + 1, NCORES - 1)
    p = np.arange(128, dtype=np.int64)
    cols = []
    for j in range(2):
        cols.append(rp * 256 + 128 + 64 * j + p)
    for j in range(2):
        cols.append(rn * 256 + 0 + 64 * j + p)
    cols.append(rp * TT + (T - 128) + p)
    cols.append(rn * TT + 0 + p)
    for h in range(8):
        cols.append(rp * 1024 + 512 + 64 * h + p)
    for h in range(8):
        cols.append(rn * 1024 + 0 + 64 * h + p)
    for c2 in range(2):
        cols.append(rp * TT + (T - 256) + 128 * c2 + p)
    for c2 in range(2):
        cols.append(rn * TT + 128 * c2 + p)
    a = np.stack(cols, 1)
    a[64:, 0:4] = 0
    a[64:, 6:22] = 0
    return np.ascontiguousarray(a.astype(np.int32))


def _layer_inputs(lay, sfx, norm_g, w_ada, b_ada, w_in):
    cols = []
    for c0 in lay.rope_cols:
        for hh in range(2):
            cols.append(c0 + 64 * hh + ROPE_PERM)
    cols = np.concatenate(cols)
    if lay.idx == 1:
        cols = np.concatenate([384 + ROPE_PERM, 384 + ROPE_PERM])
    return {"norm_g" + sfx: _arr_col(norm_g), "b_ada2" + sfx: np.ascontiguousarray(np.tile(b_ada.reshape(1, -1), (2, 1))),
            "w_ada" + sfx: np.ascontiguousarray(w_ada), "w_in" + sfx: np.ascontiguousarray(w_in),
            "w_rope" + sfx: np.ascontiguousarray(w_in[:, cols])}


_STOP = [None]


def build_fused(nc, ext):
    def scr(name, shape, dt):
        return nc.dram_tensor(name, list(shape), dt, kind="Internal").ap()

    common = {k: ext[k] for k in ("cc2", "cosT", "sinT", "ident", "hidx")}
    y = scr("y_scr", (TT, 1024), F32)
    x1 = scr("x1_scr", (T, 1024), F32)
    xc1 = scr("xc1_scr", (L, 1024), F32)
    xin = [ext["x"], x1]
    xcin = [ext["xc"], xc1]
    for li, lay in enumerate((Lay0, Lay1)):
        sfx = str(li)
        q = scr("q" + sfx, (lay.QROWS, TT), BF16)
        kt = scr("kt" + sfx, (lay.KTROWS, TT), BF16)
        v = scr("v" + sfx, (TT, lay.VCOLS), BF16)
        hk = scr("hk" + sfx, lay.HKSHAPE, BF16)
        sg = scr("sg" + sfx, (TT, 1024), F32)
        mod = scr("mod" + sfx, (2, 3072), F32)
        kt_all = scr("kt_all" + sfx, (NCORES * lay.KTROWS, TT), BF16)
        v_all = scr("v_all" + sfx, (NCORES * TT, lay.VCOLS), BF16)
        hk_all = scr("hk_all" + sfx, (NCORES * lay.HKSHAPE[0], lay.HKSHAPE[1]), BF16)
        ioA = IO(common)
        ioA.update(x=xin[li], xc=xcin[li], q=q, kt=kt, v=v, hk=hk, sg=sg, mod=mod)
        for k in ("norm_g", "b_ada2", "w_ada", "w_in", "w_rope"):
            ioA[k] = ext[k + sfx]
        if li == 1:
            for k in ("kvng_row", "w_qb_all", "qng_col", "wkT"):
                ioA[k] = ext[k]
        phase_A(nc, lay, ioA, f"a{li}_")
        nc.all_engine_barrier()
        if _STOP[0] == "A":
            return
        phase_AG(nc, [(kt, kt_all), (v, v_all), (hk, hk_all)], f"g{li}_")
        nc.all_engine_barrier()
        ioB = IO(common)
        ioB.update(q=q, kt=kt, v=v, sg=sg, mod=mod, y=y, x=xin[li], xc=xcin[li], hk_all=hk_all, v_all=v_all,
                   kt_all=kt_all.rearrange("(r a) c -> r a c", r=NCORES),
                   w_out=ext["w_out" + sfx])
        ioB3 = IO(ioB)
        ioB3["v_all"] = v_all.rearrange("(r a) c -> r a c", r=NCORES)
        if li == 0:
            for k in ("maskA", "a_sink", "b_lambda", "subln_g"):
                ioB[k] = ext[k]; ioB3[k] = ext[k]
            ioB["x1"] = x1; ioB["xc1"] = xc1
            if _STOP[0] == "AG":
                return
            phase_BA0(nc, ioB, "ba_")
            nc.all_engine_barrier()
            if _STOP[0] == "BA":
                return
            phase_BB0(nc, ioB3, "bb_")
            nc.all_engine_barrier()
            if _STOP[0] == "BB":
                return
            phase_BO(nc, ioB, "bo0_", last=False)
            nc.all_engine_barrier()
            if _STOP[0] == "BO":
                return
        else:
            for k in ("wv", "rp_pad", "colmask", "vmD", "antidiag", "final_g"):
                ioB[k] = ext[k]; ioB3[k] = ext[k]
            ioB["out"] = ext["out"]
            phase_BC1(nc, ioB3, "bc_")
            nc.all_engine_barrier()
            phase_BD1(nc, ioB, "bd_")
            nc.all_engine_barrier()
            phase_BO(nc, ioB, "bo1_", last=True)


def kernel(**inputs):
    inp = {k: np.asarray(v) for k, v in inputs.items()}
    cos, sin = _rope_tables()
    x = inp["x"][0]
    ident = np.eye(128, dtype=np.float32)
    cc2 = np.ascontiguousarray(np.stack([_arr_col(inp["c"].reshape(-1)), _arr_col(inp["c_ctx"].reshape(-1))], axis=-1))
    shared = dict(xc=np.ascontiguousarray(inp["ctx"][0]), cc2=cc2, ident=ident)
    shared.update(_layer_inputs(Lay0, "0", inp["ev_norm_g"][0], inp["ev_w_ada"][0], inp["ev_b_ada"][0], inp["ev_w_in"][0]))
    shared.update(_layer_inputs(Lay1, "1", inp["od_norm_g"][0], inp["od_w_ada"][0], inp["od_b_ada"][0], inp["od_w_in"][0]))
    shared.update(a_sink=np.ascontiguousarray(inp["ev_a_sink"][0].reshape(1, 8)),
                  b_lambda=np.ascontiguousarray(inp["ev_b_lambda"][0].reshape(1, 256)),
                  subln_g=np.ascontiguousarray(inp["ev_b_subln_g"][0].reshape(1, 128)),
                  w_out0=np.ascontiguousarray(inp["ev_w_out"][0]), w_out1=np.ascontiguousarray(inp["od_w_out"][0]))
    w_qb = inp["od_c_w_qb"][0]
    pe_cols = np.concatenate([192 * h + 128 + ROPE_PERM for h in range(4)])
    w_kvb = inp["od_c_w_kvb"][0].reshape(128, 4, 256)
    rpb = inp["od_d_rpb"][0]
    rp = np.zeros((8, 15, 128), np.float32)
    rp[:, :, 48:79] = rpb[:, :, ::-1]
    shared.update(kvng_row=np.ascontiguousarray(inp["od_c_kv_norm_g"][0].reshape(1, 128)),
                  w_qb_all=np.ascontiguousarray(np.concatenate([w_qb, w_qb[:, pe_cols]], axis=1)),
                  qng_col=np.ascontiguousarray(inp["od_c_q_norm_g"][0].reshape(2, 128).T),
                  wkT=np.ascontiguousarray(np.transpose(w_kvb[:, :, 0:128], (2, 1, 0))),
                  wv=np.ascontiguousarray(w_kvb[:, :, 128:256]), rp_pad=np.ascontiguousarray(rp.reshape(120, 128)),
                  colmask=_colmask(), antidiag=_antidiag(),
                  final_g=np.ascontiguousarray(inp["final_norm_g"].reshape(1, 1024)))
    maps = []
    for r in range(NCORES):
        cT, sT = _rope_core_tables(r, cos, sin)
        m = dict(shared)
        m.update(x=np.ascontiguousarray(x[r * T:(r + 1) * T]), cosT=cT, sinT=sT, hidx=_hidx(r), maskA=_maskA(r), vmD=_vmD(r))
        maps.append(m)
    res = _launch(build_fused, maps, [("out", (T, 1024), F32)])
    out = np.concatenate([res[r]["out"] for r in range(NCORES)], axis=0)
    return out.reshape(1, S, D).astype(np.float32)
```

```python
import math
import numpy as np
from contextlib import ExitStack
import ml_dtypes
import concourse.bass as bass
import concourse.mybir as mybir
from concourse.bass_utils import run_bass_kernel_spmd

F32 = mybir.dt.float32
BF16 = mybir.dt.bfloat16
I32 = mybir.dt.int32
AF = mybir.ActivationFunctionType
ALU = mybir.AluOpType
AX = mybir.AxisListType
NPBF = ml_dtypes.bfloat16

NCORES = 8
S = 16384
T = 2048
NTB = 16
L = 256
TT = T + L
D = 1024
EPS = 1e-6
GRID_W = 64
NIDX = 26
BD1_SKEW = 0


class Buf:
    __slots__ = ("name", "writers", "readers", "dma_sem", "dma_cnt", "excl")

    def __init__(self, name, excl=False):
        self.name = name
        self.excl = excl
        self.writers = []
        self.readers = []
        self.dma_sem = None
        self.dma_cnt = 0


class Op:
    __slots__ = ("eng", "emit", "deps", "signal", "sigval", "is_dma", "dsem", "dval", "cc_inc")

    def __init__(self, eng, emit, is_dma=False):
        self.eng = eng
        self.emit = emit
        self.deps = []
        self.signal = False
        self.sigval = 0
        self.is_dma = is_dma
        self.dsem = None
        self.dval = 0
        self.cc_inc = 16


class Prog:
    ENGS = ("pe", "act", "dve", "pool", "sp")

    def __init__(self, nc, tag=""):
        self.nc = nc
        self.tag = tag
        self.ops = {e: [] for e in self.ENGS}
        self.stack = ExitStack()
        self.esem = {}
        self.bufs = []
        self.dma_bufs = []
        self.sems = []

    def sem(self, name):
        h = self.nc.alloc_semaphore(name=self.tag + name)
        self.sems.append(h)
        return h

    def sbuf(self, name, shape, dt):
        return self.stack.enter_context(self.nc.sbuf_tensor(self.tag + name, shape, dt))

    def psum(self, name, shape, dt):
        return self.stack.enter_context(self.nc.psum_tensor(self.tag + name, shape, dt))

    def buf(self, name, excl=False):
        b = Buf(name, excl)
        self.bufs.append(b)
        return b

    def bufs_n(self, name, n):
        return [self.buf(f"{name}{i}") for i in range(n)]

    def _add(self, op, reads, writes, join=False):
        xr = [b for b in reads if b.excl]
        reads = [b for b in reads if not b.excl]
        xw = [b for b in writes if b.excl]
        writes = [b for b in writes if not b.excl]
        deps = []
        for b in xr + xw:
            deps.extend(b.readers)
            deps.extend(b.writers)
        for b in reads:
            deps.extend(b.writers)
        for b in writes:
            deps.extend(b.readers)
            if not join:
                deps.extend(b.writers)
        op.deps = [d for d in deps if not (d.eng == "pe" and op.eng == "pe" and not d.is_dma and not op.is_dma)]
        for b in xr + xw:
            b.writers = [op]
            b.readers = []
        for b in reads:
            b.readers.append(op)
        for b in writes:
            if join:
                b.writers.append(op)
            else:
                b.writers = [op]
            b.readers = []
        self.ops[op.eng].append(op)
        return op

    def op(self, eng, emit, reads=(), writes=(), join=False):
        return self._add(Op(eng, emit), list(reads), list(writes), join)

    def dma(self, eng, out, in_, reads, writes, join=False, emit=None, sem_buf=None, reads_extra=False, **kw):
        assert len(writes) == 1
        b = sem_buf if sem_buf is not None else (reads[0] if (len(reads) == 1 and self.outbound(out)) else writes[0])
        if b.dma_sem is None:
            b.dma_sem = self.sem("d_" + b.name)
            self.dma_bufs.append(b)
        b.dma_cnt += 16
        if emit is None:
            emit = lambda e, out=out, in_=in_, kw=kw: e.dma_start(out=out, in_=in_, **kw)
        o = Op(eng, emit, is_dma=True)
        o.dsem = b.dma_sem
        o.dval = b.dma_cnt
        extra = list(writes[0].writers) if reads_extra else []
        r = self._add(o, list(reads), list(writes), join)
        o.deps.extend(extra)
        return r

    @staticmethod
    def outbound(out_ap):
        try:
            return "DRam" in type(out_ap.tensor).__name__ or "Dram" in type(out_ap.tensor).__name__ or "DRAM" in type(out_ap.tensor).__name__
        except Exception:
            return False

    def gather(self, out, src2d, idx, reads, writes, join=False):
        def emit(e, out=out, src2d=src2d, idx=idx):
            return e.indirect_dma_start(out=out, out_offset=None, in_=src2d,
                                        in_offset=bass.IndirectOffsetOnAxis(ap=idx, axis=0))
        return self.dma("pool", None, None, reads, writes, join=join, emit=emit, sem_buf=writes[0])

    def collective(self, kind, in_ap, out_ap, reads, writes):
        def emit(e):
            return e.collective_compute(kind, ALU.bypass, replica_groups=[list(range(NCORES))], ins=[in_ap], outs=[out_ap])
        o = self.dma("pool", None, None, reads, writes, emit=emit, sem_buf=writes[0])
        b = writes[0]
        b.dma_cnt += 1 - 16
        o.dval = b.dma_cnt
        o.cc_inc = 1
        return o

    def wait_all(self, eng, bufs):
        return self._add(Op(eng, None), list(bufs), [])

    def build(self):
        nc = self.nc
        fin = Op("sp", None)
        fin.deps = []
        for b in self.dma_bufs:
            d = Op("sp", None, is_dma=True)
            d.dsem = b.dma_sem
            d.dval = b.dma_cnt
            fin.deps.append(d)
        self.ops["sp"].append(fin)
        for e in self.ENGS:
            self.esem[e] = self.sem("e_" + e)
        for e in self.ENGS:
            for o in self.ops[e]:
                for d in o.deps:
                    if not d.is_dma:
                        d.signal = True
        for e in self.ENGS:
            c = 0
            for o in self.ops[e]:
                if o.signal and not o.is_dma:
                    c += 1
                    o.sigval = c
        engobj = {"pe": "tensor", "act": "scalar", "dve": "vector", "pool": "gpsimd", "sp": "sync"}

        def make(e):
            def fn(eng):
                waited = {}
                for o in self.ops[e]:
                    need = {}
                    for d in o.deps:
                        if d.is_dma:
                            s, v = d.dsem, d.dval
                        else:
                            s, v = self.esem[d.eng], d.sigval
                        k = id(s)
                        if k not in need or need[k][1] < v:
                            need[k] = (s, v)
                    for k, (s, v) in need.items():
                        if waited.get(k, 0) >= v:
                            continue
                        waited[k] = v
                        eng.wait_ge(s, v)
                    if o.emit is None:
                        continue
                    inst = o.emit(eng)
                    if o.is_dma:
                        inst.then_inc(o.dsem, o.cc_inc)
                    elif o.signal:
                        inst.then_inc(self.esem[e], 1)
                last = max([o.sigval for o in self.ops[e]] + [0])
                if last > 0:
                    eng.wait_ge(self.esem[e], last)
            return fn

        with nc.Block() as block:
            for e in self.ENGS:
                if self.ops[e]:
                    getattr(block, engobj[e])(make(e))
        self.stack.close()
        nc.all_engine_barrier()
        nc.clear_and_free_semaphores(self.sems)
        nc.all_engine_barrier()


def mm(P, out, lhsT, rhs, start, stop, reads, writes):
    return P.op("pe", lambda e: e.matmul(out, lhsT, rhs, start=start, stop=stop, skip_group_check=True), reads, writes)


def tr(P, out, in_, ident, reads, writes):
    return P.op("pe", lambda e: e.transpose(out, in_, ident), reads, writes)


def act(P, out, in_, func, reads, writes, **kw):
    return P.op("act", lambda e: e.activation(out, in_, func, **kw), reads, writes)


def tsc(P, eng, out, in0, s1, s2, op0, op1, reads, writes):
    if s2 is None:
        return P.op(eng, lambda e: e.tensor_scalar(out, in0, s1, None, op0), reads, writes)
    return P.op(eng, lambda e: e.tensor_scalar(out, in0, s1, s2, op0, op1), reads, writes)


def tt(P, eng, out, in0, in1, op, reads, writes):
    return P.op(eng, lambda e: e.tensor_tensor(out, in0, in1, op), reads, writes)


def cp(P, eng, out, in_, reads, writes):
    if eng == "act":
        return P.op(eng, lambda e: e.copy(out, in_), reads, writes)
    return P.op(eng, lambda e: e.tensor_copy(out, in_), reads, writes)


def mset(P, eng, ap, val, writes):
    return P.op(eng, lambda e: e.memset(ap, val), [], writes)


def recip(P, out, in_, reads, writes):
    return P.op("dve", lambda e: e.reciprocal(out, in_), reads, writes)


def bcast_rows(ap_row, nparts):
    a = [list(x) for x in ap_row.ap]
    a[0] = [0, nparts]
    return bass.AP(ap_row.tensor, ap_row.offset, a)


class Rot:
    def __init__(self, P, kind, name, shape, dt, n):
        alloc = P.sbuf if kind == "sb" else P.psum
        self.t = [alloc(f"{name}{i}", shape, dt) for i in range(n)]
        self.b = [P.buf(f"{name}{i}", excl=(kind == "ps")) for i in range(n)]
        self.i = 0
        self.n = n

    def next(self):
        r = (self.t[self.i], self.b[self.i])
        self.i = (self.i + 1) % self.n
        return r


ROPE_PERM = np.concatenate([np.arange(16, 32), np.arange(0, 16), np.arange(48, 64), np.arange(32, 48)])
ROPE_SIGN = np.concatenate([-np.ones(16), np.ones(16), -np.ones(16), np.ones(16)]).astype(np.float32)


class Lay0:
    idx = 0
    C = 3328
    fm = ([(128 * i, 128, i, "q", 128 * i) for i in range(4)]
          + [(512, 128, 4, "kt", 0)]
          + [(768 + 128 * i, 128, 5 + i, "q", 512 + 128 * i) for i in range(4)]
          + [(1280 + 128 * i, 128, 9 + i, "kt", 128 + 128 * i) for i in range(4)])
    rope_cols = ([128 * i for i in range(4)] + [512] + [768 + 128 * i for i in range(4)]
                 + [1280 + 128 * i for i in range(4)])
    NR = 13
    QROWS = 1024
    KTROWS = 640
    VCOLS = 646
    tmv = [(640, 128, 2, 64, 0), (1792, 512, 4, 128, 130)]
    gcol = 2304
    halo = {("kt", 0): (0, 128, 128)}
    HKSHAPE = (256, 128)


class Lay1:
    idx = 1
    C = 3008
    fm = ([(448 + 128 * i, 128, None, "q", 768 + 128 * i) for i in range(4)]
          + [(960 + 128 * i, 128, None, "kt", 192 + 128 * i) for i in range(4)]
          + [(384, 64, 0, "kt", 128)])
    rope_cols = [384]
    NR = 1
    QROWS = 1280
    KTROWS = 704
    VCOLS = 649
    tmv = [(1472, 512, 8, 64, 129)]
    gcol = 1984
    halo = {("kt", 192 + 128 * i): (128 * i, 256, 512) for i in range(4)}
    HKSHAPE = (1024, 256)


def phase_A(nc, lay, io, tag):
    P = Prog(nc, tag)
    C = lay.C
    NRC = lay.NR * 128
    identb = P.sbuf("identb", [128, 128], BF16); Bidb = P.buf("identb")
    P.dma("pool", identb[:], io["ident"], [], [Bidb])
    identf = P.sbuf("identf", [128, 128], F32); Bidf = P.buf("identf")
    P.dma("sp", identf[:], io["ident"], [], [Bidf])
    wbf = P.sbuf("wbf", [128, 8, C], BF16); Bw = P.bufs_n("w", 8)
    wrp = P.sbuf("wrp", [128, 8, NRC], BF16); Bwr = P.bufs_n("wr", 8)
    cosT = P.sbuf("cosT", [128, TT], F32); Bcos = P.buf("cos")
    sinT = P.sbuf("sinT", [128, TT], F32); Bsin = P.buf("sin")
    P.dma("sp", cosT[:], io["cosT"], [], [Bcos])
    P.dma("sp", sinT[:], io["sinT"], [], [Bsin])
    cc = P.sbuf("cc", [128, 8, 2], F32); Bcc = P.buf("cc")
    P.dma("sp", cc[:], io["cc2"], [], [Bcc])
    ng = P.sbuf("ng", [128, 8], F32); Bng = P.buf("ng")
    P.dma("sp", ng[:], io["norm_g"], [], [Bng])
    sc = P.sbuf("sc", [128, 8, 2], F32); Bsc = P.buf("sc")
    act(P, sc[:], cc[:], AF.Silu, [Bcc], [Bsc])
    warot = Rot(P, "sb", "wada", [128, 8, 256], F32, 2)
    barot = Rot(P, "sb", "bada", [2, 256], F32, 2)
    mrrot = Rot(P, "sb", "mr", [2, 256], F32, 2)
    psA = Rot(P, "ps", "psA", [128, 512], F32, 2)
    psB = Rot(P, "ps", "psB", [128, 512], F32, 2)
    psm = psA
    pscol = P.psum("pscol", [128, 512], F32); Bpscol = P.buf("pscol", excl=True)
    w_ada_v = io["w_ada"].rearrange("(k p) c -> p k c", p=128)
    Bmodd = P.buf("mod_dram")
    for ct in range(12):
        wa, Bwa = warot.next()
        P.dma("sp", wa[:], w_ada_v[:, :, ct * 256:(ct + 1) * 256], [], [Bwa])
        ba, Bba = barot.next()
        P.dma("sp", ba[:], io["b_ada2"][:, ct * 256:(ct + 1) * 256], [], [Bba])
        ps, Bps = psm.next()
        for k in range(8):
            mm(P, ps[0:2, 0:256], sc[:, k, :], wa[:, k, :], k == 0, k == 7, [Bsc, Bwa], [Bps])
        mr, Bmr = mrrot.next()
        tt(P, "dve", mr[:], ps[0:2, 0:256], ba[:], ALU.add, [Bps, Bba], [Bmr])
        P.dma("sp", io["mod"][:, ct * 256:(ct + 1) * 256], mr[:], [Bmr], [Bmodd], join=True)
        if ct < 8:
            for cc_ in range(2):
                ch = 2 * ct + cc_
                mm(P, pscol[:, 2 * ch:2 * ch + 2], mr[0:2, cc_ * 128:(cc_ + 1) * 128], identf[0:2, 0:2], True, True,
                   [Bmr, Bidf], [Bpscol])
    for k in range(8):
        P.dma("pool", wbf[:, k, :], io["w_in"][k * 128:(k + 1) * 128, :], [], [Bw[k]])
    for k in range(8):
        P.dma("pool", wrp[:, k, :], io["w_rope"][k * 128:(k + 1) * 128, :], [], [Bwr[k]])
    ps, Bps = pscol, Bpscol
    modT = P.sbuf("modT", [128, 16, 2], F32); BmodT = P.buf("modT")
    cp(P, "dve", modT[:].rearrange("p a b -> p (a b)"), ps[:, 0:32], [Bps], [BmodT])
    Acol = P.sbuf("Acol", [128, 8, 2], F32); BA = P.buf("Acol")
    tsc(P, "dve", Acol[:].rearrange("p a b -> p (a b)"), modT[:, 8:16, :].rearrange("p a b -> p (a b)"), 1.0, None,
        ALU.add, None, [BmodT], [BA])
    for j in range(2):
        P.op("dve", lambda e, j=j: e.tensor_tensor(Acol[:, :, j], Acol[:, :, j], ng[:, :], ALU.mult), [BA, Bng], [BA])

    if io.get("_stop") == 1:
        P.build(); return
    hT = P.sbuf("hT", [128, 8, TT], BF16); BhT = P.bufs_n("hT", TT // 128)
    xrot = Rot(P, "sb", "xt", [128, 1024], F32, 2)
    junk = Rot(P, "sb", "junk", [128, 1024], BF16, 1)
    xnrot = Rot(P, "sb", "xn", [128, 1024], BF16, 2)
    strot = Rot(P, "sb", "st", [128, 4], F32, 3)
    ptr = Rot(P, "ps", "ptr", [128, 8, 128], BF16, 2)
    for i in range(io.get("_ntiles", TT // 128)):
        j = 0 if i < NTB else 1
        src = io["x"][i * 128:(i + 1) * 128, :] if i < NTB else io["xc"][(i - NTB) * 128:(i - NTB + 1) * 128, :]
        xt, Bxt = xrot.next()
        P.dma("sp", xt[:], src, [], [Bxt])
        jk, Bjk = junk.next()
        st, Bst = strot.next()
        act(P, jk[:], xt[:], AF.Square, [Bxt], [Bjk, Bst], accum_out=st[:, 0:1])
        act(P, st[:, 1:2], st[:, 0:1], AF.Sqrt, [Bst], [Bst], scale=1.0 / D, bias=EPS)
        recip(P, st[:, 2:3], st[:, 1:2], [Bst], [Bst])
        xn, Bxn = xnrot.next()
        tsc(P, "dve", xn[:], xt[:], st[:, 2:3], None, ALU.mult, None, [Bxt, Bst], [Bxn])
        pt, Bpt = ptr.next()
        for k in range(8):
            tr(P, pt[:, k, :], xn[:, k * 128:(k + 1) * 128], identb[:], [Bxn, Bidb], [Bpt])
        for k in range(8):
            dst = hT[:, k, i * 128:(i + 1) * 128]
            if i % 2 == 0:
                P.op("dve", lambda e, dst=dst, pt=pt, k=k, j=j: e.tensor_scalar(
                    dst, pt[:, k, :], Acol[:, k, j:j + 1], modT[:, k, j:j + 1], ALU.mult, ALU.add),
                    [Bpt, BA, BmodT], [BhT[i]], join=(k > 0))
            else:
                P.op("act", lambda e, dst=dst, pt=pt, k=k, j=j: e.activation(
                    dst, pt[:, k, :], AF.Identity, bias=modT[:, k, j:j + 1], scale=Acol[:, k, j:j + 1]),
                    [Bpt, BA, BmodT], [BhT[i]], join=(k > 0))

    if io.get("_stop") == 2:
        P.build(); return
    ttiles = [(0, 512), (512, 512), (1024, 512), (1536, 512), (2048, 256)]

    def hbufs(t0, n):
        return BhT[t0 // 128:(t0 + n) // 128]

    ostage = Rot(P, "sb", "ost", [128, TT], BF16, 2)
    t1rot = Rot(P, "sb", "t1", [128, 512], F32, 1)
    t2rot = Rot(P, "sb", "t2", [128, 512], F32, 1)
    dcount = [0]

    def dq():
        dcount[0] += 1
        return "sp" if dcount[0] % 2 else "pool"


    def fm_job(c0, M, dst_ap, w_t, Bw_l, rhs_fn, nk, rope=None, scale_bc=None, post=None):
        og, Bog = ostage.next()
        first = True
        for (t0, n) in ttiles:
            ps, Bps = psA.next()
            for k in range(nk):
                rhs, rb = rhs_fn(k, t0, n)
                mm(P, ps[0:M, 0:n], w_t[:, k, c0:c0 + M], rhs, k == 0, k == nk - 1, Bw_l + rb, [Bps])
            if rope is not None:
                wr_t, Bwr_l, rc0 = rope
                ps2, Bps2 = psB.next()
                for k in range(nk):
                    rhs, rb = rhs_fn(k, t0, n)
                    mm(P, ps2[0:M, 0:n], wr_t[:, k, rc0:rc0 + M], rhs, k == 0, k == nk - 1, Bwr_l + rb, [Bps2])
                t1, Bt1 = t1rot.next()
                t2, Bt2 = t2rot.next()
                tt(P, "dve", t1[0:M, 0:n], ps[0:M, 0:n], cosT[0:M, t0:t0 + n], ALU.mult, [Bps, Bcos], [Bt1])
                tt(P, "dve", t2[0:M, 0:n], ps2[0:M, 0:n], sinT[0:M, t0:t0 + n], ALU.mult, [Bps2, Bsin], [Bt2])
                if scale_bc is None:
                    P.op("pool", lambda e, og=og, t1=t1, t2=t2, t0=t0, n=n: e.tensor_tensor(
                        og[0:M, t0:t0 + n], t1[0:M, 0:n], t2[0:M, 0:n], ALU.add), [Bt1, Bt2], [Bog], join=not first)
                else:
                    sbt, Bsb = scale_bc
                    tt(P, "pool", t1[0:M, 0:n], t1[0:M, 0:n], t2[0:M, 0:n], ALU.add, [Bt1, Bt2], [Bt1])
                    P.op("pool", lambda e, og=og, t1=t1, t0=t0, n=n, sbt=sbt: e.tensor_tensor(
                        og[0:M, t0:t0 + n], t1[0:M, 0:n], sbt[0:M, t0:t0 + n], ALU.mult), [Bt1, Bsb], [Bog],
                        join=not first)
            elif post is not None:
                post(ps, Bps, og, Bog, t0, n, first)
            elif scale_bc is not None:
                sbt, Bsb = scale_bc
                P.op("dve", lambda e, og=og, ps=ps, t0=t0, n=n, sbt=sbt: e.tensor_tensor(
                    og[0:M, t0:t0 + n], ps[0:M, 0:n], sbt[0:M, t0:t0 + n], ALU.mult), [Bps, Bsb], [Bog],
                    join=not first)
            else:
                P.op("act", lambda e, og=og, ps=ps, t0=t0, n=n: e.copy(og[0:M, t0:t0 + n], ps[0:M, 0:n]),
                     [Bps], [Bog], join=not first)
            first = False
        if dst_ap is not None:
            P.dma(dq(), dst_ap, og[0:M, :], [Bog], [Bfmout], join=True)
        return og, Bog

    Bfmout = P.buf("fmout"); Bvout = P.buf("vout"); Bgout = P.buf("gout")

    def h_rhs(k, t0, n):
        return hT[:, k, t0:t0 + n], hbufs(t0, n)

    for (c0, M, ridx, dname, drow) in lay.fm:
        og, Bog = fm_job(c0, M, io[dname][drow:drow + M, :], wbf, Bw, h_rhs, 8,
                         rope=None if ridx is None else (wrp, Bwr, ridx * 128))
        hp = lay.halo.get((dname, drow))
        if hp is not None:
            hrow, hw, hrows = hp
            P.dma(dq(), io["hk"][hrow:hrow + M, :], og[0:M, 0:hw], [Bog], [Bfmout], join=True)
            P.dma(dq(), io["hk"][hrows + hrow:hrows + hrow + M, :], og[0:M, T - hw:T], [Bog], [Bfmout], join=True)

    if io.get("_stop") == 3:
        P.build(); return
    vst = Rot(P, "sb", "vst", [128, lay.VCOLS], BF16, 2)
    gst = Rot(P, "sb", "gst", [128, 1024], F32, 2)
    for i in range(2):
        mset(P, "pool", vst.t[i][:], 1.0, [vst.b[i]])

    def tm_mm(i, c0, ncols):
        ps, Bps = psA.next()
        for k in range(8):
            mm(P, ps[:, 0:ncols], hT[:, k, i * 128:(i + 1) * 128], wbf[:, k, c0:c0 + ncols], k == 0, k == 7,
               [BhT[i]] + Bw, [Bps])
        return ps, Bps

    if lay.idx == 1:
        gkvb = P.sbuf("gkvb", [128, 128], F32); Bgkvb = P.buf("gkvb")
        P.dma("sp", gkvb[:], bcast_rows(io["kvng_row"], 128), [], [Bgkvb])
        ckT = P.sbuf("ckT", [128, TT], BF16); BckT = P.buf("ckT")
        st2 = Rot(P, "sb", "st2", [128, 4], F32, 3)
        jk2 = Rot(P, "sb", "jk2", [128, 128], F32, 2)
        ptc = ptr

    for i in range(TT // 128):
        vt, Bvt = vst.next()
        wfirst = True
        for (c0, ncols, nh, e, dcol) in lay.tmv:
            ps, Bps = tm_mm(i, c0, ncols)
            dstv = vt[:, dcol:dcol + nh * (e + 1)].rearrange("p (h e) -> p h e", e=e + 1)[:, :, 0:e]
            srcv = ps[:, 0:ncols].rearrange("p (h e) -> p h e", e=e)
            P.op("act", lambda en, dstv=dstv, srcv=srcv: en.copy(dstv, srcv), [Bps], [Bvt], join=not wfirst)
            wfirst = False
        if lay.idx == 1:
            ps, Bps = tm_mm(i, 256, 128)
            s2, Bs2 = st2.next()
            j2, Bj2 = jk2.next()
            act(P, j2[:], ps[:, 0:128], AF.Square, [Bps], [Bj2, Bs2], accum_out=s2[:, 0:1])
            act(P, s2[:, 1:2], s2[:, 0:1], AF.Sqrt, [Bs2], [Bs2], scale=1.0 / 128, bias=EPS)
            recip(P, s2[:, 2:3], s2[:, 1:2], [Bs2], [Bs2])
            P.op("dve", lambda en, vt=vt, ps=ps, s2=s2: en.scalar_tensor_tensor(
                vt[:, 0:128], ps[:, 0:128], s2[:, 2:3], gkvb[:], ALU.mult, ALU.mult), [Bps, Bs2, Bgkvb], [Bvt], join=True)
            pc, Bpc = ptc.next()
            tr(P, pc[:, 0, :], vt[:, 0:128], identb[:], [Bvt, Bidb], [Bpc])
            P.op("act", lambda en, pc=pc, i=i: en.copy(ckT[:, i * 128:(i + 1) * 128], pc[:, 0, :]), [Bpc], [BckT], join=True)
        P.dma(dq(), io["v"][i * 128:(i + 1) * 128, :], vt[:], [Bvt], [Bvout], join=True)
        gt, Bgt = gst.next()
        for hh in range(2):
            ps, Bps = tm_mm(i, lay.gcol + 512 * hh, 512)
            P.op("act", lambda en, gt=gt, ps=ps, hh=hh: en.activation(gt[:, hh * 512:(hh + 1) * 512], ps[:], AF.Silu),
                 [Bps], [Bgt], join=(hh == 1))
        P.dma(dq(), io["sg"][i * 128:(i + 1) * 128, :], gt[:], [Bgt], [Bgout], join=True)

    if lay.idx == 1:
        P.dma(dq(), io["kt"][0:128, :], ckT[:], [BckT], [Bfmout], join=True)
        cqT = P.sbuf("cqT", [128, 2, TT], BF16); BcqT = P.buf("cqT")
        sqr = Rot(P, "sb", "sqr", [128, 2, 512], F32, 1)
        rsq = P.sbuf("rsq", [128, TT], F32); Brsq = P.buf("rsq")
        onesf = P.sbuf("onesf", [128, 128], F32); Bones = P.buf("onesf")
        mset(P, "pool", onesf[:], 1.0, [Bones])
        first = True
        for (t0, n) in ttiles:
            sq, Bsq = sqr.next()
            for kk in range(2):
                ps, Bps = psA.next()
                for k in range(8):
                    mm(P, ps[:, 0:n], wbf[:, k, kk * 128:(kk + 1) * 128], hT[:, k, t0:t0 + n], k == 0, k == 7,
                       Bw + hbufs(t0, n), [Bps])
                P.op("act", lambda e, ps=ps, kk=kk, t0=t0, n=n: e.copy(cqT[:, kk, t0:t0 + n], ps[:, 0:n]),
                     [Bps], [BcqT], join=not (first and kk == 0))
                P.op("dve", lambda e, ps=ps, sq=sq, kk=kk, n=n: e.tensor_copy(sq[:, kk, 0:n], ps[:, 0:n]),
                     [Bps], [Bsq], join=(kk == 1))
            P.op("pool", lambda e, sq=sq, n=n: e.tensor_tensor(sq[:, :, 0:n], sq[:, :, 0:n], sq[:, :, 0:n], ALU.mult),
                 [Bsq], [Bsq])
            ps, Bps = psB.next()
            for kk in range(2):
                mm(P, ps[:, 0:n], onesf[:], sq[:, kk, 0:n], kk == 0, kk == 1, [Bones, Bsq], [Bps])
            P.op("act", lambda e, ps=ps, t0=t0, n=n: e.activation(rsq[:, t0:t0 + n], ps[:, 0:n], AF.Sqrt,
                                                                scale=1.0 / 256, bias=EPS), [Bps], [Brsq], join=not first)
            first = False
        recip(P, rsq[:], rsq[:], [Brsq], [Brsq])
        wqf = P.sbuf("wqf", [128, 1024], F32); Bwqf = P.buf("wqf")
        qng = P.sbuf("qng", [128, 2], F32); Bqng = P.buf("qng")
        P.dma("sp", qng[:], io["qng_col"], [], [Bqng])
        wqb = P.sbuf("wqb", [128, 2, 1024], BF16); Bwqb = P.buf("wqb")
        for kk in range(2):
            P.dma("sp", wqf[:], io["w_qb_all"][kk * 128:(kk + 1) * 128, :], [], [Bwqf])
            P.op("dve", lambda e, kk=kk: e.tensor_scalar(wqb[:, kk, :], wqf[:], qng[:, kk:kk + 1], None, ALU.mult),
                 [Bwqf, Bqng], [Bwqb], join=(kk == 1))
        wkT = P.sbuf("wkT", [128, 4, 128], BF16); BwkT = P.buf("wkT")
        P.dma("pool", wkT[:], io["wkT"], [], [BwkT])

        def cq_rhs(k, t0, n):
            return cqT[:, k, t0:t0 + n], [BcqT]

        qn_rot = Rot(P, "sb", "qn", [128, 512], BF16, 2)
        for h in range(4):
            def post(ps, Bps, og, Bog, t0, n, first, h=h):
                qn, Bqn = qn_rot.next()
                tt(P, "dve", qn[:, 0:n], ps[:, 0:n], rsq[:, t0:t0 + n], ALU.mult, [Bps, Brsq], [Bqn])
                ps3, Bps3 = psB.next()
                mm(P, ps3[:, 0:n], wkT[:, h, :], qn[:, 0:n], True, True, [BwkT, Bqn], [Bps3])
                P.op("act", lambda e, og=og, ps3=ps3, t0=t0, n=n: e.copy(og[:, t0:t0 + n], ps3[:, 0:n]),
                     [Bps3], [Bog], join=not first)
            fm_job(192 * h, 128, io["q"][128 * h:128 * h + 128, :], wqb, [Bwqb], cq_rhs, 2, post=post)
            fm_job(192 * h + 128, 64, io["q"][512 + 64 * h:512 + 64 * h + 64, :], wqb, [Bwqb], cq_rhs, 2,
                   rope=(wqb, [Bwqb], 768 + 64 * h), scale_bc=(rsq, Brsq))
    P.build()


def load_common_B(P, io):
    identb = P.sbuf("identb", [128, 128], BF16); Bidb = P.buf("identb")
    P.dma("pool", identb[:], io["ident"], [], [Bidb])
    return identb, Bidb


def phase_BA0(nc, io, tag):
    P = Prog(nc, tag)
    scale = 64 ** -0.5
    qaT = P.sbuf("qaT", [64, 8, TT], BF16); Bqa = P.buf("qaT")
    P.dma("sp", qaT[:], io["q"][0:512, :].rearrange("(h d) t -> d h t", d=64), [], [Bqa])
    kaT = P.sbuf("kaT", [64, 2, TT], BF16); Bka = P.buf("kaT")
    P.dma("pool", kaT[:], io["kt"][0:128, :].rearrange("(j d) t -> d j t", d=64), [], [Bka])
    vaX = P.sbuf("vaX", [128, 18, 130], BF16); Bva = P.buf("vaX")
    P.dma("sp", vaX[:], io["v"][:, 0:130].rearrange("(c p) e -> p c e", p=128), [], [Bva])
    hidx = P.sbuf("hidx", [128, NIDX], I32); Bhidx = P.buf("hidx")
    P.dma("sp", hidx[:], io["hidx"], [], [Bhidx])
    kaH = P.sbuf("kaH", [64, 2, 2, 128], BF16); BkaH = P.buf("kaH")
    vaH = P.sbuf("vaH", [128, 2, 646], BF16); BvaH = P.buf("vaH")
    for side in range(2):
        for j in range(2):
            P.gather(kaH[:, j, side, :], io["hk_all"], hidx[0:64, 2 * side + j:2 * side + j + 1], [Bhidx], [BkaH],
                     join=not (side == 0 and j == 0))
        P.gather(vaH[:, side, :], io["v_all"], hidx[:, 4 + side:5 + side], [Bhidx], [BvaH], join=(side > 0))
    mk = P.sbuf("mk", [128, 4, 512], BF16); Bmk = P.buf("mk")
    P.dma("pool", mk[:], io["maskA"].rearrange("m p f -> p m f"), [], [Bmk])
    sk = P.sbuf("sk", [128, 8], F32); Bsk = P.buf("sk")
    P.dma("sp", sk[:], bcast_rows(io["a_sink"], 128), [], [Bsk])
    esk = P.sbuf("esk", [128, 8], F32); Besk = P.buf("esk")
    act(P, esk[:], sk[:], AF.Exp, [Bsk], [Besk])
    psS = Rot(P, "ps", "psS", [128, 512], F32, 2)
    accr = Rot(P, "ps", "acc", [128, 512], F32, 2)
    ptr_ = Rot(P, "sb", "pT", [128, 512], BF16, 3)
    ytr = Rot(P, "sb", "yt", [128, 512], F32, 2)
    ztr = Rot(P, "sb", "zt", [128, 8], F32, 2)
    By = P.bufs_n("y", 5)
    for n in range(TT // 128):
        own = n < NTB
        q0 = n * 128
        yt, Byt = ytr.next()
        for j in range(2):
            def kown(c, j=j):
                return (kaT[:, j, c * 128:(c + 1) * 128], vaX[:, c, 65 * j:65 * j + 65], [Bka, Bva])

            def khalo(side, j=j):
                return (kaH[:, j, side, :], vaH[:, side, 65 * j:65 * j + 65], [BkaH, BvaH])
            if own:
                chunks = [(khalo(0) if n == 0 else kown(n - 1), 0 if n == 0 else 1), (kown(n), None),
                          (khalo(1) if n == NTB - 1 else kown(n + 1), 3 if n == NTB - 1 else 2),
                          (kown(16), None), (kown(17), None)]
            else:
                chunks = [(kown(16), None), (kown(17), None)]
            acc_, Bacc = accr.next()
            acc = acc_[:, 0:260].rearrange("p (g e) -> p g e", g=4)
            for ci, ((kap, vap, kvb), m) in enumerate(chunks):
                ps, Bps = psS.next()
                mm(P, ps[:, :].rearrange("p (g q) -> p g q", g=4), kap,
                   qaT[:, 4 * j:4 * j + 4, q0:q0 + 128], True, True, kvb + [Bqa], [Bps])
                pT, BpT = ptr_.next()
                act(P, pT[:], ps[:], AF.Exp, [Bps], [BpT], scale=scale)
                if m is not None:
                    tt(P, "dve", pT[:], pT[:], mk[:, m, :], ALU.mult, [BpT, Bmk], [BpT])
                for g in range(4):
                    mm(P, acc[:, g, :], pT[:, g * 128:(g + 1) * 128], vap,
                       ci == 0 and g == 0, ci == len(chunks) - 1, [BpT] + kvb, [Bacc])
            zt, Bzt = ztr.next()
            tt(P, "dve", zt[:, 0:4], acc[:, :, 64], esk[:, 4 * j:4 * j + 4], ALU.add, [Bacc, Besk], [Bzt])
            recip(P, zt[:, 4:8], zt[:, 0:4], [Bzt], [Bzt])
            for g in range(4):
                hd = 4 * j + g
                P.op("dve", lambda e, yt=yt, acc=acc, zt=zt, g=g, hd=hd: e.tensor_scalar(
                    yt[:, hd * 64:(hd + 1) * 64], acc[:, g, 0:64], zt[:, 4 + g:5 + g], None, ALU.mult),
                    [Bacc, Bzt], [Byt], join=not (j == 0 and g == 0))
        P.dma("sp", io["y"][q0:q0 + 128, 0:512], yt[:], [Byt], [By[n // 4]], join=True)
    P.build()


def full_attn_pass(P, qk_list, nq, chunks, vaug_fn, vbufs, scale, psS, accs, ptr_, E1):
    nb = nq // 128
    a0, Ba0, a1, Ba1 = accs
    nchk = len(chunks)

    def pv(ci, ch, pT, BpT):
        for b in range(nb):
            at, Bat = (a0, Ba0) if b < 2 else (a1, Ba1)
            mm(P, at[:, b % 2, :], pT[:, b * 128:(b + 1) * 128], vaug_fn(ch), ci == 0 and b % 2 == 0, ci == nchk - 1,
               [BpT] + vbufs, [Bat])

    skew = max(1, psS.n - 1)
    pend = []
    for ci, ch in enumerate(chunks):
        ps, Bps = psS.next()
        for qi, (kT_fn, qT, bl) in enumerate(qk_list):
            mm(P, ps[:, 0:nq], kT_fn(ch), qT, qi == 0, qi == len(qk_list) - 1, bl, [Bps])
        pT, BpT = ptr_.next()
        act(P, pT[:, 0:nq], ps[:, 0:nq], AF.Exp, [Bps], [BpT], scale=scale)
        pend.append((ci, ch, pT, BpT))
        if len(pend) > skew:
            pv(*pend.pop(0))
    while pend:
        pv(*pend.pop(0))


def phase_BB0(nc, io, tag):
    P = Prog(nc, tag)
    scale = 64 ** -0.5
    lam_init = 0.8 - 0.6 * math.exp(-0.3 * 0)
    NCH = S // 128 + 2
    qbT = P.sbuf("qbT", [128, 2, 4, TT], BF16); Bqb = P.buf("qbT")
    mset(P, "pool", qbT[:], 0.0, [Bqb])
    qsrc = io["q"][512:1024, :].rearrange("(h r) t -> r h t", r=128)
    P.dma("sp", qbT[0:64, 0, :, :], qsrc[0:64], [], [Bqb], join=True, sem_buf=Bqb, reads_extra=True)
    P.dma("sp", qbT[64:128, 1, :, :], qsrc[64:128], [], [Bqb], join=True, sem_buf=Bqb, reads_extra=True)
    lb = P.sbuf("lb", [128, 256], F32); Blb = P.buf("lb")
    P.dma("sp", lb[:], bcast_rows(io["b_lambda"], 128), [], [Blb])
    lt = P.sbuf("lt", [128, 128], F32); Blt = P.buf("lt")
    ls = P.sbuf("ls", [128, 8], F32); Bls = P.buf("ls")
    tt(P, "dve", lt[:].rearrange("p (a b) -> p a b", a=2), lb[:].rearrange("p (a b c) -> p a b c", a=2, b=2)[:, :, 0, :],
       lb[:].rearrange("p (a b c) -> p a b c", a=2, b=2)[:, :, 1, :], ALU.mult, [Blb], [Blt])
    P.op("dve", lambda e: e.reduce_sum(ls[:, 0:2], lt[:].rearrange("p (a b) -> p a b", a=2), AX.X), [Blt], [Bls])
    act(P, ls[:, 2:4], ls[:, 0:2], AF.Exp, [Bls], [Bls])
    tt(P, "dve", ls[:, 4:5], ls[:, 3:4], ls[:, 2:3], ALU.subtract, [Bls], [Bls])
    tsc(P, "dve", ls[:, 5:6], ls[:, 4:5], -lam_init, None, ALU.add, None, [Bls], [Bls])
    sgl = P.sbuf("sgl", [128, 128], F32); Bsgl = P.buf("sgl")
    P.dma("sp", sgl[:], bcast_rows(io["subln_g"], 128), [], [Bsgl])
    tsc(P, "dve", sgl[:], sgl[:], 1.0 - lam_init, None, ALU.mult, None, [Bsgl], [Bsgl])

    kbr = Rot(P, "sb", "kb", [128, NCH * 128], BF16, 2)
    vbr = Rot(P, "sb", "vb", [128, NCH, 129], BF16, 2)
    psS = Rot(P, "ps", "psS", [128, 512], F32, 3)
    acc_f = [P.psum(f"acc{i}", [128, 512], F32) for i in range(4)]
    acc_t = [a[:, 0:258].rearrange("p (b e) -> p b e", b=2) for a in acc_f]
    acc_b = [P.buf(f"acc{i}", excl=True) for i in range(4)]
    ptr_ = Rot(P, "sb", "pT", [128, 512], BF16, 4)
    Or = [P.sbuf(f"O{t}", [128, 4, 129], F32) for t in range(2)]
    BO = [P.buf(f"O{t}") for t in range(2)]
    ur = Rot(P, "sb", "u", [128, 128], F32, 2)
    jr = Rot(P, "sb", "jk", [128, 128], F32, 2)
    sr = Rot(P, "sb", "s", [128, 8], F32, 3)
    ybr = Rot(P, "sb", "yb", [128, 4, 128], F32, 2)
    By = P.bufs_n("y", 5)
    qtiles = [(0, 512), (512, 512), (1024, 512), (1536, 512), (2048, 256)]
    for h in range(4):
        kb, Bkb = kbr.next()
        vb, Bvb = vbr.next()
        for r in range(NCORES):
            P.dma("sp" if r % 2 else "pool", kb[:, r * T:(r + 1) * T], io["kt_all"][r, 128 + 128 * h:256 + 128 * h, 0:T],
                  [], [Bkb], join=(r > 0))
            P.dma("pool" if r % 2 else "sp", vb[:, r * 16:(r + 1) * 16, :],
                  io["v_all"][r, 0:T, 130 + 129 * h:259 + 129 * h].rearrange("(c p) e -> p c e", p=128),
                  [], [Bvb], join=(r > 0))
        P.dma("sp", kb[:, S:S + L], io["kt_all"][0, 128 + 128 * h:256 + 128 * h, T:TT], [], [Bkb], join=True)
        P.dma("pool", vb[:, 128:130, :], io["v_all"][0, T:TT, 130 + 129 * h:259 + 129 * h].rearrange("(c p) e -> p c e", p=128),
              [], [Bvb], join=True)
        for qi_, (q0, nq) in enumerate(qtiles):
            nb = nq // 128
            chunks = list(range(NCH)) if q0 < T else [128, 129]
            for t in range(2):
                accs = (acc_t[2 * t], acc_b[2 * t], acc_t[2 * t + 1], acc_b[2 * t + 1])
                full_attn_pass(P, [(lambda ch, kb=kb: kb[:, ch * 128:(ch + 1) * 128],
                                    qbT[:, t, h, q0:q0 + nq], [Bkb, Bqb])],
                               nq, chunks, lambda ch, vb=vb: vb[:, ch, :], [Bvb], scale, psS, accs, ptr_, 129)
                P.op("act", lambda e, t=t: e.copy(Or[t][:, 0:2, :], acc_t[2 * t]), [acc_b[2 * t]], [BO[t]])
                if nb > 2:
                    P.op("act", lambda e, t=t: e.copy(Or[t][:, 2:4, :], acc_t[2 * t + 1]), [acc_b[2 * t + 1]], [BO[t]], join=True)
            yb, Byb = ybr.next()
            for b in range(nb):
                s_, Bs = sr.next()
                P.op("dve", lambda e, s_=s_, b=b: e.reciprocal(s_[:, 0:1], Or[0][:, b, 128:129]), [BO[0]], [Bs])
                P.op("dve", lambda e, s_=s_, b=b: e.reciprocal(s_[:, 1:2], Or[1][:, b, 128:129]), [BO[1]], [Bs])
                tt(P, "dve", s_[:, 2:3], s_[:, 1:2], ls[:, 5:6], ALU.mult, [Bs, Bls], [Bs])
                u, Bu = ur.next()
                P.op("dve", lambda e, u=u, s_=s_, b=b: e.tensor_scalar(u[:], Or[0][:, b, 0:128], s_[:, 0:1], None, ALU.mult),
                     [BO[0], Bs], [Bu])
                P.op("dve", lambda e, u=u, s_=s_, b=b: e.scalar_tensor_tensor(u[:], Or[1][:, b, 0:128], s_[:, 2:3], u[:],
                                                                               ALU.mult, ALU.add), [BO[1], Bs, Bu], [Bu])
                jk, Bjk = jr.next()
                act(P, jk[:], u[:], AF.Square, [Bu], [Bjk, Bs], accum_out=s_[:, 3:4])
                act(P, s_[:, 4:5], s_[:, 3:4], AF.Sqrt, [Bs], [Bs], scale=1.0 / 128, bias=EPS)
                recip(P, s_[:, 5:6], s_[:, 4:5], [Bs], [Bs])
                P.op("dve", lambda e, yb=yb, u=u, s_=s_, b=b: e.scalar_tensor_tensor(
                    yb[:, b, :], u[:], s_[:, 5:6], sgl[:], ALU.mult, ALU.mult), [Bu, Bs, Bsgl], [Byb], join=(b > 0))
            P.dma("sp", io["y"][q0:q0 + nq, 512 + 128 * h:640 + 128 * h].rearrange("(b p) c -> p b c", p=128),
                  yb[:, 0:nb, :], [Byb], [By[qi_]], join=True)
    P.build()


def phase_BO(nc, io, tag, last):
    P = Prog(nc, tag)
    identb, Bidb = load_common_B(P, io)
    wo = P.sbuf("wo", [128, 8, 1024], BF16); Bwo = P.buf("wo")
    P.dma("pool", wo[:], io["w_out"].rearrange("(k p) c -> p k c", p=128), [], [Bwo])
    gate = P.sbuf("gate", [128, 2, 1024], F32); Bgate = P.buf("gate")
    for j in range(2):
        P.dma("sp", gate[:, j, :], bcast_rows(io["mod"][j:j + 1, 2048:3072], 128), [], [Bgate], join=(j > 0))
    if last:
        fng = P.sbuf("fng", [128, 1024], F32); Bfng = P.buf("fng")
        P.dma("sp", fng[:], bcast_rows(io["final_g"], 128), [], [Bfng])
    yr = Rot(P, "sb", "yt", [128, 1024], F32, 3)
    gr = Rot(P, "sb", "gt", [128, 1024], F32, 3)
    xr = Rot(P, "sb", "xt", [128, 1024], F32, 3)
    ygr = Rot(P, "sb", "yg", [128, 1024], BF16, 3)
    ptr_ = Rot(P, "ps", "pt", [128, 8, 128], BF16, 2)
    ygTr = Rot(P, "sb", "ygT", [128, 8, 128], BF16, 2)
    pso = Rot(P, "ps", "pso", [128, 512], F32, 3)
    tmr = Rot(P, "sb", "tm", [128, 1024], F32, 3)
    x1r = Rot(P, "sb", "x1", [128, 1024], F32, 3)
    sr = Rot(P, "sb", "s", [128, 4], F32, 3)
    jr = Rot(P, "sb", "jk", [128, 1024], BF16, 2)
    Bout = P.buf("outd")
    nblk = NTB if last else TT // 128
    for i in range(nblk):
        own = i < NTB
        j = 0 if own else 1
        r0 = i * 128
        yt, Byt = yr.next(); gt, Bgt = gr.next(); xt, Bxt = xr.next()
        P.dma("sp", yt[:], io["y"][r0:r0 + 128, :], [], [Byt])
        P.dma("pool", gt[:], io["sg"][r0:r0 + 128, :], [], [Bgt])
        P.dma("sp", xt[:], io["x"][r0:r0 + 128, :] if own else io["xc"][r0 - T:r0 - T + 128, :], [], [Bxt])
        yg, Byg = ygr.next()
        tt(P, "pool" if i % 2 else "dve", yg[:], yt[:], gt[:], ALU.mult, [Byt, Bgt], [Byg])
        pt, Bpt = ptr_.next()
        for k in range(8):
            tr(P, pt[:, k, :], yg[:, k * 128:(k + 1) * 128], identb[:], [Byg, Bidb], [Bpt])
        ygT, BygT = ygTr.next()
        if i % 2 == 0:
            P.op("act", lambda e, ygT=ygT, pt=pt: e.copy(ygT[:], pt[:]), [Bpt], [BygT])
        else:
            P.op("dve", lambda e, ygT=ygT, pt=pt: e.tensor_copy(ygT[:], pt[:]), [Bpt], [BygT])
        tm, Btm = tmr.next()
        x1, Bx1 = x1r.next()
        for ct in range(2):
            ps, Bps = pso.next()
            for k in range(8):
                mm(P, ps[:], ygT[:, k, :], wo[:, k, ct * 512:(ct + 1) * 512], k == 0, k == 7, [BygT, Bwo], [Bps])
            P.op("dve", lambda e, tm=tm, ps=ps, ct=ct, j=j: e.tensor_tensor(
                tm[:, ct * 512:(ct + 1) * 512], ps[:], gate[:, j, ct * 512:(ct + 1) * 512], ALU.mult),
                [Bps, Bgate], [Btm], join=(ct == 1))
        tt(P, "dve" if i % 2 else "pool", x1[:], xt[:], tm[:], ALU.add, [Bxt, Btm], [Bx1])
        if last:
            s_, Bs = sr.next()
            jk, Bjk = jr.next()
            act(P, jk[:], x1[:], AF.Square, [Bx1], [Bjk, Bs], accum_out=s_[:, 0:1])
            act(P, s_[:, 1:2], s_[:, 0:1], AF.Sqrt, [Bs], [Bs], scale=1.0 / D, bias=EPS)
            recip(P, s_[:, 2:3], s_[:, 1:2], [Bs], [Bs])
            P.op("dve", lambda e, x1=x1, s_=s_, tm=tm: e.scalar_tensor_tensor(
                tm[:], x1[:], s_[:, 2:3], fng[:], ALU.mult, ALU.mult), [Bx1, Bs, Bfng], [Btm])
            P.dma("sp", io["out"][r0:r0 + 128, :], tm[:], [Btm], [Bout], join=True)
        else:
            dst = io["x1"][r0:r0 + 128, :] if own else io["xc1"][r0 - T:r0 - T + 128, :]
            P.dma("sp", dst, x1[:], [Bx1], [Bout], join=True)
    P.build()


def phase_BC1(nc, io, tag):
    P = Prog(nc, tag)
    scale = 192 ** -0.5
    NCH = S // 128 + 2
    identb, Bidb = load_common_B(P, io)
    qlT = P.sbuf("qlT", [128, 4, T], BF16); Bql = P.buf("qlT")
    P.dma("sp", qlT[:], io["q"][0:512, 0:T].rearrange("(h r) t -> r h t", r=128), [], [Bql])
    qpT = P.sbuf("qpT", [128, 4, T], BF16); Bqp = P.buf("qpT")
    mset(P, "pool", qpT[64:128, :, :], 0.0, [Bqp])
    P.dma("pool", qpT[0:64, :, :], io["q"][512:768, 0:T].rearrange("(h r) t -> r h t", r=64), [], [Bqp], join=True,
          sem_buf=Bqp)
    wv = P.sbuf("wv", [128, 4, 128], BF16); Bwv = P.buf("wv")
    P.dma("pool", wv[:], io["wv"], [], [Bwv])
    kl = P.sbuf("kl", [128, NCH * 128], BF16); Bkl = P.buf("kl")
    kp = P.sbuf("kp", [128, NCH * 128], BF16); Bkp = P.buf("kp")
    mset(P, "pool", kp[64:128, :], 0.0, [Bkp])
    vl = P.sbuf("vl", [128, NCH, 129], BF16); Bvl = P.buf("vl")
    for r in range(NCORES):
        P.dma("sp" if r % 2 else "pool", kl[:, r * T:(r + 1) * T], io["kt_all"][r, 0:128, 0:T], [], [Bkl], join=(r > 0))
        P.dma("pool" if r % 2 else "sp", kp[0:64, r * T:(r + 1) * T], io["kt_all"][r, 128:192, 0:T], [], [Bkp], join=True, sem_buf=Bkp)
        P.dma("sp", vl[:, r * 16:(r + 1) * 16, :], io["v_all"][r, 0:T, 0:129].rearrange("(c p) e -> p c e", p=128),
              [], [Bvl], join=(r > 0))
    P.dma("sp", kl[:, S:S + L], io["kt_all"][0, 0:128, T:TT], [], [Bkl], join=True)
    P.dma("sp", kp[0:64, S:S + L], io["kt_all"][0, 128:192, T:TT], [], [Bkp], join=True, sem_buf=Bkp)
    P.dma("pool", vl[:, 128:130, :], io["v_all"][0, T:TT, 0:129].rearrange("(c p) e -> p c e", p=128), [], [Bvl], join=True)
    psS = Rot(P, "ps", "psS", [128, 512], F32, 2)
    acc_f = [P.psum(f"acc{i}", [128, 512], F32) for i in range(4)]
    acc_t = [a[:, 0:258].rearrange("p (b e) -> p b e", b=2) for a in acc_f]
    acc_b = [P.buf(f"acc{i}", excl=True) for i in range(4)]
    ptr_ = Rot(P, "sb", "pT", [128, 512], BF16, 3)
    pst = Rot(P, "ps", "pst", [128, 8, 128], BF16, 1)
    pso = Rot(P, "ps", "pso", [128, 512], F32, 1)
    Or = Rot(P, "sb", "O", [128, 4, 129], F32, 2)
    sr = Rot(P, "sb", "s", [128, 4], F32, 3)
    ur = Rot(P, "sb", "u", [128, 128], BF16, 2)
    uTr = Rot(P, "sb", "uT", [128, 128], BF16, 2)
    ybr = Rot(P, "sb", "yb", [128, 4, 128], F32, 2)
    By = P.bufs_n("y", 5)
    par = 0
    for h in range(4):
        for qi_ in range(4):
            q0 = qi_ * 512
            accs = (acc_t[2 * par], acc_b[2 * par], acc_t[2 * par + 1], acc_b[2 * par + 1])
            full_attn_pass(P, [(lambda ch: kl[:, ch * 128:(ch + 1) * 128], qlT[:, h, q0:q0 + 512], [Bkl, Bql]),
                               (lambda ch: kp[:, ch * 128:(ch + 1) * 128], qpT[:, h, q0:q0 + 512], [Bkp, Bqp])],
                           512, list(range(NCH)), lambda ch: vl[:, ch, :], [Bvl], scale, psS, accs, ptr_, 129)
            O, BO = Or.next()
            P.op("act", lambda e, O=O, par=par: e.copy(O[:, 0:2, :], acc_t[2 * par]), [acc_b[2 * par]], [BO])
            P.op("act", lambda e, O=O, par=par: e.copy(O[:, 2:4, :], acc_t[2 * par + 1]), [acc_b[2 * par + 1]], [BO], join=True)
            par ^= 1
            yb, Byb = ybr.next()
            for b in range(4):
                s_, Bs = sr.next()
                P.op("dve", lambda e, s_=s_, O=O, b=b: e.reciprocal(s_[:, 0:1], O[:, b, 128:129]), [BO], [Bs])
                u, Bu = ur.next()
                P.op("dve", lambda e, u=u, s_=s_, O=O, b=b: e.tensor_scalar(u[:], O[:, b, 0:128], s_[:, 0:1], None, ALU.mult),
                     [BO, Bs], [Bu])
                pt, Bpt = pst.next()
                tr(P, pt[:, 0, :], u[:], identb[:], [Bu, Bidb], [Bpt])
                uT, BuT = uTr.next()
                P.op("act", lambda e, uT=uT, pt=pt: e.copy(uT[:], pt[:, 0, :]), [Bpt], [BuT])
                po, Bpo = pso.next()
                mm(P, po[:, 0:128], uT[:], wv[:, h, :], True, True, [BuT, Bwv], [Bpo])
                P.op("dve", lambda e, yb=yb, po=po, b=b: e.tensor_copy(yb[:, b, :], po[:, 0:128]), [Bpo], [Byb], join=(b > 0))
            P.dma("sp", io["y"][q0:q0 + 512, 128 * h:128 * h + 128].rearrange("(b p) c -> p b c", p=128),
                  yb[:], [Byb], [By[qi_]], join=True)
    P.build()


def bcast_mid(ap2d, n):
    a = [list(x) for x in ap2d.ap]
    return bass.AP(ap2d.tensor, ap2d.offset, [a[0], [0, n]] + a[1:])


def phase_BD1(nc, io, tag):
    P = Prog(nc, tag)
    scale = 64 ** -0.5
    NX = 22
    qdT = P.sbuf("qdT", [64, 8, T], BF16); Bqd = P.buf("qdT")
    P.dma("sp", qdT[:], io["q"][768:1280, 0:T].rearrange("(h d) t -> d h t", d=64), [], [Bqd])
    kdX = P.sbuf("kdX", [64, 8, TT], BF16); Bkd = P.buf("kdX")
    P.dma("pool", kdX[:], io["kt"][192:704, :].rearrange("(h d) t -> d h t", d=64), [], [Bkd])
    vdX = P.sbuf("vdX", [128, 18, 520], BF16); Bvd = P.buf("vdX")
    P.dma("sp", vdX[:], io["v"][:, 129:649].rearrange("(c p) e -> p c e", p=128), [], [Bvd])
    hidx = P.sbuf("hidx", [128, NIDX], I32); Bhidx = P.buf("hidx")
    P.dma("sp", hidx[:], io["hidx"], [], [Bhidx])
    kdH = P.sbuf("kdH", [64, 8, 2, 256], BF16); BkdH = P.buf("kdH")
    vdH = P.sbuf("vdH", [128, 4, 649], BF16); BvdH = P.buf("vdH")
    for side in range(2):
        for h in range(8):
            c = 6 + 8 * side + h
            P.gather(kdH[:, h, side, :], io["hk_all"], hidx[0:64, c:c + 1], [Bhidx], [BkdH], join=not (side == 0 and h == 0))
        for c2 in range(2):
            c = 22 + 2 * side + c2
            P.gather(vdH[:, 2 * side + c2, :], io["v_all"], hidx[:, c:c + 1], [Bhidx], [BvdH], join=not (side == 0 and c2 == 0))
    cm = P.sbuf("cm", [128, 64], F32); Bcm = P.buf("cm")
    P.dma("sp", cm[:], io["colmask"], [], [Bcm])
    vm = P.sbuf("vm", [128, NTB, 6, 2], F32); Bvm = P.buf("vm")
    P.dma("sp", vm[:], io["vmD"], [], [Bvm])
    TB = P.sbuf("TB", [128, 120, 64], F32); BTB = P.buf("TB")
    rp = io["rp_pad"]
    TBr = P.sbuf("TBr", [128, 120 * 64], F32); BTBr = P.buf("TBr")
    for a in range(2):
        src = bass.AP(rp.tensor, rp.offset, [[1, 64], [128, 120], [1, 64]])
        P.dma("sp" if a else "pool", TBr[64 * a:64 * a + 64, :].rearrange("p (r q) -> p r q", q=64), src, [], [BTBr], join=(a > 0))
    J2 = P.sbuf("J2", [128, 128], F32); BJ2 = P.buf("J2")
    P.dma("sp", J2[:], io["antidiag"], [], [BJ2])
    psS = Rot(P, "ps", "psS", [128, 512], F32, 3)
    psT = psS
    TBf = TB[:].rearrange("p r q -> p (r q)")
    for cchunk in range(15):
        ps, Bps = psT.next()
        mm(P, ps[:], J2[:], TBr[:, cchunk * 512:(cchunk + 1) * 512], True, True, [BJ2, BTBr], [Bps])
        P.op("act", lambda e, ps=ps, cchunk=cchunk: e.activation(TBf[:, cchunk * 512:(cchunk + 1) * 512], ps[:], AF.Exp),
             [Bps], [BTB], join=(cchunk > 0))
    tt(P, "dve", TB[:], TB[:], bcast_mid(cm[:], 120), ALU.mult, [BTB, Bcm], [BTB])
    TB4 = TB[:].rearrange("p (h r) q -> p h r q", h=8)
    accr = Rot(P, "ps", "acc", [128, 512], F32, 4)
    ptr_ = Rot(P, "sb", "pT", [128, 512], BF16, 5)
    ebr = Rot(P, "sb", "eb", [128, 8, 128], BF16, 3)
    ytr = Rot(P, "sb", "yt", [128, 512], F32, 2)
    ztr = Rot(P, "sb", "zt", [128, 8], F32, 2)
    By = P.bufs_n("y", 5)
    ebstd_t = P.sbuf("ebstd", [128, 5, 8, 128], BF16); Bebstd = P.buf("ebstd")
    ebstd = [ebstd_t[:, ci, :, :] for ci in range(5)]
    mset(P, "pool", ebstd_t[:], 0.0, [Bebstd])
    efirst = True
    for ci in range(5):
        for a in range(2):
            for b in range(2):
                dlt = -4 + 2 * ci + a - b
                if -4 <= dlt <= 3:
                    dr = dlt + 7
                    P.op("dve", lambda e, ci=ci, a=a, b=b, dr=dr: e.tensor_copy(
                        ebstd_t[64 * a:64 * a + 64, ci, :, 64 * b:64 * b + 64], TB4[64 * a:64 * a + 64, :, dr, :]),
                        [BTB, Bebstd] if efirst else [BTB], [Bebstd], join=not efirst)
                    efirst = False
    for n in range(NTB):
        q0 = n * 128
        cis = list(range(0, 6)) if n == 0 else (list(range(-1, 5)) if n == NTB - 1 else list(range(0, 5)))
        def ksrc(oc):
            if oc < 0:
                return (lambda h, oc=oc: kdH[:, h, 0, (oc + 2) * 128:(oc + 3) * 128],
                        lambda h, oc=oc: vdH[:, oc + 2, 129 + 65 * h:129 + 65 * h + 65], [BkdH, BvdH])
            if oc >= NTB and oc < NTB + 2:
                return (lambda h, oc=oc: kdH[:, h, 1, (oc - NTB) * 128:(oc - NTB + 1) * 128],
                        lambda h, oc=oc: vdH[:, 2 + oc - NTB, 129 + 65 * h:129 + 65 * h + 65], [BkdH, BvdH])
            if oc >= 100:
                oc = NTB + (oc - 100)
            return (lambda h, oc=oc: kdX[:, h, oc * 128:(oc + 1) * 128],
                    lambda h, oc=oc: vdX[:, oc, 65 * h:65 * h + 65], [Bkd, Bvd])
        chunks = [(ksrc(n + ci - 2), ci) for ci in cis] + [(ksrc(100), None), (ksrc(101), None)]
        accs = [accr.next() for _ in range(2)]
        nchk = len(chunks)
        pend = []

        def pvD(idx, hg, pT, BpT, vfn, kvb, accs=accs, nchk=nchk):
            acc_, Bacc = accs[hg]
            acc = acc_[:, 0:260].rearrange("p (g e) -> p g e", g=4)
            for hh in range(4):
                h = 4 * hg + hh
                mm(P, acc[:, hh, :], pT[:, hh * 128:(hh + 1) * 128], vfn(h),
                   idx == 0 and hh == 0, idx == nchk - 1, [BpT] + kvb, [Bacc])
        for idx, ((kfn, vfn, kvb), ci) in enumerate(chunks):
            eb = None
            if ci is not None and 2 <= n <= NTB - 3:
                eb, Beb = ebstd[ci], Bebstd
            elif ci is not None:
                eb, Beb = ebr.next()
                slot = ci - cis[0]
                first = True
                for a in range(2):
                    for b in range(2):
                        dr = 3 + 2 * ci + a - b
                        P.op("dve", lambda e, eb=eb, a=a, b=b, dr=dr, n=n, slot=slot: e.tensor_scalar(
                            eb[64 * a:64 * a + 64, :, 64 * b:64 * b + 64], TB4[64 * a:64 * a + 64, :, dr, :],
                            vm[64 * a:64 * a + 64, n, slot, b:b + 1], None, ALU.mult), [BTB, Bvm], [Beb], join=not first)
                        first = False
            for hg in range(2):
                ps, Bps = psS.next()
                for hh in range(4):
                    h = 4 * hg + hh
                    mm(P, ps[:, hh * 128:(hh + 1) * 128], kfn(h), qdT[:, h, q0:q0 + 128],
                       True, True, kvb + [Bqd], [Bps])
                pT, BpT = ptr_.next()
                act(P, pT[:], ps[:], AF.Exp, [Bps], [BpT], scale=scale)
                if eb is not None:
                    P.op("dve", lambda e, pT=pT, eb=eb, hg=hg: e.tensor_tensor(
                        pT[:], pT[:], eb[:, 4 * hg:4 * hg + 4, :].rearrange("p h q -> p (h q)"), ALU.mult),
                        [BpT, Beb], [BpT])
                pend.append((idx, hg, pT, BpT, vfn, kvb))
                if len(pend) > BD1_SKEW:
                    pvD(*pend.pop(0))
        while pend:
            pvD(*pend.pop(0))
        yt, Byt = ytr.next()
        for hg in range(2):
            acc_, Bacc = accs[hg]
            acc = acc_[:, 0:260].rearrange("p (g e) -> p g e", g=4)
            zt, Bzt = ztr.next()
            P.op("dve", lambda e, zt=zt, acc=acc: e.reciprocal(zt[:, 0:4], acc[:, :, 64]), [Bacc], [Bzt])
            for hh in range(4):
                h = 4 * hg + hh
                P.op("dve", lambda e, yt=yt, acc=acc, zt=zt, hh=hh, h=h: e.tensor_scalar(
                    yt[:, h * 64:(h + 1) * 64], acc[:, hh, 0:64], zt[:, hh:hh + 1], None, ALU.mult),
                    [Bacc, Bzt], [Byt], join=not (hg == 0 and hh == 0))
        P.dma("sp", io["y"][q0:q0 + 128, 512:1024], yt[:], [Byt], [By[n // 4]], join=True)
    P.build()


def _rope_tables():
    t = np.arange(S)
    row = (t // GRID_W).astype(np.float32)
    col = (t % GRID_W).astype(np.float32)
    inv = (np.float32(10000.0) ** (-np.arange(16, dtype=np.float32) / np.float32(16))).astype(np.float32)
    ang_r = row[:, None] * inv[None, :]
    ang_c = col[:, None] * inv[None, :]
    ang = np.concatenate([ang_r, ang_r, ang_c, ang_c], axis=-1).astype(np.float32)
    return np.cos(ang).astype(np.float32), np.sin(ang).astype(np.float32)


def _rope_core_tables(r, cos, sin):
    cT = np.ones((128, TT), np.float32)
    sT = np.zeros((128, TT), np.float32)
    c = cos[r * T:(r + 1) * T].T
    s_ = (sin[r * T:(r + 1) * T] * ROPE_SIGN[None, :]).T
    cT[0:64, 0:T] = c; cT[64:128, 0:T] = c
    sT[0:64, 0:T] = s_; sT[64:128, 0:T] = s_
    return cT, sT


class IO(dict):
    pass


def _mk(nc, specs):
    io = IO()
    for (name, shape, dt, kind) in specs:
        io[name] = nc.dram_tensor(name, list(shape), dt, kind=kind).ap()
    return io


def _dt(a):
    if a.dtype == np.float32:
        return F32
    if a.dtype == NPBF:
        return BF16
    if a.dtype == np.int32:
        return I32
    raise ValueError(a.dtype)


def _launch(build, in_maps, outs):
    nc = bass.Bass("TRN2", target_bir_lowering=False)
    specs = [(k, v.shape, _dt(v), "ExternalInput") for k, v in in_maps[0].items()]
    specs += [(k, shp, dt, "ExternalOutput") for (k, shp, dt) in outs]
    io = _mk(nc, specs)
    build(nc, io)
    res = run_bass_kernel_spmd(nc, in_maps, core_ids=list(range(NCORES)))
    return res.results


def _arr_col(v):
    return np.ascontiguousarray(v.reshape(8, 128).T)


def _maskA(r):
    kk = np.arange(128)[:, None]
    qq = np.arange(128)[None, :]
    tp = np.tile((qq <= kk).astype(np.float32), (1, 4))
    tn = np.tile((kk <= qq).astype(np.float32), (1, 4))
    z = np.zeros_like(tp)
    return np.stack([z if r == 0 else tp, tp, tn, z if r == NCORES - 1 else tn]).astype(NPBF)


def _colmask():
    qc = np.arange(64)
    cs = np.clip(qc - 8, 0, 48)
    kc = np.arange(64)[:, None]
    m = ((kc >= cs[None, :]) & (kc < cs[None, :] + 16)).astype(np.float32)
    return np.ascontiguousarray(np.concatenate([m, m], 0))


def _antidiag():
    j = np.zeros((128, 128), np.float32)
    for a in range(2):
        for kc in range(64):
            j[64 * a + 63 - kc, 64 * a + kc] = 1.0
    return j


def _vmD(r):
    vm = np.zeros((128, NTB, 6, 2), np.float32)
    for n in range(NTB):
        cis = list(range(0, 6)) if n == 0 else (list(range(-1, 5)) if n == NTB - 1 else list(range(0, 5)))
        for slot, ci in enumerate(cis):
            for a in range(2):
                for b in range(2):
                    gr = 32 * r + 2 * n + b
                    rs = min(max(gr - 4, 0), 248)
                    kr = 32 * r + 2 * n - 4 + 2 * ci + a
                    if rs <= kr <= rs + 7:
                        vm[64 * a:64 * a + 64, n, slot, b] = 1.0
    return vm


def phase_AG(nc, pairs, tag):
    P = Prog(nc, tag)
    prev = []
    for i, (own, allg) in enumerate(pairs):
        b = P.buf(f"ag{i}")
        P.collective("AllGather", own, allg, prev, [b])
        prev = [b]
    P.build()


def _hidx(r):
    rp, rn = max(r - 1, 0), min(r + 1, NCORES - 1)
    p = np.arange(128, dtype=np.int64)
    cols = []
    for j in range(2):
        cols.append(rp * 256 + 128 + 64 * j + p)
    for j in range(2):
        cols.append(rn * 256 + 0 + 64 * j + p)
    cols.append(rp * TT + (T - 128) + p)
    cols.append(rn * TT + 0 + p)
    for h in range(8):
        cols.append(rp * 1024 + 512 + 64 * h + p)
    for h in range(8):
        cols.append(rn * 1024 + 0 + 64 * h + p)
    for c2 in range(2):
        cols.append(rp * TT + (T - 256) + 128 * c2 + p)
    for c2 in range(2):
        cols.append(rn * TT + 128 * c2 + p)
    a = np.stack(cols, 1)
    a[64:, 0:4] = 0
    a[64:, 6:22] = 0
    return np.ascontiguousarray(a.astype(np.int32))


def _layer_inputs(lay, sfx, norm_g, w_ada, b_ada, w_in):
    cols = []
    for c0 in lay.rope_cols:
        for hh in range(2):
            cols.append(c0 + 64 * hh + ROPE_PERM)
    cols = np.concatenate(cols)
    if lay.idx == 1:
        cols = np.concatenate([384 + ROPE_PERM, 384 + ROPE_PERM])
    return {"norm_g" + sfx: _arr_col(norm_g), "b_ada2" + sfx: np.ascontiguousarray(np.tile(b_ada.reshape(1, -1), (2, 1))),
            "w_ada" + sfx: np.ascontiguousarray(w_ada), "w_in" + sfx: np.ascontiguousarray(w_in),
            "w_rope" + sfx: np.ascontiguousarray(w_in[:, cols])}


_STOP = [None]


def build_fused(nc, ext):
    def scr(name, shape, dt):
        return nc.dram_tensor(name, list(shape), dt, kind="Internal").ap()

    common = {k: ext[k] for k in ("cc2", "cosT", "sinT", "ident", "hidx")}
    y = scr("y_scr", (TT, 1024), F32)
    x1 = scr("x1_scr", (T, 1024), F32)
    xc1 = scr("xc1_scr", (L, 1024), F32)
    xin = [ext["x"], x1]
    xcin = [ext["xc"], xc1]
    for li, lay in enumerate((Lay0, Lay1)):
        sfx = str(li)
        q = scr("q" + sfx, (lay.QROWS, TT), BF16)
        kt = scr("kt" + sfx, (lay.KTROWS, TT), BF16)
        v = scr("v" + sfx, (TT, lay.VCOLS), BF16)
        hk = scr("hk" + sfx, lay.HKSHAPE, BF16)
        sg = scr("sg" + sfx, (TT, 1024), F32)
        mod = scr("mod" + sfx, (2, 3072), F32)
        kt_all = scr("kt_all" + sfx, (NCORES * lay.KTROWS, TT), BF16)
        v_all = scr("v_all" + sfx, (NCORES * TT, lay.VCOLS), BF16)
        hk_all = scr("hk_all" + sfx, (NCORES * lay.HKSHAPE[0], lay.HKSHAPE[1]), BF16)
        ioA = IO(common)
        ioA.update(x=xin[li], xc=xcin[li], q=q, kt=kt, v=v, hk=hk, sg=sg, mod=mod)
        for k in ("norm_g", "b_ada2", "w_ada", "w_in", "w_rope"):
            ioA[k] = ext[k + sfx]
        if li == 1:
            for k in ("kvng_row", "w_qb_all", "qng_col", "wkT"):
                ioA[k] = ext[k]
        phase_A(nc, lay, ioA, f"a{li}_")
        nc.all_engine_barrier()
        if _STOP[0] == "A":
            return
        phase_AG(nc, [(kt, kt_all), (v, v_all), (hk, hk_all)], f"g{li}_")
        nc.all_engine_barrier()
        ioB = IO(common)
        ioB.update(q=q, kt=kt, v=v, sg=sg, mod=mod, y=y, x=xin[li], xc=xcin[li], hk_all=hk_all, v_all=v_all,
                   kt_all=kt_all.rearrange("(r a) c -> r a c", r=NCORES),
                   w_out=ext["w_out" + sfx])
        ioB3 = IO(ioB)
        ioB3["v_all"] = v_all.rearrange("(r a) c -> r a c", r=NCORES)
        if li == 0:
            for k in ("maskA", "a_sink", "b_lambda", "subln_g"):
                ioB[k] = ext[k]; ioB3[k] = ext[k]
            ioB["x1"] = x1; ioB["xc1"] = xc1
            if _STOP[0] == "AG":
                return
            phase_BA0(nc, ioB, "ba_")
            nc.all_engine_barrier()
            if _STOP[0] == "BA":
                return
            phase_BB0(nc, ioB3, "bb_")
            nc.all_engine_barrier()
            if _STOP[0] == "BB":
                return
            phase_BO(nc, ioB, "bo0_", last=False)
            nc.all_engine_barrier()
            if _STOP[0] == "BO":
                return
        else:
            for k in ("wv", "rp_pad", "colmask", "vmD", "antidiag", "final_g"):
                ioB[k] = ext[k]; ioB3[k] = ext[k]
            ioB["out"] = ext["out"]
            phase_BC1(nc, ioB3, "bc_")
            nc.all_engine_barrier()
            phase_BD1(nc, ioB, "bd_")
            nc.all_engine_barrier()
            phase_BO(nc, ioB, "bo1_", last=True)


def kernel(**inputs):
    inp = {k: np.asarray(v) for k, v in inputs.items()}
    cos, sin = _rope_tables()
    x = inp["x"][0]
    ident = np.eye(128, dtype=np.float32)
    cc2 = np.ascontiguousarray(np.stack([_arr_col(inp["c"].reshape(-1)), _arr_col(inp["c_ctx"].reshape(-1))], axis=-1))
    shared = dict(xc=np.ascontiguousarray(inp["ctx"][0]), cc2=cc2, ident=ident)
    shared.update(_layer_inputs(Lay0, "0", inp["ev_norm_g"][0], inp["ev_w_ada"][0], inp["ev_b_ada"][0], inp["ev_w_in"][0]))
    shared.update(_layer_inputs(Lay1, "1", inp["od_norm_g"][0], inp["od_w_ada"][0], inp["od_b_ada"][0], inp["od_w_in"][0]))
    shared.update(a_sink=np.ascontiguousarray(inp["ev_a_sink"][0].reshape(1, 8)),
                  b_lambda=np.ascontiguousarray(inp["ev_b_lambda"][0].reshape(1, 256)),
                  subln_g=np.ascontiguousarray(inp["ev_b_subln_g"][0].reshape(1, 128)),
                  w_out0=np.ascontiguousarray(inp["ev_w_out"][0]), w_out1=np.ascontiguousarray(inp["od_w_out"][0]))
    w_qb = inp["od_c_w_qb"][0]
    pe_cols = np.concatenate([192 * h + 128 + ROPE_PERM for h in range(4)])
    w_kvb = inp["od_c_w_kvb"][0].reshape(128, 4, 256)
    rpb = inp["od_d_rpb"][0]
    rp = np.zeros((8, 15, 128), np.float32)
    rp[:, :, 48:79] = rpb[:, :, ::-1]
    shared.update(kvng_row=np.ascontiguousarray(inp["od_c_kv_norm_g"][0].reshape(1, 128)),
                  w_qb_all=np.ascontiguousarray(np.concatenate([w_qb, w_qb[:, pe_cols]], axis=1)),
                  qng_col=np.ascontiguousarray(inp["od_c_q_norm_g"][0].reshape(2, 128).T),
                  wkT=np.ascontiguousarray(np.transpose(w_kvb[:, :, 0:128], (2, 1, 0))),
                  wv=np.ascontiguousarray(w_kvb[:, :, 128:256]), rp_pad=np.ascontiguousarray(rp.reshape(120, 128)),
                  colmask=_colmask(), antidiag=_antidiag(),
                  final_g=np.ascontiguousarray(inp["final_norm_g"].reshape(1, 1024)))
    maps = []
    for r in range(NCORES):
        cT, sT = _rope_core_tables(r, cos, sin)
        m = dict(shared)
        m.update(x=np.ascontiguousarray(x[r * T:(r + 1) * T]), cosT=cT, sinT=sT, hidx=_hidx(r), maskA=_maskA(r), vmD=_vmD(r))
        maps.append(m)
    res = _launch(build_fused, maps, [("out", (T, 1024), F32)])
    out = np.concatenate([res[r]["out"] for r in range(NCORES)], axis=0)
    return out.reshape(1, S, D).astype(np.float32)
```
